# Optimizing a Trainium2 kernel written in Bass

```python
import math
import jax
import jax.numpy as jnp
from jax import lax
import numpy as np

D_MODEL = 1024
BATCH = 8
SEQ = 4096
DEPTH = 2

HEAD_DIM = 64
NORM_EPS = 1e-6
ROPE_THETA = 10000.0

GDN_HEADS = 4
GDN_DK = 64
GDN_DV = 64
GDN_CHUNK = 64
CONV_K = 5

DIL_HEADS = 4
DIL_PATTERNS = ((128, 1), (512, 4), (2048, 16))

DIFF_HEADS = 4
DIFF_DIM = HEAD_DIM
Q_BLOCK = 128

D_FF = ((8 * D_MODEL + 3 * 256 - 1) // (3 * 256)) * 256

A_QK = GDN_HEADS * GDN_DK
A_V = GDN_HEADS * GDN_DV
A_GATE = 2 * GDN_HEADS
B_W = DIL_HEADS * HEAD_DIM
C_QK = DIFF_HEADS * 2 * DIFF_DIM
C_V = DIFF_HEADS * 2 * DIFF_DIM
IN_SPLITS = (A_QK, A_QK, A_V, A_V, A_GATE, A_GATE, B_W, B_W, B_W, C_QK, C_QK, C_V)
IN_WIDTH = sum(IN_SPLITS)
GDN_CONV_CH = 2 * A_QK + A_V
MIX_WIDTH = A_V + B_W + C_V
MAX_POS_OFFSET = 4096

kernel_name = 'hybrid_gdn_dilated_diff_encoder'


def rms_norm(x, w):
    xf = x.astype(jnp.float32)
    y = xf * lax.rsqrt(jnp.mean(xf * xf, axis=-1, keepdims=True) + NORM_EPS)
    return (y * w.astype(jnp.float32)).astype(x.dtype)


def l2norm(t):
    tf = t.astype(jnp.float32)
    return tf * lax.rsqrt(jnp.sum(tf * tf, axis=-1, keepdims=True) + 1e-6)


def rope_tables(positions, dim):
    inv = ROPE_THETA ** (-jnp.arange(0, dim, 2, dtype=jnp.float32) / dim)
    ang = positions.astype(jnp.float32)[..., None] * inv
    return jnp.cos(ang)[:, :, None, :], jnp.sin(ang)[:, :, None, :]


def apply_rope(x, cos, sin):
    x1, x2 = jnp.split(x.astype(jnp.float32), 2, axis=-1)
    return jnp.concatenate([x1 * cos - x2 * sin, x2 * cos + x1 * sin], axis=-1).astype(x.dtype)


def centred_depthwise_conv(x, w):
    pad = (CONV_K - 1) // 2
    return lax.conv_general_dilated(
        x, w[:, None, :].astype(x.dtype), window_strides=(1,), padding=[(pad, pad)],
        dimension_numbers=('NWC', 'WIO', 'NWC'), feature_group_count=x.shape[-1])


def chunk_gated_delta_rule(q, k, v, g, beta):
    f32 = jnp.float32
    N, H, S, dk = q.shape
    dv = v.shape[-1]
    C = GDN_CHUNK
    nc = S // C
    q = q.astype(f32) * (dk ** -0.5)
    k = k.astype(f32)
    v = v.astype(f32)
    beta = beta.astype(f32)
    q, k, v = (t.reshape(N, H, nc, C, t.shape[-1]) for t in (q, k, v))
    beta = beta.reshape(N, H, nc, C)
    g = jnp.cumsum(g.astype(f32).reshape(N, H, nc, C), axis=-1)
    tril = jnp.tril(jnp.ones((C, C), dtype=bool))
    strict = jnp.tril(jnp.ones((C, C), dtype=bool), -1)
    decay = jnp.exp(jnp.where(tril, g[..., :, None] - g[..., None, :], -jnp.inf))
    k_beta = k * beta[..., None]
    v_beta = v * beta[..., None]
    lower = jnp.where(strict, jnp.einsum('nhcik,nhcjk->nhcij', k_beta, k) * decay, 0.0)
    eye = jnp.eye(C, dtype=f32)
    t_inv = lax.linalg.triangular_solve(eye + lower, jnp.broadcast_to(eye, lower.shape),
                                        left_side=True, lower=True, unit_diagonal=True)
    u = jnp.einsum('nhcij,nhcjv->nhciv', t_inv, v_beta)
    k_cum = jnp.einsum('nhcij,nhcjk->nhcik', t_inv, k_beta * jnp.exp(g)[..., None])
    intra = jnp.where(tril, jnp.einsum('nhcik,nhcjk->nhcij', q, k) * decay, 0.0)
    xs = tuple(jnp.moveaxis(t, 2, 0) for t in (q, k, u, k_cum, intra, g))

    def step(state, inp):
        q_c, k_c, u_c, kc_c, a_c, g_c = inp
        v_new = u_c - jnp.einsum('nhck,nhkv->nhcv', kc_c, state)
        out = (jnp.einsum('nhck,nhkv->nhcv', q_c * jnp.exp(g_c)[..., None], state)
               + jnp.einsum('nhij,nhjv->nhiv', a_c, v_new))
        g_last = g_c[..., -1]
        state = (state * jnp.exp(g_last)[..., None, None]
                 + jnp.einsum('nhck,nhcv->nhkv', k_c * jnp.exp(g_last[..., None] - g_c)[..., None], v_new))
        return state, out

    state0 = jnp.zeros((N, H, dk, dv), f32)
    _, out = lax.scan(step, state0, xs)
    return jnp.moveaxis(out, 0, 2).reshape(N, H, S, dv)


def gdn_mixer(q, k, v, z, a, b, conv_w, a_log, dt_bias, norm_w):
    f32 = jnp.float32
    B, S, _ = q.shape
    qkv = jax.nn.silu(centred_depthwise_conv(jnp.concatenate([q, k, v], axis=-1), conv_w))
    q, k, v = jnp.split(qkv, [A_QK, 2 * A_QK], axis=-1)
    q = l2norm(q.reshape(B, S, GDN_HEADS, GDN_DK))
    k = l2norm(k.reshape(B, S, GDN_HEADS, GDN_DK))
    v = v.reshape(B, S, GDN_HEADS, GDN_DV)
    a = a.reshape(B, S, 2, GDN_HEADS).astype(f32)
    b = b.reshape(B, S, 2, GDN_HEADS).astype(f32)
    g = -jnp.exp(a_log.astype(f32)) * jax.nn.softplus(a + dt_bias.astype(f32))
    beta = jax.nn.sigmoid(b)

    def both(fwd, bwd):
        return jnp.concatenate([fwd, bwd[:, ::-1]], axis=0)

    qd = both(q, q).transpose(0, 2, 1, 3)
    kd = both(k, k).transpose(0, 2, 1, 3)
    vd = both(v, v).transpose(0, 2, 1, 3)
    gd = both(g[:, :, 0], g[:, :, 1]).transpose(0, 2, 1)
    bd = both(beta[:, :, 0], beta[:, :, 1]).transpose(0, 2, 1)
    o = chunk_gated_delta_rule(qd, kd, vd, gd, bd)
    o = (o[:B] + o[B:, :, ::-1]).transpose(0, 2, 1, 3)
    o = rms_norm(o, norm_w) * jax.nn.silu(z.reshape(B, S, GDN_HEADS, GDN_DV).astype(f32))
    return o.reshape(B, S, A_V)


def banded_attention(q, k, v, half):
    f32 = jnp.float32
    N, L, H, D = q.shape
    W = half
    nb = -(-L // W)
    pad = nb * W - L
    qb = jnp.pad(q, ((0, 0), (0, pad), (0, 0), (0, 0))).reshape(N, nb, W, H, D)
    kv_pad = ((0, 0), (W, pad + W), (0, 0), (0, 0))
    kp = jnp.pad(k, kv_pad).reshape(N, nb + 2, W, H, D)
    vp = jnp.pad(v, kv_pad).reshape(N, nb + 2, W, H, D)
    kb = jnp.concatenate([kp[:, :-2], kp[:, 1:-1], kp[:, 2:]], axis=2)
    vb = jnp.concatenate([vp[:, :-2], vp[:, 1:-1], vp[:, 2:]], axis=2)
    qpos = jnp.arange(nb * W).reshape(nb, W)
    kpos = jnp.arange(nb)[:, None] * W - W + jnp.arange(3 * W)[None, :]
    dist = kpos[:, None, :] - qpos[:, :, None]
    valid = (jnp.abs(dist) <= W) & (kpos[:, None, :] >= 0) & (kpos[:, None, :] < L)
    s = jnp.einsum('nbqhd,nbkhd->nbhqk', qb, kb).astype(f32) * (D ** -0.5)
    s = jnp.where(valid[None, :, None], s, -jnp.inf)
    m = jnp.max(s, axis=-1, keepdims=True)
    p = jnp.exp(s - m)
    den = jnp.sum(p, axis=-1, keepdims=True)
    o = jnp.einsum('nbhqk,nbkhd->nbqhd', p / den, vb.astype(f32)).reshape(N, nb * W, H, D)[:, :L]
    lse = (m + jnp.log(den))[..., 0].transpose(0, 1, 3, 2).reshape(N, nb * W, H)[:, :L]
    return o, lse


def dilated_mixer(q, k, v):
    B, S, H, D = q.shape
    outs, lses = [], []
    for window, dil in DIL_PATTERNS:
        half = window // (2 * dil)
        L = S // dil

        def by_residue(t):
            return t.reshape(B, L, dil, H, D).transpose(0, 2, 1, 3, 4).reshape(B * dil, L, H, D)

        o, lse = banded_attention(by_residue(q), by_residue(k), by_residue(v), half)
        outs.append(o.reshape(B, dil, L, H, D).transpose(0, 2, 1, 3, 4).reshape(B, S, H, D))
        lses.append(lse.reshape(B, dil, L, H).transpose(0, 2, 1, 3).reshape(B, S, H))
    weights = jax.nn.softmax(jnp.stack(lses, axis=0), axis=0)
    return jnp.einsum('pbsh,pbshd->bshd', weights, jnp.stack(outs, axis=0))


def diff_mixer(q, k, v, lambda_q1, lambda_k1, lambda_q2, lambda_k2, subln_w, lambda_init):
    f32 = jnp.float32
    B, S, H, _, Dc = q.shape
    lam = (jnp.exp(jnp.sum(lambda_q1.astype(f32) * lambda_k1.astype(f32)))
           - jnp.exp(jnp.sum(lambda_q2.astype(f32) * lambda_k2.astype(f32))) + lambda_init)
    nq = S // Q_BLOCK
    qb = q.reshape(B, nq, Q_BLOCK, H, 2, Dc).transpose(1, 0, 2, 3, 4, 5)
    vf = v.astype(f32)

    def block(q_blk):
        s = jnp.einsum('bqhcd,bkhcd->bhcqk', q_blk, k).astype(f32) * (Dc ** -0.5)
        p = jax.nn.softmax(s, axis=-1)
        attn = p[:, :, 0] - lam * p[:, :, 1]
        return jnp.einsum('bhqk,bkhd->bqhd', attn, vf)

    o = lax.map(block, qb)
    o = o.transpose(1, 0, 2, 3, 4).reshape(B, S, H, 2 * Dc)
    return rms_norm(o, subln_w) * (1.0 - lambda_init)


def hybrid_layer(x, cos, sin, attn_norm_w, w_in, conv_w, a_log, dt_bias, gdn_norm_w,
                 lambda_q1, lambda_k1, lambda_q2, lambda_k2, subln_w, w_out,
                 ffn_norm_w, w_gate, w_up, w_down, lambda_init):
    B, S, _ = x.shape
    h = rms_norm(x, attn_norm_w)
    split_at = [int(i) for i in np.cumsum(IN_SPLITS)[:-1]]
    aq, ak, av, az, aa, ab, bq, bk, bv, cq, ck, cv = jnp.split(h @ w_in, split_at, axis=-1)
    o_a = gdn_mixer(aq, ak, av, az, aa, ab, conv_w, a_log, dt_bias, gdn_norm_w)
    bq = apply_rope(bq.reshape(B, S, DIL_HEADS, HEAD_DIM), cos, sin)
    bk = apply_rope(bk.reshape(B, S, DIL_HEADS, HEAD_DIM), cos, sin)
    o_b = dilated_mixer(bq, bk, bv.reshape(B, S, DIL_HEADS, HEAD_DIM)).reshape(B, S, B_W)
    cq = apply_rope(cq.reshape(B, S, 2 * DIFF_HEADS, DIFF_DIM), cos, sin).reshape(B, S, DIFF_HEADS, 2, DIFF_DIM)
    ck = apply_rope(ck.reshape(B, S, 2 * DIFF_HEADS, DIFF_DIM), cos, sin).reshape(B, S, DIFF_HEADS, 2, DIFF_DIM)
    o_c = diff_mixer(cq, ck, cv.reshape(B, S, DIFF_HEADS, 2 * DIFF_DIM), lambda_q1, lambda_k1,
                     lambda_q2, lambda_k2, subln_w, lambda_init).reshape(B, S, C_V)
    mix = jnp.concatenate([o_a.astype(x.dtype), o_b.astype(x.dtype), o_c.astype(x.dtype)], axis=-1)
    x = x + mix @ w_out
    h = rms_norm(x, ffn_norm_w)
    return x + (jax.nn.silu(h @ w_gate) * (h @ w_up)) @ w_down


def setup_inputs(seed: int = 0) -> dict:
    key = jax.random.key(seed)
    ks = jax.random.split(key, 20)
    f32 = jnp.float32

    def nrm(k, shape, scale):
        return jax.random.normal(k, shape, f32) * scale

    x = jax.random.normal(ks[0], (BATCH, SEQ, D_MODEL), f32)
    positions = (jnp.arange(SEQ, dtype=jnp.int32)[None, :]
                 + jax.random.randint(ks[1], (BATCH, 1), 0, MAX_POS_OFFSET, dtype=jnp.int32))
    attn_norm_w = 1.0 + nrm(ks[2], (DEPTH, D_MODEL), 0.02)
    w_in = nrm(ks[3], (DEPTH, D_MODEL, IN_WIDTH), D_MODEL ** -0.5)
    conv_w = nrm(ks[4], (DEPTH, CONV_K, GDN_CONV_CH), CONV_K ** -0.5)
    a_log = jnp.log(jax.random.uniform(ks[5], (DEPTH, 2, GDN_HEADS), f32, 1.0, 16.0))
    dt = jnp.exp(jax.random.uniform(ks[6], (DEPTH, 2, GDN_HEADS), f32, math.log(1e-3), math.log(1e-1)))
    dt_bias = dt + jnp.log(-jnp.expm1(-dt))
    gdn_norm_w = 1.0 + nrm(ks[7], (DEPTH, GDN_DV), 0.02)
    lambda_q1 = nrm(ks[8], (DEPTH, DIFF_DIM), 0.1)
    lambda_k1 = nrm(ks[9], (DEPTH, DIFF_DIM), 0.1)
    lambda_q2 = nrm(ks[10], (DEPTH, DIFF_DIM), 0.1)
    lambda_k2 = nrm(ks[11], (DEPTH, DIFF_DIM), 0.1)
    subln_w = 1.0 + nrm(ks[12], (DEPTH, 2 * DIFF_DIM), 0.02)
    w_out = nrm(ks[13], (DEPTH, MIX_WIDTH, D_MODEL), MIX_WIDTH ** -0.5)
    ffn_norm_w = 1.0 + nrm(ks[14], (DEPTH, D_MODEL), 0.02)
    w_gate = nrm(ks[15], (DEPTH, D_MODEL, D_FF), D_MODEL ** -0.5)
    w_up = nrm(ks[16], (DEPTH, D_MODEL, D_FF), D_MODEL ** -0.5)
    w_down = nrm(ks[17], (DEPTH, D_FF, D_MODEL), D_FF ** -0.5)
    final_norm_w = 1.0 + nrm(ks[18], (D_MODEL,), 0.02)
    return {'x': x, 'positions': positions, 'attn_norm_w': attn_norm_w, 'w_in': w_in,
            'conv_w': conv_w, 'a_log': a_log, 'dt_bias': dt_bias, 'gdn_norm_w': gdn_norm_w,
            'lambda_q1': lambda_q1, 'lambda_k1': lambda_k1, 'lambda_q2': lambda_q2,
            'lambda_k2': lambda_k2, 'subln_w': subln_w, 'w_out': w_out,
            'ffn_norm_w': ffn_norm_w, 'w_gate': w_gate, 'w_up': w_up, 'w_down': w_down,
            'final_norm_w': final_norm_w}


def reference(x, positions, attn_norm_w, w_in, conv_w, a_log, dt_bias, gdn_norm_w,
              lambda_q1, lambda_k1, lambda_q2, lambda_k2, subln_w, w_out,
              ffn_norm_w, w_gate, w_up, w_down, final_norm_w):
    cos, sin = rope_tables(positions, HEAD_DIM)
    for l in range(DEPTH):
        lambda_init = 0.8 - 0.6 * math.exp(-0.3 * l)
        x = hybrid_layer(x, cos, sin, attn_norm_w[l], w_in[l], conv_w[l], a_log[l], dt_bias[l],
                         gdn_norm_w[l], lambda_q1[l], lambda_k1[l], lambda_q2[l], lambda_k2[l],
                         subln_w[l], w_out[l], ffn_norm_w[l], w_gate[l], w_up[l], w_down[l],
                         lambda_init)
    return rms_norm(x, final_norm_w)
```

```python
import math
import numpy as np
import concourse.bass as bass
import concourse.mybir as mybir
from concourse.bass_utils import run_bass_kernel_spmd

F32 = mybir.dt.float32
BF16 = mybir.dt.bfloat16
I32 = mybir.dt.int32
AF = mybir.ActivationFunctionType
ALU = mybir.AluOpType
AX = mybir.AxisListType

S_LEN = 4096
D = 1024
NT = 32
L = 2
DFF = 2816
NCOL = 4880
EPS = 1e-6
TWO_PI = 2.0 * math.pi
C1 = 6.28125
C2 = TWO_PI - C1


class Buf:
    __slots__ = ("name", "writer", "readers", "excl")

    def __init__(self, name="", excl=False):
        self.name = name
        self.writer = None
        self.readers = []
        self.excl = excl


class _Rec:
    def __getattr__(self, name):
        def f(*a, **kw):
            self.__dict__["call"] = (name, a, kw)
            return self
        return f


def _bind(fn):
    rec = _Rec()
    fn(rec)
    name, a, kw = rec.call
    return lambda e: getattr(e, name)(*a, **kw)


class Sched:
    CENG = ("pe", "act", "dve", "pool")
    DQ = ("sp", "act", "pool")

    def __init__(self, nc, n_dma_sems=8):
        self.nc = nc
        self.prog = {e: [] for e in ("pe", "act", "dve", "pool", "sp")}
        self.csem = {e: nc.alloc_semaphore("c_" + e) for e in self.CENG}
        self.cnt = {e: 0 for e in self.CENG}
        self.nd = n_dma_sems
        self.dsem = {q: [nc.alloc_semaphore("d_%s%d" % (q, i)) for i in range(n_dma_sems)]
                     for q in self.DQ}
        self.dcnt = {q: 0 for q in self.DQ}
        self.seen = {e: {} for e in self.prog}
        self.ninstr = 0

    def _sem(self, key):
        return self.csem[key[1]] if key[0] == "c" else self.dsem[key[1]][key[2]]

    def _need(self, eng, tok, waits):
        key, val, _ = tok
        if self.seen[eng].get(key, 0) >= val:
            return
        self.seen[eng][key] = val
        waits.append((self._sem(key), val))

    def _deps(self, eng, reads, writes, is_dma):
        waits = []
        for b in reads:
            t = b.writer
            if t is not None and not (eng == "pe" and t[2] == "pe"):
                self._need(eng, t, waits)
        for b in writes:
            t = b.writer
            if t is not None and (is_dma or t[2] != eng):
                self._need(eng, t, waits)
            for t in b.readers:
                if is_dma or t[2] != eng:
                    self._need(eng, t, waits)
        return waits

    def _commit(self, tok, reads, writes):
        for b in reads:
            b.readers.append(tok)
        for b in writes:
            b.writer = tok
            b.readers = []

    def op(self, eng, fn, reads=(), writes=()):
        ex = [b for b in reads if b.excl and b not in writes]
        if ex:
            writes = list(writes) + ex
        waits = self._deps(eng, reads, writes, False)
        self.cnt[eng] += 1
        n = self.cnt[eng]
        self.prog[eng].append((waits, _bind(fn), (self.csem[eng], 1)))
        self._commit((("c", eng), n, eng), reads, writes)
        self.ninstr += 1

    def dma(self, q, out, in_, reads=(), writes=(), **kw):
        waits = self._deps(q, reads, writes, True)
        i = self.dcnt[q]
        self.dcnt[q] += 1
        j = i % self.nd
        key = ("d", q, j)
        prev = 16 * (i // self.nd)
        if prev > 0:
            self._need(q, (key, prev, None), waits)
        tgt = prev + 16
        self.prog[q].append((waits, lambda e: e.dma_start(out=out, in_=in_, **kw), (self.dsem[q][j], 16)))
        self._commit((key, tgt, None), reads, writes)
        self.ninstr += 1

    def _all_tokens(self):
        toks = []
        for q in self.DQ:
            for j in range(self.nd):
                n = (self.dcnt[q] - j + self.nd - 1) // self.nd
                if n > 0:
                    toks.append((("d", q, j), 16 * n, None))
        for e in self.CENG:
            if self.cnt[e] > 0:
                toks.append((("c", e), self.cnt[e], e))
        return toks

    def barrier(self):
        toks = self._all_tokens()
        for eng in self.prog:
            waits = []
            for t in toks:
                if t[2] == eng:
                    continue
                self._need(eng, t, waits)
            if waits:
                self.prog[eng].append((waits, None, None))

    def finish(self):
        nc = self.nc
        final = [(self._sem(k), v) for k, v, _ in self._all_tokens()]
        prog = self.prog

        def replay(eng, lst, extra=()):
            for waits, fn, inc in lst:
                for s, v in waits:
                    eng.wait_ge(s, v)
                if fn is None:
                    continue
                ins = fn(eng)
                if inc is not None:
                    ins.then_inc(inc[0], inc[1])
            for s, v in extra:
                eng.wait_ge(s, v)

        with nc.Block() as block:
            @block.sync
            def _(e):
                replay(e, prog["sp"], final)

            @block.tensor
            def _(e):
                replay(e, prog["pe"])

            @block.scalar
            def _(e):
                replay(e, prog["act"])

            @block.vector
            def _(e):
                replay(e, prog["dve"])

            @block.gpsimd
            def _(e):
                replay(e, prog["pool"])


class Rot:
    def __init__(self, items):
        self.items = items
        self.i = 0

    def next(self):
        it = self.items[self.i % len(self.items)]
        self.i += 1
        return it


class K:
    pass


def sb(k, name, shape, dt, n=1):
    items = []
    for i in range(n):
        t = k.nc.alloc_sbuf_tensor("%s_%d_%d" % (name, k.uid, i), list(shape), dt)
        items.append((t, Buf(name)))
    k.uid += 1
    return items[0] if n == 1 else Rot(items)


def phase_setup(k):
    S, nc = k.S, k.nc
    S.op("pool", lambda e: e.memset(k.identf[:], 0.0), writes=[k.bidentf])
    S.op("pool", lambda e: e.affine_select(out=k.identf[:], in_=k.identf[:], pattern=[[-1, 128]],
                                           compare_op=ALU.not_equal, fill=1.0, base=0, channel_multiplier=1),
         reads=[k.bidentf], writes=[k.bidentf])
    S.op("dve", lambda e: e.tensor_copy(out=k.identb[:], in_=k.identf[:]), reads=[k.bidentf], writes=[k.bidentb])
    S.dma("sp", k.cst[:], k.cst_d, writes=[k.bcst])
    with nc.sbuf_tensor("su_pi", [128, S_LEN], I32) as pi, nc.sbuf_tensor("su_a", [128, S_LEN], F32) as ang, \
            nc.sbuf_tensor("su_k", [128, S_LEN], I32) as ki, nc.sbuf_tensor("su_kf", [128, S_LEN], F32) as kf, \
            nc.sbuf_tensor("su_r", [128, S_LEN], F32) as r, nc.sbuf_tensor("su_o", [128, S_LEN], F32) as o:
        bpi, bang, bki, bkf, br, bo = Buf(), Buf(), Buf(), Buf(), Buf(), Buf()
        S.dma("sp", pi[:], k.pos_d.rearrange("(o n) -> o n", o=1).broadcast_to([128, S_LEN]), writes=[bpi])
        S.op("dve", lambda e: e.tensor_copy(out=ang[:], in_=pi[:]), reads=[bpi], writes=[bang])
        S.op("dve", lambda e: e.tensor_scalar(out=ang[:], in0=ang[:], scalar1=k.cst[:, 0:1], scalar2=None, op0=ALU.mult),
             reads=[bang, k.bcst], writes=[bang])
        S.op("dve", lambda e: e.tensor_scalar(out=ki[:], in0=ang[:], scalar1=1.0 / TWO_PI, scalar2=None, op0=ALU.mult),
             reads=[bang], writes=[bki])
        S.op("dve", lambda e: e.tensor_copy(out=kf[:], in_=ki[:]), reads=[bki], writes=[bkf])
        S.op("dve", lambda e: e.scalar_tensor_tensor(out=r[:], in0=kf[:], scalar=-C1, in1=ang[:], op0=ALU.mult, op1=ALU.add),
             reads=[bkf, bang], writes=[br])
        S.op("dve", lambda e: e.scalar_tensor_tensor(out=r[:], in0=kf[:], scalar=-C2, in1=r[:], op0=ALU.mult, op1=ALU.add),
             reads=[bkf, br], writes=[br])
        S.op("dve", lambda e: e.tensor_scalar(out=r[:], in0=r[:], scalar1=-3.1415925, scalar2=3.1415925, op0=ALU.max, op1=ALU.min),
             reads=[br], writes=[br])
        S.op("act", lambda e: e.activation(out=o[:], in_=r[:], func=AF.Sin), reads=[br], writes=[bo])
        S.op("dve", lambda e: e.tensor_scalar(out=o[:], in0=o[:], scalar1=k.cst[:, 1:2], scalar2=None, op0=ALU.mult),
             reads=[bo, k.bcst], writes=[bo])
        S.dma("sp", k.sinT_d, o[:], reads=[bo])
        S.op("act", lambda e: e.activation(out=r[:], in_=r[:], func=AF.Abs), reads=[br], writes=[br])
        S.op("act", lambda e: e.activation(out=kf[:], in_=r[:], func=AF.Sin, scale=-1.0, bias=k.cst[:, 2:3]),
             reads=[br, k.bcst], writes=[bkf])
        S.dma("sp", k.cosT_d, kf[:], reads=[bkf])
        S.barrier()


def norm_tile(k, xt, bx, nwb, bnwb, h, bh, ss_r, junk_r):
    S = k.S
    ss, bss = ss_r.next()
    junk, bj = junk_r.next()
    S.op("act", lambda e: e.activation(out=junk[:], in_=xt[:], func=AF.Square, accum_out=ss[:]),
         reads=[bx], writes=[bj, bss])
    S.op("dve", lambda e: e.tensor_scalar(out=ss[:], in0=ss[:], scalar1=1.0 / D, scalar2=EPS, op0=ALU.mult, op1=ALU.add),
         reads=[bss], writes=[bss])
    S.op("act", lambda e: e.activation(out=ss[:], in_=ss[:], func=AF.Sqrt), reads=[bss], writes=[bss])
    S.op("dve", lambda e: e.reciprocal(out=ss[:], in_=ss[:]), reads=[bss], writes=[bss])
    S.op("dve", lambda e: e.scalar_tensor_tensor(out=h[:], in0=xt[:], scalar=ss[:, 0:1], in1=nwb[:], op0=ALU.mult, op1=ALU.mult),
         reads=[bx, bss, bnwb], writes=[bh])


def phase_a(k, l):
    S, nc = k.S, k.nc
    k.uid += 1
    src = k.x_d if l == 0 else k.xres_d
    with nc.sbuf_tensor("a_wb%d" % l, [128, 8, NCOL], BF16) as wb, \
            nc.sbuf_tensor("a_nwb%d" % l, [128, D], F32) as nwb, \
            nc.sbuf_tensor("a_arena%d" % l, [128, 12000], F32) as arena, nc.sbuf_tensor("a_arenab%d" % l, [128, 16000], BF16) as arenab:
        off = [0]

        offb = [0]

        def carve(n_f32, dt=F32):
            if dt == BF16:
                a = arenab[:, offb[0]:offb[0] + 2 * n_f32]
                offb[0] += 2 * n_f32
                return a
            a = arena[:, off[0]:off[0] + n_f32]
            off[0] += n_f32
            return a

        bwb = [Buf() for _ in range(8)]
        bnwb = Buf()
        for kk in range(8):
            S.dma("pool", wb[:, kk, :], k.w_in_d[l, kk * 128:(kk + 1) * 128, :], writes=[bwb[kk]])
        S.dma("sp", nwb[:], k.attn_nw_d[l:l + 1, :].broadcast_to([128, D]), writes=[bnwb])
        xt_r = Rot([(carve(1024), Buf()) for _ in range(2)])
        junk_r = Rot([(carve(1024), Buf()) for _ in range(1)])
        h_r = Rot([(carve(512, BF16), Buf()) for _ in range(2)])
        ss_r = Rot([(carve(1), Buf()) for _ in range(2)])
        hT_r = Rot([(carve(2048, BF16), Buf()) for _ in range(2)])
        cos_r = Rot([(carve(512), Buf()) for _ in range(2)])
        sin_r = Rot([(carve(512), Buf()) for _ in range(2)])
        ofm_r = Rot([(carve(512), Buf()) for _ in range(3)])
        t1_r = Rot([(carve(512), Buf()) for _ in range(2)])
        t2_r = Rot([(carve(512), Buf()) for _ in range(2)])
        oqk_r = Rot([(carve(256, BF16), Buf()) for _ in range(3)])
        oza_r = Rot([(carve(272), Buf()) for _ in range(2)])
        ovb_r = Rot([(carve(130, BF16), Buf()) for _ in range(2)])
        ovc_r = Rot([(carve(258, BF16), Buf()) for _ in range(2)])
        for (t, b) in ovb_r.items:
            S.op("dve", lambda e, t=t: e.memset(t, 1.0), writes=[b])
        for (t, b) in ovc_r.items:
            S.op("dve", lambda e, t=t: e.memset(t, 1.0), writes=[b])
        ptr_r = Rot([(k.pb[0], k.bpb[0]), (k.pb[1], k.bpb[1])])
        pfm_r = Rot([(k.pb[i], k.bpb[i]) for i in (2, 3, 4, 5)])
        ptm_r = Rot([(k.pb[i], k.bpb[i]) for i in (6, 7)])

        for g in range(8):
            hT, bhT = hT_r.next()
            hT3 = hT.rearrange("p (k n) -> p k n", k=8)
            cg, bcg = cos_r.next()
            sg, bsg = sin_r.next()
            S.dma("sp", cg, k.cosT_d[:, g * 512:(g + 1) * 512], writes=[bcg])
            S.dma("sp", sg, k.sinT_d[:, g * 512:(g + 1) * 512], writes=[bsg])
            for s in range(4):
                t0 = g * 512 + s * 128
                xt, bx = xt_r.next()
                h, bh = h_r.next()
                S.dma("sp", xt, src[t0:t0 + 128, :], writes=[bx])
                norm_tile(k, xt, bx, nwb, bnwb, h, bh, ss_r, junk_r)
                pt, bpt = ptr_r.next()
                ptb = pt[:].bitcast(BF16).rearrange("p (k n) -> p k n", k=8)
                for kk in range(8):
                    S.op("pe", lambda e, kk=kk, ptb=ptb, h=h: e.transpose(out=ptb[:, kk, :], in_=h[:, kk * 128:(kk + 1) * 128],
                                                                          identity=k.identb[:]),
                         reads=[bh, k.bidentb], writes=[bpt])
                S.op("dve", lambda e, ptb=ptb, hT3=hT3, s=s: e.tensor_copy(out=hT3[:, :, s * 128:(s + 1) * 128], in_=ptb),
                     reads=[bpt], writes=[bhT])
            tsl = slice(g * 512, (g + 1) * 512)

            def fm_mm(ch):
                pf, bpf = pfm_r.next()
                for kk in range(8):
                    S.op("pe", lambda e, kk=kk, pf=pf, ch=ch: e.matmul(pf[:], lhsT=wb[:, kk, ch * 128:(ch + 1) * 128],
                                                                      rhs=hT3[:, kk, :], start=(kk == 0), stop=(kk == 7)),
                         reads=[bwb[kk], bhT], writes=[bpf])
                return pf, bpf

            for ch in range(6):
                pf, bpf = fm_mm(ch)
                o, bo = ofm_r.next()
                S.op("act", lambda e, o=o, pf=pf: e.copy(out=o, in_=pf[:]), reads=[bpf], writes=[bo])
                S.dma("sp", k.qkvT_a_d[ch * 128:(ch + 1) * 128, tsl], o, reads=[bo])
            for (c0, nch, dst) in ((6, 4, k.qkT_b_d), (14, 8, k.qkT_c_d)):
                for j in range(nch):
                    pf, bpf = fm_mm(c0 + j)
                    pr, bpr = fm_mm(c0 + nch + j)
                    t1, bt1 = t1_r.next()
                    t2, bt2 = t2_r.next()
                    o, bo = oqk_r.next()
                    S.op("dve", lambda e, t1=t1, pf=pf: e.tensor_tensor(out=t1, in0=pf[:], in1=cg, op=ALU.mult),
                         reads=[bpf, bcg], writes=[bt1])
                    S.op("dve", lambda e, t2=t2, pr=pr: e.tensor_tensor(out=t2, in0=pr[:], in1=sg, op=ALU.mult),
                         reads=[bpr, bsg], writes=[bt2])
                    S.op("dve", lambda e, t1=t1, t2=t2, o=o: e.tensor_tensor(out=o, in0=t1, in1=t2, op=ALU.add),
                         reads=[bt1, bt2], writes=[bo])
                    S.dma("sp", dst[j * 128:(j + 1) * 128, tsl], o, reads=[bo])
            for s in range(4):
                t0 = g * 512 + s * 128

                def tm_mm(c0, n):
                    pm, bpm = ptm_r.next()
                    for kk in range(8):
                        S.op("pe", lambda e, kk=kk, pm=pm: e.matmul(pm[:, 0:n], lhsT=hT3[:, kk, s * 128:(s + 1) * 128],
                                                                   rhs=wb[:, kk, c0:c0 + n], start=(kk == 0), stop=(kk == 7)),
                             reads=[bwb[kk], bhT], writes=[bpm])
                    return pm, bpm

                pm, bpm = tm_mm(3840, 272)
                o, bo = oza_r.next()
                S.op("act", lambda e, o=o, pm=pm: e.copy(out=o, in_=pm[:, 0:272]), reads=[bpm], writes=[bo])
                S.dma("sp", k.za_d[t0:t0 + 128, :], o, reads=[bo])
                pm, bpm = tm_mm(4112, 256)
                o, bo = ovb_r.next()
                o3 = o.rearrange("p (h d) -> p h d", h=4)
                S.op("act", lambda e, o3=o3, pm=pm: e.copy(out=o3[:, :, 0:64], in_=pm[:, 0:256].rearrange("p (h d) -> p h d", h=4)),
                     reads=[bpm], writes=[bo])
                S.dma("sp", k.v_b_d[t0:t0 + 128, :], o, reads=[bo])
                pm, bpm = tm_mm(4368, 512)
                o, bo = ovc_r.next()
                o3 = o.rearrange("p (h d) -> p h d", h=4)
                S.op("act", lambda e, o3=o3, pm=pm: e.copy(out=o3[:, :, 0:128], in_=pm[:, 0:512].rearrange("p (h d) -> p h d", h=4)),
                     reads=[bpm], writes=[bo])
                S.dma("sp", k.v_c_d[t0:t0 + 128, :], o, reads=[bo])
        S.barrier()


def phase_c(k, l):
    S, nc = k.S, k.nc
    k.uid += 1
    lam_init = 0.8 - 0.6 * math.exp(-0.3 * l)
    with nc.sbuf_tensor("c_kT%d" % l, [128, S_LEN], BF16) as kT, nc.sbuf_tensor("c_qT%d" % l, [128, S_LEN], BF16) as qT, \
            nc.sbuf_tensor("c_v%d" % l, [128, NT, 129], BF16) as va, \
            nc.sbuf_tensor("c_arena%d" % l, [128, 3000], F32) as arena, nc.sbuf_tensor("c_arenab%d" % l, [128, 4000], BF16) as arenab:
        off = [0]

        offb = [0]

        def carve(n_f32, dt=F32):
            if dt == BF16:
                a = arenab[:, offb[0]:offb[0] + 2 * n_f32]
                offb[0] += 2 * n_f32
                return a
            a = arena[:, off[0]:off[0] + n_f32]
            off[0] += n_f32
            return a

        bkT, bqT, bva = Buf(), Buf(), Buf()
        p_r = Rot([(carve(256, BF16), Buf()) for _ in range(3)])
        o1_r = Rot([(carve(512), Buf()) for _ in range(1)])
        rc_r = Rot([(carve(4), Buf()) for _ in range(2)])
        o_r = Rot([(carve(128), Buf()) for _ in range(2)])
        junk_r = Rot([(carve(128), Buf()) for _ in range(1)])
        ss_r = Rot([(carve(1), Buf()) for _ in range(2)])
        ob_r = Rot([(carve(64, BF16), Buf()) for _ in range(2)])
        lam, blam = carve(4), Buf()
        ps_r = Rot([(k.pb[i], k.bpb[i]) for i in (0, 1)])
        acc = [(k.pb[i], k.bpb[i]) for i in (2, 3, 4, 5)]
        sp_ = k.smallp
        c0 = 256 * l
        jk, bjk = junk_r.next()
        S.op("dve", lambda e: e.tensor_tensor(out=jk[:, 0:64], in0=sp_[:, c0:c0 + 64], in1=sp_[:, c0 + 64:c0 + 128], op=ALU.mult),
             reads=[k.bsmallp], writes=[bjk])
        S.op("dve", lambda e: e.tensor_reduce(out=lam[:, 0:1], in_=jk[:, 0:64], axis=AX.X, op=ALU.add), reads=[bjk], writes=[blam])
        S.op("dve", lambda e: e.tensor_tensor(out=jk[:, 0:64], in0=sp_[:, c0 + 128:c0 + 192], in1=sp_[:, c0 + 192:c0 + 256], op=ALU.mult),
             reads=[k.bsmallp, blam], writes=[bjk])
        S.op("dve", lambda e: e.tensor_reduce(out=lam[:, 1:2], in_=jk[:, 0:64], axis=AX.X, op=ALU.add), reads=[bjk, blam], writes=[blam])
        S.op("act", lambda e: e.activation(out=lam[:, 0:2], in_=lam[:, 0:2], func=AF.Exp), reads=[blam], writes=[blam])
        S.op("dve", lambda e: e.tensor_tensor(out=lam[:, 2:3], in0=lam[:, 1:2], in1=lam[:, 0:1], op=ALU.subtract), reads=[blam], writes=[blam])
        S.op("dve", lambda e: e.tensor_scalar(out=lam[:, 2:3], in0=lam[:, 2:3], scalar1=-lam_init, scalar2=None, op0=ALU.add),
             reads=[blam], writes=[blam])
        subw = sp_[:, 512 + 128 * l:512 + 128 * (l + 1)]
        for hh in range(4):
            S.dma("sp", qT[:], k.qkT_c_d[hh * 128:(hh + 1) * 128, :], writes=[bqT])
            S.dma("sp", kT[:], k.qkT_c_d[512 + hh * 128:512 + (hh + 1) * 128, :], writes=[bkT])
            S.dma("sp", va[:], k.v_c_d.rearrange("(t p) (h d) -> p t h d", p=128, h=4)[:, :, hh, :], writes=[bva])
            for qg in range(8):
                o1, bo1 = o1_r.next()
                for c in range(2):
                    rs = slice(c * 64, (c + 1) * 64)
                    for kb in range(NT):
                        ps, bps = ps_r.next()
                        S.op("pe", lambda e, ps=ps, kb=kb, rs=rs: e.matmul(ps[:], lhsT=kT[rs, kb * 128:(kb + 1) * 128],
                                                                          rhs=qT[rs, qg * 512:(qg + 1) * 512], start=True, stop=True),
                             reads=[bkT, bqT], writes=[bps])
                        p, bp = p_r.next()
                        S.op("act", lambda e, p=p, ps=ps: e.activation(out=p, in_=ps[:], func=AF.Exp, scale=0.125),
                             reads=[bps], writes=[bp])
                        for s in range(4):
                            a, ba = acc[s]
                            S.op("pe", lambda e, a=a, p=p, s=s, kb=kb: e.matmul(a[:, 0:129], lhsT=p[:, s * 128:(s + 1) * 128],
                                                                               rhs=va[:, kb, :], start=(kb == 0), stop=(kb == NT - 1)),
                                 reads=[bp, bva], writes=[ba])
                    for s in range(4):
                        a, ba = acc[s]
                        rc, brc = rc_r.next()
                        S.op("dve", lambda e, rc=rc, a=a: e.reciprocal(out=rc[:, 0:1], in_=a[:, 128:129]), reads=[ba], writes=[brc])
                        if c == 0:
                            S.op("dve", lambda e, a=a, rc=rc, s=s: e.tensor_scalar(out=o1[:, s * 128:(s + 1) * 128], in0=a[:, 0:128],
                                                                                  scalar1=rc[:, 0:1], scalar2=None, op0=ALU.mult),
                                 reads=[ba, brc], writes=[bo1])
                        else:
                            S.op("dve", lambda e, rc=rc: e.tensor_tensor(out=rc[:, 1:2], in0=rc[:, 0:1], in1=lam[:, 2:3], op=ALU.mult),
                                 reads=[brc, blam], writes=[brc])
                            o, bo = o_r.next()
                            S.op("dve", lambda e, a=a, rc=rc, s=s, o=o: e.scalar_tensor_tensor(
                                out=o, in0=a[:, 0:128], scalar=rc[:, 1:2], in1=o1[:, s * 128:(s + 1) * 128], op0=ALU.mult, op1=ALU.add),
                                 reads=[ba, brc, bo1], writes=[bo])
                            ss, bss = ss_r.next()
                            jk, bjk = junk_r.next()
                            S.op("act", lambda e, jk=jk, o=o, ss=ss: e.activation(out=jk, in_=o, func=AF.Square, accum_out=ss),
                                 reads=[bo], writes=[bjk, bss])
                            S.op("dve", lambda e, ss=ss: e.tensor_scalar(out=ss, in0=ss, scalar1=1.0 / 128, scalar2=EPS, op0=ALU.mult, op1=ALU.add),
                                 reads=[bss], writes=[bss])
                            S.op("act", lambda e, ss=ss: e.activation(out=ss, in_=ss, func=AF.Sqrt), reads=[bss], writes=[bss])
                            S.op("dve", lambda e, ss=ss: e.reciprocal(out=ss, in_=ss), reads=[bss], writes=[bss])
                            S.op("dve", lambda e, ss=ss: e.tensor_scalar(out=ss, in0=ss, scalar1=1.0 - lam_init, scalar2=None, op0=ALU.mult),
                                 reads=[bss], writes=[bss])
                            ob, bob = ob_r.next()
                            S.op("dve", lambda e, ob=ob, o=o, ss=ss: e.scalar_tensor_tensor(out=ob, in0=o, scalar=ss[:, 0:1], in1=subw,
                                                                                            op0=ALU.mult, op1=ALU.mult),
                                 reads=[bo, bss, k.bsmallp], writes=[bob])
                            t0 = qg * 512 + s * 128
                            S.dma("sp", k.mix_d[t0:t0 + 128, 512 + hh * 128:512 + (hh + 1) * 128], ob, reads=[bob])
        S.barrier()


def phase_b(k, l):
    S, nc = k.S, k.nc
    k.uid += 1
    with nc.sbuf_tensor("b_kT%d" % l, [128, S_LEN], BF16) as kT, nc.sbuf_tensor("b_qT%d" % l, [128, S_LEN], BF16) as qT, \
            nc.sbuf_tensor("b_v%d" % l, [128, NT, 65], BF16) as va, \
            nc.sbuf_tensor("b_mask%d" % l, [128, 17, 128], F32) as mask, \
            nc.sbuf_tensor("b_arena%d" % l, [128, 2000], F32) as arena, nc.sbuf_tensor("b_arenab%d" % l, [128, 4000], BF16) as arenab:
        off = [0]

        offb = [0]

        def carve(n_f32, dt=F32):
            if dt == BF16:
                a = arenab[:, offb[0]:offb[0] + 2 * n_f32]
                offb[0] += 2 * n_f32
                return a
            a = arena[:, off[0]:off[0] + n_f32]
            off[0] += n_f32
            return a

        bkT, bqT, bva, bmask = Buf(), Buf(), Buf(), Buf()
        S.dma("sp", mask[:], k.mask_d, writes=[bmask])
        pe_r = Rot([(carve(512), Buf()) for _ in range(2)])
        p_r = Rot([(carve(256, BF16), Buf()) for _ in range(3)])
        rc_r = Rot([(carve(1), Buf()) for _ in range(2)])
        ob_r = Rot([(carve(32, BF16), Buf()) for _ in range(2)])
        ps_r = Rot([(k.pb[i], k.bpb[i]) for i in (0, 1, 2)])
        acc_r = Rot([(k.pb[i], k.bpb[i]) for i in (3, 4)])
        for hp in range(2):
            S.dma("sp", qT[:], k.qkT_b_d[hp * 128:(hp + 1) * 128, :], writes=[bqT])
            S.dma("sp", kT[:], k.qkT_b_d[256 + hp * 128:256 + (hp + 1) * 128, :], writes=[bkT])
            for hl in range(2):
                hh = hp * 2 + hl
                rs = slice(hl * 64, (hl + 1) * 64)
                S.dma("sp", va[:], k.v_b_d.rearrange("(t p) (h d) -> p t h d", p=128, h=4)[:, :, hh, :], writes=[bva])
                for qb in range(NT):
                    lo, hi = max(0, qb - 8), min(NT - 1, qb + 8)
                    kbs = list(range(lo, hi + 1))
                    a, ba = acc_r.next()
                    for b0 in range(0, len(kbs), 4):
                        grp = kbs[b0:b0 + 4]
                        n = len(grp)
                        ps, bps = ps_r.next()
                        for i, kb in enumerate(grp):
                            S.op("pe", lambda e, ps=ps, kb=kb, i=i: e.matmul(ps[:, i * 128:(i + 1) * 128], lhsT=kT[rs, kb * 128:(kb + 1) * 128],
                                                                            rhs=qT[rs, qb * 128:(qb + 1) * 128], start=True, stop=True),
                                 reads=[bkT, bqT], writes=[bps])
                        pe_, bpe = pe_r.next()
                        S.op("act", lambda e, pe_=pe_, ps=ps, n=n: e.activation(out=pe_[:, 0:n * 128], in_=ps[:, 0:n * 128], func=AF.Exp, scale=0.125),
                             reads=[bps], writes=[bpe])
                        p, bp = p_r.next()
                        d0 = grp[0] - qb + 8
                        S.op("dve", lambda e, p=p, pe_=pe_, n=n, d0=d0: e.tensor_tensor(
                            out=p[:, 0:n * 128], in0=pe_[:, 0:n * 128], in1=mask[:, d0:d0 + n, :].rearrange("p a b -> p (a b)"), op=ALU.mult),
                             reads=[bpe, bmask], writes=[bp])
                        for i, kb in enumerate(grp):
                            S.op("pe", lambda e, a=a, p=p, i=i, kb=kb: e.matmul(a[:, 0:65], lhsT=p[:, i * 128:(i + 1) * 128], rhs=va[:, kb, :],
                                                                               start=(kb == lo), stop=(kb == hi)),
                                 reads=[bp, bva], writes=[ba])
                    rc, brc = rc_r.next()
                    S.op("dve", lambda e, rc=rc, a=a: e.reciprocal(out=rc, in_=a[:, 64:65]), reads=[ba], writes=[brc])
                    ob, bob = ob_r.next()
                    S.op("dve", lambda e, ob=ob, a=a, rc=rc: e.tensor_scalar(out=ob, in0=a[:, 0:64], scalar1=rc[:, 0:1], scalar2=None, op0=ALU.mult),
                         reads=[ba, brc], writes=[bob])
                    S.dma("sp", k.mix_d[qb * 128:(qb + 1) * 128, 256 + hh * 64:256 + (hh + 1) * 64], ob, reads=[bob])
        S.barrier()


def phase_d1(k, l):
    S, nc = k.S, k.nc
    k.uid += 1
    src = k.x_d if l == 0 else k.xres_d
    with nc.sbuf_tensor("d1_w%d" % l, [128, 8, D], BF16) as wo, nc.sbuf_tensor("d1_arena%d" % l, [128, 5000], F32) as arena, nc.sbuf_tensor("d1_arenab%d" % l, [128, 5000], BF16) as arenab:
        off = [0]

        offb = [0]

        def carve(n_f32, dt=F32):
            if dt == BF16:
                a = arenab[:, offb[0]:offb[0] + 2 * n_f32]
                offb[0] += 2 * n_f32
                return a
            a = arena[:, off[0]:off[0] + n_f32]
            off[0] += n_f32
            return a

        bwo = Buf()
        S.dma("pool", wo[:], k.w_out_d[l].rearrange("(k p) n -> p k n", p=128), writes=[bwo])
        xt_r = Rot([(carve(1024), Buf()) for _ in range(2)])
        m_r = Rot([(carve(512, BF16), Buf()) for _ in range(2)])
        mT_r = Rot([(carve(512, BF16), Buf()) for _ in range(2)])
        xo_r = Rot([(carve(1024), Buf()) for _ in range(2)])
        ptr_r = Rot([(k.pb[0], k.bpb[0]), (k.pb[1], k.bpb[1])])
        po_r = Rot([(k.pb[i], k.bpb[i]) for i in (2, 3, 4, 5)])
        for t in range(NT):
            t0 = t * 128
            xt, bx = xt_r.next()
            m, bm = m_r.next()
            S.dma("sp", xt, src[t0:t0 + 128, :], writes=[bx])
            S.dma("sp", m, k.mix_d[t0:t0 + 128, :], writes=[bm])
            pt, bpt = ptr_r.next()
            ptb = pt[:].bitcast(BF16).rearrange("p (k n) -> p k n", k=8)
            for kk in range(8):
                S.op("pe", lambda e, kk=kk, ptb=ptb, m=m: e.transpose(out=ptb[:, kk, :], in_=m[:, kk * 128:(kk + 1) * 128], identity=k.identb[:]),
                     reads=[bm, k.bidentb], writes=[bpt])
            mT, bmT = mT_r.next()
            mT3 = mT.rearrange("p (k n) -> p k n", k=8)
            S.op("act", lambda e, mT3=mT3, ptb=ptb: e.copy(out=mT3, in_=ptb), reads=[bpt], writes=[bmT])
            xo, bxo = xo_r.next()
            for half in range(2):
                po, bpo = po_r.next()
                for kk in range(8):
                    S.op("pe", lambda e, kk=kk, po=po, mT3=mT3, half=half: e.matmul(po[:], lhsT=mT3[:, kk, :], rhs=wo[:, kk, half * 512:(half + 1) * 512],
                                                                                   start=(kk == 0), stop=(kk == 7)),
                         reads=[bmT, bwo], writes=[bpo])
                S.op("dve", lambda e, xo=xo, po=po, xt=xt, half=half: e.tensor_tensor(out=xo[:, half * 512:(half + 1) * 512], in0=po[:],
                                                                                     in1=xt[:, half * 512:(half + 1) * 512], op=ALU.add),
                     reads=[bpo, bx], writes=[bxo])
            S.dma("sp", k.xres_d[t0:t0 + 128, :], xo, reads=[bxo])
        S.barrier()


def phase_d2(k, l):
    S, nc = k.S, k.nc
    k.uid += 1
    last = (l == L - 1)
    NFF = DFF // 128
    with nc.sbuf_tensor("d2_wg%d" % l, [128, 8, DFF], BF16) as wg, nc.sbuf_tensor("d2_wu%d" % l, [128, 8, DFF], BF16) as wu, \
            nc.sbuf_tensor("d2_wd%d" % l, [128, NFF, D], BF16) as wd, nc.sbuf_tensor("d2_nwb%d" % l, [128, D], F32) as nwb, \
            nc.sbuf_tensor("d2_fnw%d" % l, [128, D], F32) as fnw, \
            nc.sbuf_tensor("d2_arena%d" % l, [128, 7700], F32) as arena, nc.sbuf_tensor("d2_arenab%d" % l, [128, 11800], BF16) as arenab:
        off = [0]

        offb = [0]

        def carve(n_f32, dt=F32):
            if dt == BF16:
                a = arenab[:, offb[0]:offb[0] + 2 * n_f32]
                offb[0] += 2 * n_f32
                return a
            a = arena[:, off[0]:off[0] + n_f32]
            off[0] += n_f32
            return a

        bwg = [Buf() for _ in range(8)]
        bwu = [Buf() for _ in range(8)]
        bwd, bnwb, bfnw = Buf(), Buf(), Buf()
        for kk in range(8):
            S.dma("pool", wg[:, kk, :], k.w_gate_d[l, kk * 128:(kk + 1) * 128, :], writes=[bwg[kk]])
            S.dma("pool", wu[:, kk, :], k.w_up_d[l, kk * 128:(kk + 1) * 128, :], writes=[bwu[kk]])
        S.dma("pool", wd[:], k.w_down_d[l].rearrange("(k p) n -> p k n", p=128), writes=[bwd])
        S.dma("sp", nwb[:], k.ffn_nw_d[l:l + 1, :].broadcast_to([128, D]), writes=[bnwb])
        if last:
            S.dma("sp", fnw[:], k.final_nw_d.rearrange("(o n) -> o n", o=1).broadcast_to([128, D]), writes=[bfnw])
        TG = 256
        xt_r = Rot([(carve(1024), Buf()) for _ in range(3)])
        junk_r = Rot([(carve(1024), Buf()) for _ in range(1)])
        h_r = Rot([(carve(512, BF16), Buf()) for _ in range(2)])
        ss_r = Rot([(carve(1), Buf()) for _ in range(2)])
        hT_r = Rot([(carve(1024, BF16), Buf()) for _ in range(2)])
        aT_r = Rot([(carve(NFF * TG // 2, BF16), Buf()) for _ in range(1)])
        sg_r = Rot([(carve(TG), Buf()) for _ in range(2)])
        xo_r = Rot([(carve(1024), Buf()) for _ in range(2)])
        yo_r = Rot([(carve(1024), Buf()) for _ in range(1)])
        ptr_r = Rot([(k.pb[0], k.bpb[0]), (k.pb[1], k.bpb[1])])
        pg_r = Rot([(k.pb[i], k.bpb[i]) for i in (2, 3)])
        pu_r = Rot([(k.pb[i], k.bpb[i]) for i in (4, 5)])
        po_r = Rot([(k.pb[i], k.bpb[i]) for i in (6, 7)])
        for g in range(S_LEN // TG):
            hT, bhT = hT_r.next()
            hT3 = hT.rearrange("p (k n) -> p k n", k=8)
            xts = []
            for s in range(TG // 128):
                t0 = g * TG + s * 128
                xt, bx = xt_r.next()
                xts.append((xt, bx))
                h, bh = h_r.next()
                S.dma("sp", xt, k.xres_d[t0:t0 + 128, :], writes=[bx])
                norm_tile(k, xt, bx, nwb, bnwb, h, bh, ss_r, junk_r)
                pt, bpt = ptr_r.next()
                ptb = pt[:].bitcast(BF16).rearrange("p (k n) -> p k n", k=8)
                for kk in range(8):
                    S.op("pe", lambda e, kk=kk, ptb=ptb, h=h: e.transpose(out=ptb[:, kk, :], in_=h[:, kk * 128:(kk + 1) * 128], identity=k.identb[:]),
                         reads=[bh, k.bidentb], writes=[bpt])
                S.op("dve", lambda e, ptb=ptb, hT3=hT3, s=s: e.tensor_copy(out=hT3[:, :, s * 128:(s + 1) * 128], in_=ptb),
                     reads=[bpt], writes=[bhT])
            aT, baT = aT_r.next()
            aT3 = aT.rearrange("p (f n) -> p f n", f=NFF)
            for f in range(NFF):
                pg, bpg = pg_r.next()
                pu, bpu = pu_r.next()
                for kk in range(8):
                    S.op("pe", lambda e, kk=kk, pg=pg, f=f: e.matmul(pg[:, 0:TG], lhsT=wg[:, kk, f * 128:(f + 1) * 128], rhs=hT3[:, kk, :],
                                                                    start=(kk == 0), stop=(kk == 7)),
                         reads=[bwg[kk], bhT], writes=[bpg])
                for kk in range(8):
                    S.op("pe", lambda e, kk=kk, pu=pu, f=f: e.matmul(pu[:, 0:TG], lhsT=wu[:, kk, f * 128:(f + 1) * 128], rhs=hT3[:, kk, :],
                                                                    start=(kk == 0), stop=(kk == 7)),
                         reads=[bwu[kk], bhT], writes=[bpu])
                sg, bsg = sg_r.next()
                S.op("act", lambda e, sg=sg, pg=pg: e.activation(out=sg, in_=pg[:, 0:TG], func=AF.Silu), reads=[bpg], writes=[bsg])
                S.op("dve", lambda e, sg=sg, pu=pu, f=f, aT3=aT3: e.tensor_tensor(out=aT3[:, f, :], in0=pu[:, 0:TG], in1=sg, op=ALU.mult),
                     reads=[bpu, bsg], writes=[baT])
            for s in range(TG // 128):
                t0 = g * TG + s * 128
                xt, bx = xts[s]
                xo, bxo = xo_r.next()
                for half in range(2):
                    po, bpo = po_r.next()
                    for f in range(NFF):
                        S.op("pe", lambda e, f=f, po=po, s=s, half=half, aT3=aT3: e.matmul(po[:], lhsT=aT3[:, f, s * 128:(s + 1) * 128],
                                                                                         rhs=wd[:, f, half * 512:(half + 1) * 512],
                                                                                         start=(f == 0), stop=(f == NFF - 1)),
                             reads=[baT, bwd], writes=[bpo])
                    S.op("dve", lambda e, xo=xo, po=po, xt=xt, half=half: e.tensor_tensor(out=xo[:, half * 512:(half + 1) * 512], in0=po[:],
                                                                                         in1=xt[:, half * 512:(half + 1) * 512], op=ALU.add),
                         reads=[bpo, bx], writes=[bxo])
                if not last:
                    S.dma("sp", k.xres_d[t0:t0 + 128, :], xo, reads=[bxo])
                else:
                    yo, byo = yo_r.next()
                    norm_tile(k, xo, bxo, fnw, bfnw, yo, byo, ss_r, junk_r)
                    S.dma("sp", k.out_d[t0:t0 + 128, :], yo, reads=[byo])
        S.barrier()


def build(dbg=False, phases=None):
    nc = bass.Bass("TRN2", target_bir_lowering=False)
    k = K()
    k.nc = nc
    k.uid = 0
    k.S = Sched(nc)
    ext = "ExternalInput"
    k.x_d = nc.dram_tensor("x", [S_LEN, D], F32, kind=ext).ap()
    k.pos_d = nc.dram_tensor("pos", [S_LEN], I32, kind=ext).ap()
    k.w_in_d = nc.dram_tensor("w_in", [L, D, NCOL], F32, kind=ext).ap()
    k.attn_nw_d = nc.dram_tensor("attn_nw", [L, D], F32, kind=ext).ap()
    k.w_out_d = nc.dram_tensor("w_out", [L, D, D], F32, kind=ext).ap()
    k.ffn_nw_d = nc.dram_tensor("ffn_nw", [L, D], F32, kind=ext).ap()
    k.w_gate_d = nc.dram_tensor("w_gate", [L, D, DFF], F32, kind=ext).ap()
    k.w_up_d = nc.dram_tensor("w_up", [L, D, DFF], F32, kind=ext).ap()
    k.w_down_d = nc.dram_tensor("w_down", [L, DFF, D], F32, kind=ext).ap()
    k.final_nw_d = nc.dram_tensor("final_nw", [D], F32, kind=ext).ap()
    k.smallp_d = nc.dram_tensor("smallp", [128, 1024], F32, kind=ext).ap()
    k.cst_d = nc.dram_tensor("cst", [128, 4], F32, kind=ext).ap()
    k.mask_d = nc.dram_tensor("maskb", [128, 17, 128], F32, kind=ext).ap()
    k.convw_d = nc.dram_tensor("convw", [L, 128, 30], F32, kind=ext).ap()
    k.tri_d = nc.dram_tensor("tri", [128, 4, 128], F32, kind=ext).ap()
    k.out_d = nc.dram_tensor("out", [S_LEN, D], F32, kind="ExternalOutput").ap()
    sk = "ExternalOutput" if dbg else "Internal"
    k.cosT_d = nc.dram_tensor("cosT", [128, S_LEN], F32, kind=sk).ap()
    k.sinT_d = nc.dram_tensor("sinT", [128, S_LEN], F32, kind=sk).ap()
    k.qkvT_a_d = nc.dram_tensor("qkvT_a", [768, S_LEN], F32, kind=sk).ap()
    k.za_d = nc.dram_tensor("za", [S_LEN, 272], F32, kind=sk).ap()
    k.qkT_b_d = nc.dram_tensor("qkT_b", [512, S_LEN], BF16, kind=sk).ap()
    k.qkT_c_d = nc.dram_tensor("qkT_c", [1024, S_LEN], BF16, kind=sk).ap()
    k.v_b_d = nc.dram_tensor("v_b", [S_LEN, 4 * 65], BF16, kind=sk).ap()
    k.v_c_d = nc.dram_tensor("v_c", [S_LEN, 4 * 129], BF16, kind=sk).ap()
    k.mix_d = nc.dram_tensor("mix", [S_LEN, D], BF16, kind=sk).ap()
    k.xres_d = nc.dram_tensor("xres", [S_LEN, D], F32, kind=sk).ap()
    k.identf = nc.alloc_sbuf_tensor("identf", [128, 128], F32)
    k.identb = nc.alloc_sbuf_tensor("identb", [128, 128], BF16)
    k.cst = nc.alloc_sbuf_tensor("cst_sb", [128, 4], F32)
    k.smallp = nc.alloc_sbuf_tensor("smallp_sb", [128, 1024], F32)
    k.bidentf, k.bidentb, k.bcst, k.bsmallp = Buf(), Buf(), Buf(), Buf()
    k.pb = [nc.alloc_psum_tensor("pb%d" % i, [128, 512], F32) for i in range(8)]
    k.bpb = [Buf("pb%d" % i, excl=True) for i in range(8)]
    k.S.dma("sp", k.smallp[:], k.smallp_d, writes=[k.bsmallp])
    if phases is None:
        phases = ["setup"] + [p + str(l) for l in range(L) for p in ("a", "c", "b", "g", "d1", "d2")]
    for ph in phases:
        if ph == "setup":
            phase_setup(k)
        elif ph[0] == "a":
            phase_a(k, int(ph[1:]))
        elif ph[0] == "c":
            phase_c(k, int(ph[1:]))
        elif ph[0] == "b":
            phase_b(k, int(ph[1:]))
        elif ph[0] == "g":
            phase_g(k, int(ph[1:]))
        elif ph[0] == "z":
            phase_z(k)
        elif ph[:2] == "d1":
            phase_d1(k, int(ph[2:]))
        elif ph[:2] == "d2":
            phase_d2(k, int(ph[2:]))
    k.S.finish()
    k.ninstr = k.S.ninstr
    return nc, k


def phase_z(k):
    S, nc = k.S, k.nc
    with nc.sbuf_tensor("z_t", [128, 256], BF16) as zt:
        bz = Buf()
        S.op("dve", lambda e: e.memset(zt[:], 0.0), writes=[bz])
        for t in range(NT):
            S.dma("sp", k.mix_d[t * 128:(t + 1) * 128, 0:256], zt[:], reads=[bz])
        S.barrier()


def phase_g(k, l):
    S, nc = k.S, k.nc
    k.uid += 1
    sp_ = k.smallp
    with nc.sbuf_tensor("g_qkvn%d" % l, [128, 6, S_LEN], F32) as qkvn, \
            nc.sbuf_tensor("g_oall%d" % l, [128, NT, 256], F32) as oall, \
            nc.sbuf_tensor("g_gate%d" % l, [128, NT, 16], F32) as gab, \
            nc.sbuf_tensor("g_g%d" % l, [128, NT, 8], F32) as gg, \
            nc.sbuf_tensor("g_beta%d" % l, [128, NT, 8], F32) as beta, \
            nc.sbuf_tensor("g_cw%d" % l, [128, 30], F32) as cw, \
            nc.sbuf_tensor("g_tri%d" % l, [128, 4, 128], F32) as tri, \
            nc.sbuf_tensor("g_bd%d" % l, [128, 128], F32) as bd, \
            nc.sbuf_tensor("g_ones%d" % l, [128, 128], F32) as onesf, \
            nc.sbuf_tensor("g_st%d" % l, [128, 2, 2, 128], F32) as st, \
            nc.sbuf_tensor("g_obt%d" % l, [128, 2, 256], BF16) as obt, \
            nc.sbuf_tensor("g_arena%d" % l, [128, 15000], F32) as arena:
        off = [0]

        def carve(n):
            a = arena[:, off[0]:off[0] + n]
            off[0] += n
            return a

        bq = [Buf() for _ in range(6)]
        boall, bgab, bgg, bbeta, bcw, btri, bbd, bones = Buf(), Buf(), Buf(), Buf(), Buf(), Buf(), Buf(), Buf()
        bst = [Buf(), Buf()]
        S.dma("sp", cw[:], k.convw_d[l], writes=[bcw])
        S.dma("sp", tri[:], k.tri_d, writes=[btri])
        for t in range(NT):
            S.dma("sp", gab[:, t, :], k.za_d[t * 128:(t + 1) * 128, 256:272], writes=[bgab])
        S.op("dve", lambda e: e.memset(bd[:], 0.0), writes=[bbd])
        S.op("dve", lambda e: e.memset(bd[0:64, 0:64], 1.0), writes=[bbd])
        S.op("dve", lambda e: e.memset(bd[64:128, 64:128], 1.0), writes=[bbd])
        S.op("dve", lambda e: e.memset(onesf[:], 1.0), writes=[bones])
        S.op("dve", lambda e: e.memset(oall[:], 0.0), writes=[boall])
        S.op("dve", lambda e: e.memset(st[:], 0.0), writes=bst)
        ab = sp_[:, 896 + 8 * l:904 + 8 * l]
        db = sp_[:, 912 + 8 * l:920 + 8 * l]
        eal, beal = carve(8), Buf()
        S.op("act", lambda e: e.activation(out=eal, in_=ab, func=AF.Exp), reads=[k.bsmallp], writes=[beal])
        S.op("dve", lambda e: e.tensor_tensor(out=gg[:], in0=gab[:, :, 0:8], in1=db.rearrange("p (o c) -> p o c", o=1).broadcast_to([128, NT, 8]), op=ALU.add),
             reads=[bgab, k.bsmallp], writes=[bgg])
        S.op("act", lambda e: e.activation(out=gg[:], in_=gg[:], func=AF.Exp), reads=[bgg], writes=[bgg])
        S.op("act", lambda e: e.activation(out=gg[:], in_=gg[:], func=AF.Ln, bias=1.0), reads=[bgg], writes=[bgg])
        S.op("dve", lambda e: e.scalar_tensor_tensor(out=gg[:], in0=gg[:], scalar=-1.0, in1=eal.rearrange("p (o c) -> p o c", o=1).broadcast_to([128, NT, 8]),
                                                     op0=ALU.mult, op1=ALU.mult), reads=[bgg, beal], writes=[bgg])
        S.op("act", lambda e: e.activation(out=beta[:], in_=gab[:, :, 8:16], func=AF.Sigmoid), reads=[bgab], writes=[bbeta])
        xin, bxin = carve(S_LEN + 4), Buf()
        rs_r = Rot([(carve(512), Buf()) for _ in range(2)])
        S.op("dve", lambda e: e.memset(xin[:, 0:2], 0.0), writes=[bxin])
        S.op("dve", lambda e: e.memset(xin[:, S_LEN + 2:S_LEN + 4], 0.0), writes=[bxin])
        for ch in range(6):
            S.dma("sp", xin[:, 2:S_LEN + 2], k.qkvT_a_d[ch * 128:(ch + 1) * 128, :], writes=[bxin])
            qc = qkvn[:, ch, :]
            S.op("dve", lambda e: e.tensor_scalar(out=qc, in0=xin[:, 0:S_LEN], scalar1=cw[:, ch * 5:ch * 5 + 1], scalar2=None, op0=ALU.mult),
                 reads=[bxin, bcw], writes=[bq[ch]])
            for j in range(1, 5):
                S.op("dve", lambda e: e.scalar_tensor_tensor(out=qc, in0=xin[:, j:j + S_LEN], scalar=cw[:, ch * 5 + j:ch * 5 + j + 1], in1=qc,
                                                             op0=ALU.mult, op1=ALU.add), reads=[bxin, bcw, bq[ch]], writes=[bq[ch]])
            S.op("act", lambda e: e.activation(out=qc, in_=qc, func=AF.Silu), reads=[bq[ch]], writes=[bq[ch]])
            if ch < 4:
                S.op("act", lambda e: e.activation(out=xin[:, 2:S_LEN + 2], in_=qc, func=AF.Square), reads=[bq[ch]], writes=[bxin])
                for c8 in range(8):
                    cs = slice(c8 * 512, (c8 + 1) * 512)
                    pp, bpp = k.pb[c8 % 2], k.bpb[c8 % 2]
                    S.op("pe", lambda e: e.matmul(pp[:], lhsT=bd[:], rhs=xin[:, 2 + c8 * 512:2 + (c8 + 1) * 512], start=True, stop=True),
                         reads=[bbd, bxin], writes=[bpp])
                    rs, brs = rs_r.next()
                    S.op("dve", lambda e: e.tensor_scalar(out=rs, in0=pp[:], scalar1=1e-6, scalar2=None, op0=ALU.add), reads=[bpp], writes=[brs])
                    S.op("act", lambda e: e.activation(out=rs, in_=rs, func=AF.Sqrt, scale=(64.0 if ch < 2 else 1.0)), reads=[brs], writes=[brs])
                    S.op("dve", lambda e: e.reciprocal(out=rs, in_=rs), reads=[brs], writes=[brs])
                    S.op("dve", lambda e: e.tensor_tensor(out=qkvn[:, ch, cs], in0=qkvn[:, ch, cs], in1=rs, op=ALU.mult), reads=[bq[ch], brs], writes=[bq[ch]])
        def rot(n, cnt):
            return Rot([(carve(n), Buf()) for _ in range(cnt)])
        gc_r = rot(8, 4)
        eg_r = rot(12, 4)
        bge_r = rot(4, 4)
        ktok_r, vtok_r = rot(256, 2), rot(256, 2)
        vb_r, kbg_r, kdec_r = rot(256, 2), rot(256, 2), rot(256, 2)
        dg_r, expg_r, d1_r, d2_r = rot(128, 2), rot(128, 2), rot(128, 2), rot(128, 2)
        a_r, at_r = rot(128, 2), rot(128, 2)
        intra_r = rot(128, 8)
        xa_r, xb_r, p_r = rot(128, 3), rot(128, 3), rot(128, 3)
        u_r, kcT_r, qgT_r = rot(256, 2), rot(256, 2), rot(256, 2)
        vnew_r = rot(256, 2)
        bank = lambda i: (k.pb[i], k.bpb[i])
        nm_r = Rot([bank(4), bank(5)])

        def unit(t, dr):
            ts = slice(t * 128, (t + 1) * 128)
            MI, MS, MSA = tri[:, dr, :], tri[:, 2 + dr, :], tri[:, 3 - dr, :]
            g_t = gg[:, t, dr * 4:(dr + 1) * 4]
            be_t = beta[:, t, dr * 4:(dr + 1) * 4]
            p0, bp0 = bank(0)
            S.op("pe", lambda e: e.matmul(p0[:, 0:4], lhsT=MI, rhs=g_t, start=True, stop=True), reads=[btri, bgg], writes=[bp0])
            S.op("pe", lambda e: e.matmul(p0[:, 4:8], lhsT=onesf[:], rhs=g_t, start=True, stop=True), reads=[bones, bgg], writes=[bp0])
            gc, bgc = gc_r.next()
            S.op("act", lambda e: e.copy(out=gc, in_=p0[:, 0:8]), reads=[bp0], writes=[bgc])
            eg, beg = eg_r.next()
            S.op("dve", lambda e: e.tensor_tensor(out=eg[:, 4:8], in0=gc[:, 4:8], in1=gc[:, 0:4], op=ALU.subtract), reads=[bgc], writes=[beg])
            S.op("act", lambda e: e.activation(out=eg[:, 0:4], in_=gc[:, 0:4], func=AF.Exp), reads=[bgc], writes=[beg])
            S.op("act", lambda e: e.activation(out=eg[:, 4:8], in_=eg[:, 4:8], func=AF.Exp), reads=[beg], writes=[beg])
            S.op("act", lambda e: e.activation(out=eg[:, 8:12], in_=gc[:, 4:8], func=AF.Exp), reads=[bgc], writes=[beg])
            bge, bbge = bge_r.next()
            S.op("dve", lambda e: e.tensor_tensor(out=bge, in0=be_t, in1=eg[:, 0:4], op=ALU.mult), reads=[bbeta, beg], writes=[bbge])
            p1, bp1 = bank(1)
            for i, ch in enumerate((2, 3, 4, 5)):
                S.op("pe", lambda e: e.transpose(out=p1[:, i * 128:(i + 1) * 128], in_=qkvn[:, ch, ts], identity=k.identf[:]),
                     reads=[bq[ch], k.bidentf], writes=[bp1])
            ktok, bktok = ktok_r.next()
            vtok, bvtok = vtok_r.next()
            S.op("act", lambda e: e.copy(out=ktok, in_=p1[:, 0:256]), reads=[bp1], writes=[bktok])
            S.op("dve", lambda e: e.tensor_copy(out=vtok, in_=p1[:, 256:512]), reads=[bp1], writes=[bvtok])
            v3 = lambda a: a.rearrange("p (h d) -> p h d", h=4)
            bc = lambda a: a.rearrange("p (h o) -> p h o", o=1).broadcast_to([128, 4, 64])
            vb, bvb = vb_r.next()
            kbg, bkbg = kbg_r.next()
            kdec, bkdec = kdec_r.next()
            S.op("dve", lambda e: e.tensor_tensor(out=v3(vb), in0=v3(vtok), in1=bc(be_t), op=ALU.mult), reads=[bvtok, bbeta], writes=[bvb])
            S.op("dve", lambda e: e.tensor_tensor(out=v3(kbg), in0=v3(ktok), in1=bc(bge), op=ALU.mult), reads=[bktok, bbge], writes=[bkbg])
            S.op("dve", lambda e: e.tensor_tensor(out=v3(kdec), in0=v3(ktok), in1=bc(eg[:, 4:8]), op=ALU.mult), reads=[bktok, beg], writes=[bkdec])
            u, bu = u_r.next()
            kcT, bkcT = kcT_r.next()
            qgT, bqgT = qgT_r.next()
            intras = []
            p6, bp6 = bank(6)
            for h in range(4):
                rs = slice((h % 2) * 64, (h % 2) * 64 + 64)
                qT_h = qkvn[rs, h // 2, ts]
                kT_h = qkvn[rs, 2 + h // 2, ts]
                dg, bdg = dg_r.next()
                S.op("dve", lambda e: e.tensor_scalar(out=dg, in0=k.identf[:], scalar1=gc[:, h:h + 1], scalar2=None, op0=ALU.mult),
                     reads=[k.bidentf, bgc], writes=[bdg])
                p2, bp2 = bank(2)
                S.op("pe", lambda e: e.matmul(p2[:, 0:128], lhsT=onesf[:], rhs=dg, start=True, stop=True), reads=[bones, bdg], writes=[bp2])
                expg, bexpg = expg_r.next()
                d1, bd1 = d1_r.next()
                d2, bd2 = d2_r.next()
                S.op("act", lambda e: e.activation(out=expg, in_=p2[:, 0:128], func=AF.Exp), reads=[bp2], writes=[bexpg])
                S.op("dve", lambda e: e.tensor_scalar(out=d1, in0=p2[:, 0:128], scalar1=gc[:, h:h + 1], scalar2=0.0, op0=ALU.subtract, op1=ALU.min),
                     reads=[bp2, bgc], writes=[bd1])
                S.op("dve", lambda e: e.tensor_scalar(out=d2, in0=p2[:, 0:128], scalar1=gc[:, h:h + 1], scalar2=0.0, op0=ALU.subtract, op1=ALU.max),
                     reads=[bp2, bgc], writes=[bd2])
                S.op("act", lambda e: e.activation(out=d1, in_=d1, func=AF.Exp), reads=[bd1], writes=[bd1])
                S.op("act", lambda e: e.activation(out=d2, in_=d2, func=AF.Exp, scale=-1.0), reads=[bd2], writes=[bd2])
                S.op("dve", lambda e: e.tensor_tensor(out=d1, in0=d1, in1=MI, op=ALU.mult), reads=[bd1, btri], writes=[bd1])
                S.op("dve", lambda e: e.tensor_tensor(out=d2, in0=d2, in1=MSA, op=ALU.mult), reads=[bd2, btri], writes=[bd2])
                p3, bp3 = bank(3)
                S.op("pe", lambda e: e.matmul(p3[:, 0:128], lhsT=kT_h, rhs=kT_h, start=True, stop=True), reads=[bq[2 + h // 2]], writes=[bp3])
                S.op("pe", lambda e: e.matmul(p3[:, 128:256], lhsT=kT_h, rhs=qT_h, start=True, stop=True), reads=[bq[2 + h // 2], bq[h // 2]], writes=[bp3])
                xa, bxa = a_r.next()
                S.op("dve", lambda e: e.scalar_tensor_tensor(out=xa, in0=p3[:, 0:128], scalar=be_t[:, h:h + 1], in1=d2, op0=ALU.mult, op1=ALU.mult),
                     reads=[bp3, bbeta, bd2], writes=[bxa])
                intra, bintra = intra_r.next()
                S.op("dve", lambda e: e.tensor_tensor(out=intra, in0=p3[:, 128:256], in1=d1, op=ALU.mult), reads=[bp3, bd1], writes=[bintra])
                intras.append((intra, bintra))
                S.op("dve", lambda e: e.tensor_tensor(out=qgT[rs, (h // 2) * 128:(h // 2) * 128 + 128], in0=qT_h, in1=expg[rs, :], op=ALU.mult),
                     reads=[bq[h // 2], bexpg], writes=[bqgT])
                pn, bpn = nm_r.next()
                S.op("pe", lambda e: e.transpose(out=pn[:, 0:128], in_=xa, identity=k.identf[:]), reads=[bxa, k.bidentf], writes=[bpn])
                xb, bxb = at_r.next()
                S.op("act", lambda e: e.copy(out=xb, in_=pn[:, 0:128]), reads=[bpn], writes=[bxb])
                P, bP = p_r.next()
                S.op("dve", lambda e: e.tensor_tensor(out=P, in0=k.identf[:], in1=xb, op=ALU.subtract), reads=[k.bidentf, bxb], writes=[bP])
                for it in range(6):
                    pn, bpn = nm_r.next()
                    S.op("pe", lambda e: e.matmul(pn[:, 0:128], lhsT=xb, rhs=xa, start=True, stop=True), reads=[bxb, bxa], writes=[bpn])
                    xa2, bxa2 = xa_r.next()
                    S.op("act", lambda e: e.copy(out=xa2, in_=pn[:, 0:128]), reads=[bpn], writes=[bxa2])
                    if it < 5:
                        pn2, bpn2 = nm_r.next()
                        S.op("pe", lambda e: e.matmul(pn2[:, 0:128], lhsT=xa, rhs=xb, start=True, stop=True), reads=[bxb, bxa], writes=[bpn2])
                        xb2, bxb2 = xb_r.next()
                        S.op("dve", lambda e: e.tensor_copy(out=xb2, in_=pn2[:, 0:128]), reads=[bpn2], writes=[bxb2])
                    pn3, bpn3 = nm_r.next()
                    S.op("pe", lambda e: e.matmul(pn3[:, 0:128], lhsT=xa2, rhs=P, start=True, stop=True), reads=[bxa2, bP], writes=[bpn3])
                    P2, bP2 = p_r.next()
                    S.op("dve", lambda e: e.tensor_tensor(out=P2, in0=pn3[:, 0:128], in1=P, op=ALU.add), reads=[bpn3, bP], writes=[bP2])
                    P, bP = P2, bP2
                    xa, bxa = xa2, bxa2
                    if it < 5:
                        xb, bxb = xb2, bxb2
                S.op("pe", lambda e: e.matmul(p6[:, h * 64:(h + 1) * 64], lhsT=P, rhs=vb[:, h * 64:(h + 1) * 64], start=True, stop=True),
                     reads=[bP, bvb], writes=[bp6])
                S.op("pe", lambda e: e.matmul(p6[rs, 256 + (h // 2) * 128:256 + (h // 2) * 128 + 128], lhsT=kbg[:, h * 64:(h + 1) * 64], rhs=P,
                                              start=True, stop=True), reads=[bP, bkbg], writes=[bp6])
            S.op("act", lambda e: e.copy(out=u, in_=p6[:, 0:256]), reads=[bp6], writes=[bu])
            S.op("dve", lambda e: e.tensor_copy(out=kcT, in_=p6[:, 256:512]), reads=[bp6], writes=[bkcT])
            return dict(u=(u, bu), kcT=(kcT, bkcT), qgT=(qgT, bqgT), intras=intras, kdec=(kdec, bkdec), eg=(eg, beg))

        def step(t, dr, un):
            u, bu = un["u"]
            kcT, bkcT = un["kcT"]
            qgT, bqgT = un["qgT"]
            kdec, bkdec = un["kdec"]
            eg, beg = un["eg"]
            p7, bp7 = bank(7)
            for pr in range(2):
                ps_ = slice(pr * 128, (pr + 1) * 128)
                S.op("pe", lambda e: e.matmul(p7[:, ps_], lhsT=kcT[:, ps_], rhs=st[:, dr, pr, :], start=True, stop=True),
                     reads=[bkcT, bst[dr]], writes=[bp7])
            vnew, bvnew = vnew_r.next()
            S.op("dve", lambda e: e.tensor_tensor(out=vnew, in0=u, in1=p7[:, 0:256], op=ALU.subtract), reads=[bu, bp7], writes=[bvnew])
            for pr in range(2):
                ps_ = slice(256 + pr * 128, 256 + (pr + 1) * 128)
                S.op("pe", lambda e: e.matmul(p7[:, ps_], lhsT=qgT[:, pr * 128:(pr + 1) * 128], rhs=st[:, dr, pr, :], start=True, stop=False),
                     reads=[bqgT, bst[dr]], writes=[bp7])
                for h in (2 * pr, 2 * pr + 1):
                    intra, bintra = un["intras"][h]
                    S.op("pe", lambda e: e.matmul(p7[:, 256 + h * 64:256 + (h + 1) * 64], lhsT=intra, rhs=vnew[:, h * 64:(h + 1) * 64],
                                                  start=False, stop=(h == 2 * pr + 1)), reads=[bintra, bvnew], writes=[bp7])
            S.op("dve", lambda e: e.tensor_tensor(out=oall[:, t, :], in0=p7[:, 256:512], in1=oall[:, t, :], op=ALU.add), reads=[bp7, boall], writes=[boall])
            p0, bp0 = bank(0)
            for pr in range(2):
                ps_ = slice(pr * 128, (pr + 1) * 128)
                S.op("pe", lambda e: e.matmul(p0[:, 64 + pr * 128:64 + (pr + 1) * 128], lhsT=kdec[:, ps_], rhs=vnew[:, ps_], start=True, stop=True),
                     reads=[bkdec, bvnew], writes=[bp0])
            for h in range(4):
                pr = h // 2
                rs = slice((h % 2) * 64, (h % 2) * 64 + 64)
                cs = slice((h % 2) * 64, (h % 2) * 64 + 64)
                sv = st[rs, dr, pr, cs]
                S.op("dve", lambda e: e.scalar_tensor_tensor(out=sv, in0=sv, scalar=eg[rs, 8 + h:9 + h],
                                                             in1=p0[rs, 64 + pr * 128 + (h % 2) * 64:64 + pr * 128 + (h % 2) * 64 + 64],
                                                             op0=ALU.mult, op1=ALU.add), reads=[bst[dr], beg, bp0], writes=[bst[dr]])

        for i in range(NT):
            for dr in range(2):
                t = i if dr == 0 else NT - 1 - i
                un = unit(t, dr)
                step(t, dr, un)
        z_r = rot(256, 2)
        sq_r = rot(256, 2)
        r4_r = rot(4, 2)
        gnw = sp_[:, 768 + 64 * l:768 + 64 * (l + 1)]
        ob_r = Rot([(obt[:, i, :], Buf()) for i in range(2)])
        for t in range(NT):
            o3 = oall[:, t, :].rearrange("p (h d) -> p h d", h=4)
            z, bz = z_r.next()
            S.dma("sp", z, k.za_d[t * 128:(t + 1) * 128, 0:256], writes=[bz])
            S.op("act", lambda e: e.activation(out=z, in_=z, func=AF.Silu), reads=[bz], writes=[bz])
            sq, bsq = sq_r.next()
            S.op("dve", lambda e: e.tensor_tensor(out=sq, in0=oall[:, t, :], in1=oall[:, t, :], op=ALU.mult), reads=[boall], writes=[bsq])
            r4, br4 = r4_r.next()
            S.op("dve", lambda e: e.tensor_reduce(out=r4, in_=sq.rearrange("p (h d) -> p h d", h=4), axis=AX.X, op=ALU.add), reads=[bsq], writes=[br4])
            S.op("dve", lambda e: e.tensor_scalar(out=r4, in0=r4, scalar1=1.0 / 64, scalar2=EPS, op0=ALU.mult, op1=ALU.add), reads=[br4], writes=[br4])
            S.op("act", lambda e: e.activation(out=r4, in_=r4, func=AF.Sqrt), reads=[br4], writes=[br4])
            S.op("dve", lambda e: e.reciprocal(out=r4, in_=r4), reads=[br4], writes=[br4])
            s3 = sq.rearrange("p (h d) -> p h d", h=4)
            S.op("dve", lambda e: e.tensor_tensor(out=s3, in0=o3, in1=r4.rearrange("p (h o) -> p h o", o=1).broadcast_to([128, 4, 64]), op=ALU.mult),
                 reads=[boall, br4], writes=[bsq])
            S.op("dve", lambda e: e.tensor_tensor(out=s3, in0=s3, in1=gnw.rearrange("p (o d) -> p o d", o=1).broadcast_to([128, 4, 64]), op=ALU.mult),
                 reads=[bsq, k.bsmallp], writes=[bsq])
            ob, bob = ob_r.next()
            S.op("dve", lambda e: e.tensor_tensor(out=ob, in0=sq, in1=z, op=ALU.mult), reads=[bsq, bz], writes=[bob])
            S.dma("sp", k.mix_d[t * 128:(t + 1) * 128, 0:256], ob, reads=[bob])
        S.barrier()


def _col_index():
    def rot(a):
        return a.reshape(-1, 2, 32)[:, ::-1, :].reshape(-1)
    bqk = np.arange(1040, 1552)
    cqk = np.arange(1808, 2832)
    return np.concatenate([np.arange(0, 768), bqk, rot(bqk), cqk, rot(cqk),
                           np.arange(768, 1040), np.arange(1552, 1808), np.arange(2832, 3344)])


def _consts():
    p = np.arange(128)
    inv = (10000.0 ** (-np.arange(0, 64, 2, dtype=np.float32) / np.float32(64))).astype(np.float32)
    cst = np.zeros((128, 4), np.float32)
    cst[:, 0] = inv[p % 32]
    cst[:, 1] = np.where((p % 64) < 32, -1.0, 1.0)
    cst[:, 2] = math.pi / 2
    kk = np.arange(128)[:, None, None]
    dd = np.arange(17)[None, :, None] - 8
    qq = np.arange(128)[None, None, :]
    dist = np.abs(dd * 128 + kk - qq)
    m = (dist <= 64).astype(np.float32) + ((dist % 4 == 0) & (dist <= 256)) + ((dist % 16 == 0) & (dist <= 1024))
    return cst, m.astype(np.float32)


def make_in_maps(inputs):
    f = lambda a: np.ascontiguousarray(np.asarray(a))
    idx = _col_index()
    w_in_ext = f(np.asarray(inputs["w_in"])[:, :, idx])
    cst, mask = _consts()
    sp = np.zeros((1024,), np.float32)
    for l in range(L):
        sp[256 * l:256 * l + 64] = inputs["lambda_q1"][l]
        sp[256 * l + 64:256 * l + 128] = inputs["lambda_k1"][l]
        sp[256 * l + 128:256 * l + 192] = inputs["lambda_q2"][l]
        sp[256 * l + 192:256 * l + 256] = inputs["lambda_k2"][l]
        sp[512 + 128 * l:512 + 128 * (l + 1)] = inputs["subln_w"][l]
        sp[768 + 64 * l:768 + 64 * (l + 1)] = inputs["gdn_norm_w"][l]
        sp[896 + 8 * l:896 + 8 * (l + 1)] = np.asarray(inputs["a_log"][l]).reshape(-1)
        sp[912 + 8 * l:912 + 8 * (l + 1)] = np.asarray(inputs["dt_bias"][l]).reshape(-1)
    smallp = f(np.broadcast_to(sp[None, :], (128, 1024)))
    cwl = np.asarray(inputs["conv_w"]).astype(np.float32)
    convw = f(cwl.transpose(0, 2, 1).reshape(L, 6, 128, 5).transpose(0, 2, 1, 3).reshape(L, 128, 30))
    r_, c_ = np.arange(128)[:, None], np.arange(128)[None, :]
    tri = f(np.stack([(c_ >= r_), (c_ <= r_), (c_ > r_), (c_ < r_)], axis=1).astype(np.float32))
    shared = {
        "convw": convw, "tri": tri,
        "w_in": w_in_ext, "attn_nw": f(inputs["attn_norm_w"]), "w_out": f(inputs["w_out"]),
        "ffn_nw": f(inputs["ffn_norm_w"]), "w_gate": f(inputs["w_gate"]), "w_up": f(inputs["w_up"]),
        "w_down": f(inputs["w_down"]), "final_nw": f(inputs["final_norm_w"]), "smallp": smallp,
        "cst": cst, "maskb": mask,
    }
    x = np.asarray(inputs["x"])
    pos = np.asarray(inputs["positions"]).astype(np.int32)
    maps = []
    for b in range(8):
        m = dict(shared)
        m["x"] = f(x[b])
        m["pos"] = f(pos[b])
        maps.append(m)
    return maps


def kernel(**inputs):
    nc, _ = build()
    maps = make_in_maps(inputs)
    res = run_bass_kernel_spmd(nc, maps, core_ids=list(range(8)))
    return np.stack([r["out"] for r in res.results], axis=0).astype(np.float32)
```

```python
import math
import numpy as np
import concourse.bass as bass
import concourse.mybir as mybir
from concourse.bass_utils import run_bass_kernel_spmd

F32 = mybir.dt.float32
BF16 = mybir.dt.bfloat16
I32 = mybir.dt.int32
AF = mybir.ActivationFunctionType
ALU = mybir.AluOpType
AX = mybir.AxisListType

S_LEN = 4096
D = 1024
NT = 32
L = 2
DFF = 2816
NCOL = 4880
EPS = 1e-6
TWO_PI = 2.0 * math.pi
C1 = 6.28125
C2 = TWO_PI - C1


class Buf:
    __slots__ = ("name", "writer", "readers", "excl")

    def __init__(self, name="", excl=False):
        self.name = name
        self.writer = None
        self.readers = []
        self.excl = excl


class _Rec:
    def __getattr__(self, name):
        def f(*a, **kw):
            self.__dict__["call"] = (name, a, kw)
            return self
        return f


def _bind(fn):
    rec = _Rec()
    fn(rec)
    name, a, kw = rec.call
    return lambda e: getattr(e, name)(*a, **kw)


class Sched:
    CENG = ("pe", "act", "dve", "pool")
    DQ = ("sp", "act", "pool")

    def __init__(self, nc, n_dma_sems=8):
        self.nc = nc
        self.prog = {e: [] for e in ("pe", "act", "dve", "pool", "sp")}
        self.csem = {e: nc.alloc_semaphore("c_" + e) for e in self.CENG}
        self.cnt = {e: 0 for e in self.CENG}
        self.nd = n_dma_sems
        self.dsem = {q: [nc.alloc_semaphore("d_%s%d" % (q, i)) for i in range(n_dma_sems)]
                     for q in self.DQ}
        self.dcnt = {q: 0 for q in self.DQ}
        self.seen = {e: {} for e in self.prog}
        self.ninstr = 0

    def _sem(self, key):
        return self.csem[key[1]] if key[0] == "c" else self.dsem[key[1]][key[2]]

    def _need(self, eng, tok, waits):
        key, val, _ = tok
        if self.seen[eng].get(key, 0) >= val:
            return
        self.seen[eng][key] = val
        waits.append((self._sem(key), val))

    def _deps(self, eng, reads, writes, is_dma):
        waits = []
        for b in reads:
            t = b.writer
            if t is not None and not (eng == "pe" and t[2] == "pe"):
                self._need(eng, t, waits)
        for b in writes:
            t = b.writer
            if t is not None and (is_dma or t[2] != eng or eng != "pe"):
                self._need(eng, t, waits)
            for t in b.readers:
                if is_dma or t[2] != eng:
                    self._need(eng, t, waits)
        return waits

    def _commit(self, tok, reads, writes):
        for b in reads:
            b.readers.append(tok)
        for b in writes:
            b.writer = tok
            b.readers = []

    def op(self, eng, fn, reads=(), writes=()):
        ex = [b for b in reads if b.excl and b not in writes]
        if ex:
            writes = list(writes) + ex
        waits = self._deps(eng, reads, writes, False)
        self.cnt[eng] += 1
        n = self.cnt[eng]
        self.prog[eng].append((waits, _bind(fn), (self.csem[eng], 1)))
        self._commit((("c", eng), n, eng), reads, writes)
        self.ninstr += 1

    def dma(self, q, out, in_, reads=(), writes=(), **kw):
        waits = self._deps(q, reads, writes, True)
        i = self.dcnt[q]
        self.dcnt[q] += 1
        j = i % self.nd
        key = ("d", q, j)
        prev = 16 * (i // self.nd)
        if prev > 0:
            self._need(q, (key, prev, None), waits)
        tgt = prev + 16
        self.prog[q].append((waits, lambda e: e.dma_start(out=out, in_=in_, **kw), (self.dsem[q][j], 16)))
        self._commit((key, tgt, None), reads, writes)
        self.ninstr += 1

    def _all_tokens(self):
        toks = []
        for q in self.DQ:
            for j in range(self.nd):
                n = (self.dcnt[q] - j + self.nd - 1) // self.nd
                if n > 0:
                    toks.append((("d", q, j), 16 * n, None))
        for e in self.CENG:
            if self.cnt[e] > 0:
                toks.append((("c", e), self.cnt[e], e))
        return toks

    def barrier(self):
        toks = self._all_tokens()
        for eng in self.prog:
            waits = []
            for t in toks:
                if t[2] == eng:
                    continue
                self._need(eng, t, waits)
            if waits:
                self.prog[eng].append((waits, None, None))

    def finish(self):
        nc = self.nc
        final = [(self._sem(k), v) for k, v, _ in self._all_tokens()]
        prog = self.prog

        def replay(eng, lst, extra=()):
            for waits, fn, inc in lst:
                for s, v in waits:
                    eng.wait_ge(s, v)
                if fn is None:
                    continue
                ins = fn(eng)
                if inc is not None:
                    ins.then_inc(inc[0], inc[1])
            for s, v in extra:
                eng.wait_ge(s, v)

        with nc.Block() as block:
            @block.sync
            def _(e):
                replay(e, prog["sp"], final)

            @block.tensor
            def _(e):
                replay(e, prog["pe"])

            @block.scalar
            def _(e):
                replay(e, prog["act"])

            @block.vector
            def _(e):
                replay(e, prog["dve"])

            @block.gpsimd
            def _(e):
                replay(e, prog["pool"])


class Rot:
    def __init__(self, items):
        self.items = items
        self.i = 0

    def next(self):
        it = self.items[self.i % len(self.items)]
        self.i += 1
        return it


class K:
    pass


def sb(k, name, shape, dt, n=1):
    items = []
    for i in range(n):
        t = k.nc.alloc_sbuf_tensor("%s_%d_%d" % (name, k.uid, i), list(shape), dt)
        items.append((t, Buf(name)))
    k.uid += 1
    return items[0] if n == 1 else Rot(items)


def phase_setup(k):
    S, nc = k.S, k.nc
    S.op("pool", lambda e: e.memset(k.identf[:], 0.0), writes=[k.bidentf])
    S.op("pool", lambda e: e.affine_select(out=k.identf[:], in_=k.identf[:], pattern=[[-1, 128]],
                                           compare_op=ALU.not_equal, fill=1.0, base=0, channel_multiplier=1),
         reads=[k.bidentf], writes=[k.bidentf])
    S.op("dve", lambda e: e.tensor_copy(out=k.identb[:], in_=k.identf[:]), reads=[k.bidentf], writes=[k.bidentb])
    S.dma("sp", k.cst[:], k.cst_d, writes=[k.bcst])
    with nc.sbuf_tensor("su_pi", [128, S_LEN], I32) as pi, nc.sbuf_tensor("su_a", [128, S_LEN], F32) as ang, \
            nc.sbuf_tensor("su_k", [128, S_LEN], I32) as ki, nc.sbuf_tensor("su_kf", [128, S_LEN], F32) as kf, \
            nc.sbuf_tensor("su_r", [128, S_LEN], F32) as r, nc.sbuf_tensor("su_o", [128, S_LEN], F32) as o:
        bpi, bang, bki, bkf, br, bo = Buf(), Buf(), Buf(), Buf(), Buf(), Buf()
        S.dma("sp", pi[:], k.pos_d.rearrange("(o n) -> o n", o=1).broadcast_to([128, S_LEN]), writes=[bpi])
        S.op("dve", lambda e: e.tensor_copy(out=ang[:], in_=pi[:]), reads=[bpi], writes=[bang])
        S.op("dve", lambda e: e.tensor_scalar(out=ang[:], in0=ang[:], scalar1=k.cst[:, 0:1], scalar2=None, op0=ALU.mult),
             reads=[bang, k.bcst], writes=[bang])
        S.op("dve", lambda e: e.tensor_scalar(out=ki[:], in0=ang[:], scalar1=1.0 / TWO_PI, scalar2=None, op0=ALU.mult),
             reads=[bang], writes=[bki])
        S.op("dve", lambda e: e.tensor_copy(out=kf[:], in_=ki[:]), reads=[bki], writes=[bkf])
        S.op("dve", lambda e: e.scalar_tensor_tensor(out=r[:], in0=kf[:], scalar=-C1, in1=ang[:], op0=ALU.mult, op1=ALU.add),
             reads=[bkf, bang], writes=[br])
        S.op("dve", lambda e: e.scalar_tensor_tensor(out=r[:], in0=kf[:], scalar=-C2, in1=r[:], op0=ALU.mult, op1=ALU.add),
             reads=[bkf, br], writes=[br])
        S.op("dve", lambda e: e.tensor_scalar(out=r[:], in0=r[:], scalar1=-3.1415925, scalar2=3.1415925, op0=ALU.max, op1=ALU.min),
             reads=[br], writes=[br])
        S.op("act", lambda e: e.activation(out=o[:], in_=r[:], func=AF.Sin), reads=[br], writes=[bo])
        S.op("dve", lambda e: e.tensor_scalar(out=o[:], in0=o[:], scalar1=k.cst[:, 1:2], scalar2=None, op0=ALU.mult),
             reads=[bo, k.bcst], writes=[bo])
        S.dma("sp", k.sinT_d, o[:], reads=[bo])
        S.op("act", lambda e: e.activation(out=r[:], in_=r[:], func=AF.Abs), reads=[br], writes=[br])
        S.op("act", lambda e: e.activation(out=kf[:], in_=r[:], func=AF.Sin, scale=-1.0, bias=k.cst[:, 2:3]),
             reads=[br, k.bcst], writes=[bkf])
        S.dma("sp", k.cosT_d, kf[:], reads=[bkf])
        S.barrier()


def norm_tile(k, xt, bx, nwb, bnwb, h, bh, ss_r, junk_r):
    S = k.S
    ss, bss = ss_r.next()
    junk, bj = junk_r.next()
    S.op("act", lambda e: e.activation(out=junk[:], in_=xt[:], func=AF.Square, accum_out=ss[:]),
         reads=[bx], writes=[bj, bss])
    S.op("dve", lambda e: e.tensor_scalar(out=ss[:], in0=ss[:], scalar1=1.0 / D, scalar2=EPS, op0=ALU.mult, op1=ALU.add),
         reads=[bss], writes=[bss])
    S.op("act", lambda e: e.activation(out=ss[:], in_=ss[:], func=AF.Sqrt), reads=[bss], writes=[bss])
    S.op("dve", lambda e: e.reciprocal(out=ss[:], in_=ss[:]), reads=[bss], writes=[bss])
    S.op("dve", lambda e: e.scalar_tensor_tensor(out=h[:], in0=xt[:], scalar=ss[:, 0:1], in1=nwb[:], op0=ALU.mult, op1=ALU.mult),
         reads=[bx, bss, bnwb], writes=[bh])


def phase_a(k, l):
    S, nc = k.S, k.nc
    k.uid += 1
    src = k.x_d if l == 0 else k.xres_d
    with nc.sbuf_tensor("a_wb%d" % l, [128, 8, NCOL], BF16) as wb, \
            nc.sbuf_tensor("a_nwb%d" % l, [128, D], F32) as nwb, \
            nc.sbuf_tensor("a_arena%d" % l, [128, 12000], F32) as arena, nc.sbuf_tensor("a_arenab%d" % l, [128, 16000], BF16) as arenab:
        off = [0]

        offb = [0]

        def carve(n_f32, dt=F32):
            if dt == BF16:
                a = arenab[:, offb[0]:offb[0] + 2 * n_f32]
                offb[0] += 2 * n_f32
                return a
            a = arena[:, off[0]:off[0] + n_f32]
            off[0] += n_f32
            return a

        bwb = [Buf() for _ in range(8)]
        bnwb = Buf()
        for kk in range(8):
            S.dma("pool", wb[:, kk, :], k.w_in_d[l, kk * 128:(kk + 1) * 128, :], writes=[bwb[kk]])
        S.dma("sp", nwb[:], k.attn_nw_d[l:l + 1, :].broadcast_to([128, D]), writes=[bnwb])
        xt_r = Rot([(carve(1024), Buf()) for _ in range(2)])
        junk_r = Rot([(carve(1024), Buf()) for _ in range(1)])
        h_r = Rot([(carve(512, BF16), Buf()) for _ in range(2)])
        ss_r = Rot([(carve(1), Buf()) for _ in range(2)])
        hT_r = Rot([(carve(2048, BF16), Buf()) for _ in range(2)])
        cos_r = Rot([(carve(512), Buf()) for _ in range(2)])
        sin_r = Rot([(carve(512), Buf()) for _ in range(2)])
        ofm_r = Rot([(carve(512), Buf()) for _ in range(3)])
        t1_r = Rot([(carve(512), Buf()) for _ in range(2)])
        t2_r = Rot([(carve(512), Buf()) for _ in range(2)])
        oqk_r = Rot([(carve(256, BF16), Buf()) for _ in range(3)])
        oza_r = Rot([(carve(272), Buf()) for _ in range(2)])
        ovb_r = Rot([(carve(130, BF16), Buf()) for _ in range(2)])
        ovc_r = Rot([(carve(258, BF16), Buf()) for _ in range(2)])
        for (t, b) in ovb_r.items:
            S.op("dve", lambda e, t=t: e.memset(t, 1.0), writes=[b])
        for (t, b) in ovc_r.items:
            S.op("dve", lambda e, t=t: e.memset(t, 1.0), writes=[b])
        ptr_r = Rot([(k.pb[0], k.bpb[0]), (k.pb[1], k.bpb[1])])
        pfm_r = Rot([(k.pb[i], k.bpb[i]) for i in (2, 3, 4, 5)])
        ptm_r = Rot([(k.pb[i], k.bpb[i]) for i in (6, 7)])

        for g in range(8):
            hT, bhT = hT_r.next()
            hT3 = hT.rearrange("p (k n) -> p k n", k=8)
            cg, bcg = cos_r.next()
            sg, bsg = sin_r.next()
            S.dma("sp", cg, k.cosT_d[:, g * 512:(g + 1) * 512], writes=[bcg])
            S.dma("sp", sg, k.sinT_d[:, g * 512:(g + 1) * 512], writes=[bsg])
            for s in range(4):
                t0 = g * 512 + s * 128
                xt, bx = xt_r.next()
                h, bh = h_r.next()
                S.dma("sp", xt, src[t0:t0 + 128, :], writes=[bx])
                norm_tile(k, xt, bx, nwb, bnwb, h, bh, ss_r, junk_r)
                pt, bpt = ptr_r.next()
                ptb = pt[:].bitcast(BF16).rearrange("p (k n) -> p k n", k=8)
                for kk in range(8):
                    S.op("pe", lambda e, kk=kk, ptb=ptb, h=h: e.transpose(out=ptb[:, kk, :], in_=h[:, kk * 128:(kk + 1) * 128],
                                                                          identity=k.identb[:]),
                         reads=[bh, k.bidentb], writes=[bpt])
                S.op("dve", lambda e, ptb=ptb, hT3=hT3, s=s: e.tensor_copy(out=hT3[:, :, s * 128:(s + 1) * 128], in_=ptb),
                     reads=[bpt], writes=[bhT])
            tsl = slice(g * 512, (g + 1) * 512)

            def fm_mm(ch):
                pf, bpf = pfm_r.next()
                for kk in range(8):
                    S.op("pe", lambda e, kk=kk, pf=pf, ch=ch: e.matmul(pf[:], lhsT=wb[:, kk, ch * 128:(ch + 1) * 128],
                                                                      rhs=hT3[:, kk, :], start=(kk == 0), stop=(kk == 7)),
                         reads=[bwb[kk], bhT], writes=[bpf])
                return pf, bpf

            for ch in range(6):
                pf, bpf = fm_mm(ch)
                o, bo = ofm_r.next()
                S.op("act", lambda e, o=o, pf=pf: e.copy(out=o, in_=pf[:]), reads=[bpf], writes=[bo])
                S.dma("sp", k.qkvT_a_d[ch * 128:(ch + 1) * 128, tsl], o, reads=[bo])
            for (c0, nch, dst) in ((6, 4, k.qkT_b_d), (14, 8, k.qkT_c_d)):
                for j in range(nch):
                    pf, bpf = fm_mm(c0 + j)
                    pr, bpr = fm_mm(c0 + nch + j)
                    t1, bt1 = t1_r.next()
                    t2, bt2 = t2_r.next()
                    o, bo = oqk_r.next()
                    S.op("dve", lambda e, t1=t1, pf=pf: e.tensor_tensor(out=t1, in0=pf[:], in1=cg, op=ALU.mult),
                         reads=[bpf, bcg], writes=[bt1])
                    S.op("dve", lambda e, t2=t2, pr=pr: e.tensor_tensor(out=t2, in0=pr[:], in1=sg, op=ALU.mult),
                         reads=[bpr, bsg], writes=[bt2])
                    S.op("dve", lambda e, t1=t1, t2=t2, o=o: e.tensor_tensor(out=o, in0=t1, in1=t2, op=ALU.add),
                         reads=[bt1, bt2], writes=[bo])
                    S.dma("sp", dst[j * 128:(j + 1) * 128, tsl], o, reads=[bo])
            for s in range(4):
                t0 = g * 512 + s * 128

                def tm_mm(c0, n):
                    pm, bpm = ptm_r.next()
                    for kk in range(8):
                        S.op("pe", lambda e, kk=kk, pm=pm: e.matmul(pm[:, 0:n], lhsT=hT3[:, kk, s * 128:(s + 1) * 128],
                                                                   rhs=wb[:, kk, c0:c0 + n], start=(kk == 0), stop=(kk == 7)),
                             reads=[bwb[kk], bhT], writes=[bpm])
                    return pm, bpm

                pm, bpm = tm_mm(3840, 272)
                o, bo = oza_r.next()
                S.op("act", lambda e, o=o, pm=pm: e.copy(out=o, in_=pm[:, 0:272]), reads=[bpm], writes=[bo])
                S.dma("sp", k.za_d[t0:t0 + 128, :], o, reads=[bo])
                pm, bpm = tm_mm(4112, 256)
                o, bo = ovb_r.next()
                o3 = o.rearrange("p (h d) -> p h d", h=4)
                S.op("act", lambda e, o3=o3, pm=pm: e.copy(out=o3[:, :, 0:64], in_=pm[:, 0:256].rearrange("p (h d) -> p h d", h=4)),
                     reads=[bpm], writes=[bo])
                S.dma("sp", k.v_b_d[t0:t0 + 128, :], o, reads=[bo])
                pm, bpm = tm_mm(4368, 512)
                o, bo = ovc_r.next()
                o3 = o.rearrange("p (h d) -> p h d", h=4)
                S.op("act", lambda e, o3=o3, pm=pm: e.copy(out=o3[:, :, 0:128], in_=pm[:, 0:512].rearrange("p (h d) -> p h d", h=4)),
                     reads=[bpm], writes=[bo])
                S.dma("sp", k.v_c_d[t0:t0 + 128, :], o, reads=[bo])
        S.barrier()


def phase_c(k, l):
    S, nc = k.S, k.nc
    k.uid += 1
    lam_init = 0.8 - 0.6 * math.exp(-0.3 * l)
    with nc.sbuf_tensor("c_kT%d" % l, [128, S_LEN], BF16) as kT, nc.sbuf_tensor("c_qT%d" % l, [128, S_LEN], BF16) as qT, \
            nc.sbuf_tensor("c_v%d" % l, [128, NT, 129], BF16) as va, \
            nc.sbuf_tensor("c_arena%d" % l, [128, 3000], F32) as arena, nc.sbuf_tensor("c_arenab%d" % l, [128, 4000], BF16) as arenab:
        off = [0]

        offb = [0]

        def carve(n_f32, dt=F32):
            if dt == BF16:
                a = arenab[:, offb[0]:offb[0] + 2 * n_f32]
                offb[0] += 2 * n_f32
                return a
            a = arena[:, off[0]:off[0] + n_f32]
            off[0] += n_f32
            return a

        bkT, bqT, bva = Buf(), Buf(), Buf()
        p_r = Rot([(carve(256, BF16), Buf()) for _ in range(4)])
        o1_r = Rot([(carve(512), Buf()) for _ in range(1)])
        rc_r = Rot([(carve(4), Buf()) for _ in range(2)])
        o_r = Rot([(carve(128), Buf()) for _ in range(2)])
        junk_r = Rot([(carve(128), Buf()) for _ in range(1)])
        ss_r = Rot([(carve(1), Buf()) for _ in range(2)])
        ob_r = Rot([(carve(64, BF16), Buf()) for _ in range(2)])
        lam, blam = carve(4), Buf()
        ps_r = Rot([(k.pb[i], k.bpb[i]) for i in (0, 1, 6, 7)])
        acc = [(k.pb[i], k.bpb[i]) for i in (2, 3, 4, 5)]
        sp_ = k.smallp
        c0 = 256 * l
        jk, bjk = junk_r.next()
        S.op("dve", lambda e: e.tensor_tensor(out=jk[:, 0:64], in0=sp_[:, c0:c0 + 64], in1=sp_[:, c0 + 64:c0 + 128], op=ALU.mult),
             reads=[k.bsmallp], writes=[bjk])
        S.op("dve", lambda e: e.tensor_reduce(out=lam[:, 0:1], in_=jk[:, 0:64], axis=AX.X, op=ALU.add), reads=[bjk], writes=[blam])
        S.op("dve", lambda e: e.tensor_tensor(out=jk[:, 0:64], in0=sp_[:, c0 + 128:c0 + 192], in1=sp_[:, c0 + 192:c0 + 256], op=ALU.mult),
             reads=[k.bsmallp, blam], writes=[bjk])
        S.op("dve", lambda e: e.tensor_reduce(out=lam[:, 1:2], in_=jk[:, 0:64], axis=AX.X, op=ALU.add), reads=[bjk, blam], writes=[blam])
        S.op("act", lambda e: e.activation(out=lam[:, 0:2], in_=lam[:, 0:2], func=AF.Exp), reads=[blam], writes=[blam])
        S.op("dve", lambda e: e.tensor_tensor(out=lam[:, 2:3], in0=lam[:, 1:2], in1=lam[:, 0:1], op=ALU.subtract), reads=[blam], writes=[blam])
        S.op("dve", lambda e: e.tensor_scalar(out=lam[:, 2:3], in0=lam[:, 2:3], scalar1=-lam_init, scalar2=None, op0=ALU.add),
             reads=[blam], writes=[blam])
        subw = sp_[:, 512 + 128 * l:512 + 128 * (l + 1)]
        for hh in range(4):
            S.dma("sp", qT[:], k.qkT_c_d[hh * 128:(hh + 1) * 128, :], writes=[bqT])
            S.dma("sp", kT[:], k.qkT_c_d[512 + hh * 128:512 + (hh + 1) * 128, :], writes=[bkT])
            S.dma("sp", va[:], k.v_c_d.rearrange("(t p) (h d) -> p t h d", p=128, h=4)[:, :, hh, :], writes=[bva])
            for qg in range(8):
                o1, bo1 = o1_r.next()
                for c in range(2):
                    rs = slice(c * 64, (c + 1) * 64)
                    pend = []
                    for kb in range(NT + 2):
                        if kb < NT:
                            ps, bps = ps_r.next()
                            S.op("pe", lambda e, ps=ps, kb=kb, rs=rs: e.matmul(ps[:], lhsT=kT[rs, kb * 128:(kb + 1) * 128],
                                                                              rhs=qT[rs, qg * 512:(qg + 1) * 512], start=True, stop=True),
                                 reads=[bkT, bqT], writes=[bps])
                            p, bp = p_r.next()
                            S.op("act", lambda e, p=p, ps=ps: e.activation(out=p, in_=ps[:], func=AF.Exp, scale=0.125),
                                 reads=[bps], writes=[bp])
                            pend.append((p, bp, kb))
                        if kb >= 2:
                            p, bp, kq = pend.pop(0)
                            for s in range(4):
                                a, ba = acc[s]
                                S.op("pe", lambda e, a=a, p=p, s=s, kq=kq: e.matmul(a[:, 0:129], lhsT=p[:, s * 128:(s + 1) * 128],
                                                                                   rhs=va[:, kq, :], start=(kq == 0), stop=(kq == NT - 1)),
                                     reads=[bp, bva], writes=[ba])
                    for s in range(4):
                        a, ba = acc[s]
                        rc, brc = rc_r.next()
                        S.op("dve", lambda e, rc=rc, a=a: e.reciprocal(out=rc[:, 0:1], in_=a[:, 128:129]), reads=[ba], writes=[brc])
                        if c == 0:
                            S.op("dve", lambda e, a=a, rc=rc, s=s: e.tensor_scalar(out=o1[:, s * 128:(s + 1) * 128], in0=a[:, 0:128],
                                                                                  scalar1=rc[:, 0:1], scalar2=None, op0=ALU.mult),
                                 reads=[ba, brc], writes=[bo1])
                        else:
                            S.op("dve", lambda e, rc=rc: e.tensor_tensor(out=rc[:, 1:2], in0=rc[:, 0:1], in1=lam[:, 2:3], op=ALU.mult),
                                 reads=[brc, blam], writes=[brc])
                            o, bo = o_r.next()
                            S.op("dve", lambda e, a=a, rc=rc, s=s, o=o: e.scalar_tensor_tensor(
                                out=o, in0=a[:, 0:128], scalar=rc[:, 1:2], in1=o1[:, s * 128:(s + 1) * 128], op0=ALU.mult, op1=ALU.add),
                                 reads=[ba, brc, bo1], writes=[bo])
                            ss, bss = ss_r.next()
                            jk, bjk = junk_r.next()
                            S.op("act", lambda e, jk=jk, o=o, ss=ss: e.activation(out=jk, in_=o, func=AF.Square, accum_out=ss),
                                 reads=[bo], writes=[bjk, bss])
                            S.op("dve", lambda e, ss=ss: e.tensor_scalar(out=ss, in0=ss, scalar1=1.0 / 128, scalar2=EPS, op0=ALU.mult, op1=ALU.add),
                                 reads=[bss], writes=[bss])
                            S.op("act", lambda e, ss=ss: e.activation(out=ss, in_=ss, func=AF.Sqrt), reads=[bss], writes=[bss])
                            S.op("dve", lambda e, ss=ss: e.reciprocal(out=ss, in_=ss), reads=[bss], writes=[bss])
                            S.op("dve", lambda e, ss=ss: e.tensor_scalar(out=ss, in0=ss, scalar1=1.0 - lam_init, scalar2=None, op0=ALU.mult),
                                 reads=[bss], writes=[bss])
                            ob, bob = ob_r.next()
                            S.op("dve", lambda e, ob=ob, o=o, ss=ss: e.scalar_tensor_tensor(out=ob, in0=o, scalar=ss[:, 0:1], in1=subw,
                                                                                            op0=ALU.mult, op1=ALU.mult),
                                 reads=[bo, bss, k.bsmallp], writes=[bob])
                            t0 = qg * 512 + s * 128
                            S.dma("sp", k.mix_d[t0:t0 + 128, 512 + hh * 128:512 + (hh + 1) * 128], ob, reads=[bob])
        S.barrier()


def phase_b(k, l):
    S, nc = k.S, k.nc
    k.uid += 1
    with nc.sbuf_tensor("b_kT%d" % l, [128, S_LEN], BF16) as kT, nc.sbuf_tensor("b_qT%d" % l, [128, S_LEN], BF16) as qT, \
            nc.sbuf_tensor("b_v%d" % l, [128, NT, 65], BF16) as va, \
            nc.sbuf_tensor("b_mask%d" % l, [128, 17, 128], F32) as mask, \
            nc.sbuf_tensor("b_arena%d" % l, [128, 2000], F32) as arena, nc.sbuf_tensor("b_arenab%d" % l, [128, 4000], BF16) as arenab:
        off = [0]

        offb = [0]

        def carve(n_f32, dt=F32):
            if dt == BF16:
                a = arenab[:, offb[0]:offb[0] + 2 * n_f32]
                offb[0] += 2 * n_f32
                return a
            a = arena[:, off[0]:off[0] + n_f32]
            off[0] += n_f32
            return a

        bkT, bqT, bva, bmask = Buf(), Buf(), Buf(), Buf()
        S.dma("sp", mask[:], k.mask_d, writes=[bmask])
        pe_r = Rot([(carve(512), Buf()) for _ in range(3)])
        p_r = Rot([(carve(256, BF16), Buf()) for _ in range(4)])
        rc_r = Rot([(carve(1), Buf()) for _ in range(2)])
        ob_r = Rot([(carve(32, BF16), Buf()) for _ in range(2)])
        ps_r = Rot([(k.pb[i], k.bpb[i]) for i in (0, 1, 2, 5)])
        acc_r = Rot([(k.pb[i], k.bpb[i]) for i in (3, 4, 6)])
        for hp in range(2):
            S.dma("sp", qT[:], k.qkT_b_d[hp * 128:(hp + 1) * 128, :], writes=[bqT])
            S.dma("sp", kT[:], k.qkT_b_d[256 + hp * 128:256 + (hp + 1) * 128, :], writes=[bkT])
            for hl in range(2):
                hh = hp * 2 + hl
                rs = slice(hl * 64, (hl + 1) * 64)
                S.dma("sp", va[:], k.v_b_d.rearrange("(t p) (h d) -> p t h d", p=128, h=4)[:, :, hh, :], writes=[bva])
                items = []
                for qb in range(NT):
                    lo, hi = max(0, qb - 8), min(NT - 1, qb + 8)
                    kbs = list(range(lo, hi + 1))
                    for b0 in range(0, len(kbs), 4):
                        items.append((qb, lo, hi, kbs[b0:b0 + 4]))
                pend = []
                cur_acc = {}
                for it in range(len(items) + 2):
                    if it < len(items):
                        qb, lo, hi, grp = items[it]
                        n = len(grp)
                        ps, bps = ps_r.next()
                        for i, kb in enumerate(grp):
                            S.op("pe", lambda e: e.matmul(ps[:, i * 128:(i + 1) * 128], lhsT=kT[rs, kb * 128:(kb + 1) * 128],
                                                          rhs=qT[rs, qb * 128:(qb + 1) * 128], start=True, stop=True),
                                 reads=[bkT, bqT], writes=[bps])
                        pe_, bpe = pe_r.next()
                        S.op("act", lambda e: e.activation(out=pe_[:, 0:n * 128], in_=ps[:, 0:n * 128], func=AF.Exp, scale=0.125),
                             reads=[bps], writes=[bpe])
                        p, bp = p_r.next()
                        d0 = grp[0] - qb + 8
                        S.op("dve", lambda e: e.tensor_tensor(out=p[:, 0:n * 128], in0=pe_[:, 0:n * 128],
                                                              in1=mask[:, d0:d0 + n, :].rearrange("p a b -> p (a b)"), op=ALU.mult),
                             reads=[bpe, bmask], writes=[bp])
                        pend.append((items[it], p, bp))
                    if it >= 2:
                        (qb, lo, hi, grp), p, bp = pend.pop(0)
                        if grp[0] == lo:
                            cur_acc[qb] = acc_r.next()
                        a, ba = cur_acc[qb]
                        for i, kb in enumerate(grp):
                            S.op("pe", lambda e: e.matmul(a[:, 0:65], lhsT=p[:, i * 128:(i + 1) * 128], rhs=va[:, kb, :],
                                                          start=(kb == lo), stop=(kb == hi)),
                                 reads=[bp, bva], writes=[ba])
                        if grp[-1] == hi:
                            rc, brc = rc_r.next()
                            S.op("dve", lambda e: e.reciprocal(out=rc, in_=a[:, 64:65]), reads=[ba], writes=[brc])
                            ob, bob = ob_r.next()
                            S.op("dve", lambda e: e.tensor_scalar(out=ob, in0=a[:, 0:64], scalar1=rc[:, 0:1], scalar2=None, op0=ALU.mult),
                                 reads=[ba, brc], writes=[bob])
                            S.dma("sp", k.mix_d[qb * 128:(qb + 1) * 128, 256 + hh * 64:256 + (hh + 1) * 64], ob, reads=[bob])
        S.barrier()


def phase_d1(k, l):
    S, nc = k.S, k.nc
    k.uid += 1
    src = k.x_d if l == 0 else k.xres_d
    with nc.sbuf_tensor("d1_w%d" % l, [128, 8, D], BF16) as wo, nc.sbuf_tensor("d1_arena%d" % l, [128, 5000], F32) as arena, nc.sbuf_tensor("d1_arenab%d" % l, [128, 5000], BF16) as arenab:
        off = [0]

        offb = [0]

        def carve(n_f32, dt=F32):
            if dt == BF16:
                a = arenab[:, offb[0]:offb[0] + 2 * n_f32]
                offb[0] += 2 * n_f32
                return a
            a = arena[:, off[0]:off[0] + n_f32]
            off[0] += n_f32
            return a

        bwo = Buf()
        S.dma("pool", wo[:], k.w_out_d[l].rearrange("(k p) n -> p k n", p=128), writes=[bwo])
        xt_r = Rot([(carve(1024), Buf()) for _ in range(2)])
        m_r = Rot([(carve(512, BF16), Buf()) for _ in range(2)])
        mT_r = Rot([(carve(512, BF16), Buf()) for _ in range(2)])
        xo_r = Rot([(carve(1024), Buf()) for _ in range(2)])
        ptr_r = Rot([(k.pb[0], k.bpb[0]), (k.pb[1], k.bpb[1])])
        po_r = Rot([(k.pb[i], k.bpb[i]) for i in (2, 3, 4, 5)])
        for t in range(NT):
            t0 = t * 128
            xt, bx = xt_r.next()
            m, bm = m_r.next()
            S.dma("sp", xt, src[t0:t0 + 128, :], writes=[bx])
            S.dma("sp", m, k.mix_d[t0:t0 + 128, :], writes=[bm])
            pt, bpt = ptr_r.next()
            ptb = pt[:].bitcast(BF16).rearrange("p (k n) -> p k n", k=8)
            for kk in range(8):
                S.op("pe", lambda e, kk=kk, ptb=ptb, m=m: e.transpose(out=ptb[:, kk, :], in_=m[:, kk * 128:(kk + 1) * 128], identity=k.identb[:]),
                     reads=[bm, k.bidentb], writes=[bpt])
            mT, bmT = mT_r.next()
            mT3 = mT.rearrange("p (k n) -> p k n", k=8)
            S.op("act", lambda e, mT3=mT3, ptb=ptb: e.copy(out=mT3, in_=ptb), reads=[bpt], writes=[bmT])
            xo, bxo = xo_r.next()
            for half in range(2):
                po, bpo = po_r.next()
                for kk in range(8):
                    S.op("pe", lambda e, kk=kk, po=po, mT3=mT3, half=half: e.matmul(po[:], lhsT=mT3[:, kk, :], rhs=wo[:, kk, half * 512:(half + 1) * 512],
                                                                                   start=(kk == 0), stop=(kk == 7)),
                         reads=[bmT, bwo], writes=[bpo])
                S.op("dve", lambda e, xo=xo, po=po, xt=xt, half=half: e.tensor_tensor(out=xo[:, half * 512:(half + 1) * 512], in0=po[:],
                                                                                     in1=xt[:, half * 512:(half + 1) * 512], op=ALU.add),
                     reads=[bpo, bx], writes=[bxo])
            S.dma("sp", k.xres_d[t0:t0 + 128, :], xo, reads=[bxo])
        S.barrier()


def phase_d2(k, l):
    S, nc = k.S, k.nc
    k.uid += 1
    last = (l == L - 1)
    NFF = DFF // 128
    with nc.sbuf_tensor("d2_wg%d" % l, [128, 8, DFF], BF16) as wg, nc.sbuf_tensor("d2_wu%d" % l, [128, 8, DFF], BF16) as wu, \
            nc.sbuf_tensor("d2_wd%d" % l, [128, NFF, D], BF16) as wd, nc.sbuf_tensor("d2_nwb%d" % l, [128, D], F32) as nwb, \
            nc.sbuf_tensor("d2_fnw%d" % l, [128, D], F32) as fnw, \
            nc.sbuf_tensor("d2_arena%d" % l, [128, 7700], F32) as arena, nc.sbuf_tensor("d2_arenab%d" % l, [128, 11800], BF16) as arenab:
        off = [0]

        offb = [0]

        def carve(n_f32, dt=F32):
            if dt == BF16:
                a = arenab[:, offb[0]:offb[0] + 2 * n_f32]
                offb[0] += 2 * n_f32
                return a
            a = arena[:, off[0]:off[0] + n_f32]
            off[0] += n_f32
            return a

        bwg = [Buf() for _ in range(8)]
        bwu = [Buf() for _ in range(8)]
        bwd, bnwb, bfnw = Buf(), Buf(), Buf()
        for kk in range(8):
            S.dma("pool", wg[:, kk, :], k.w_gate_d[l, kk * 128:(kk + 1) * 128, :], writes=[bwg[kk]])
            S.dma("pool", wu[:, kk, :], k.w_up_d[l, kk * 128:(kk + 1) * 128, :], writes=[bwu[kk]])
        S.dma("pool", wd[:], k.w_down_d[l].rearrange("(k p) n -> p k n", p=128), writes=[bwd])
        S.dma("sp", nwb[:], k.ffn_nw_d[l:l + 1, :].broadcast_to([128, D]), writes=[bnwb])
        if last:
            S.dma("sp", fnw[:], k.final_nw_d.rearrange("(o n) -> o n", o=1).broadcast_to([128, D]), writes=[bfnw])
        TG = 256
        xt_r = Rot([(carve(1024), Buf()) for _ in range(3)])
        junk_r = Rot([(carve(1024), Buf()) for _ in range(1)])
        h_r = Rot([(carve(512, BF16), Buf()) for _ in range(2)])
        ss_r = Rot([(carve(1), Buf()) for _ in range(2)])
        hT_r = Rot([(carve(1024, BF16), Buf()) for _ in range(2)])
        aT_r = Rot([(carve(NFF * TG // 2, BF16), Buf()) for _ in range(1)])
        sg_r = Rot([(carve(TG), Buf()) for _ in range(2)])
        xo_r = Rot([(carve(1024), Buf()) for _ in range(2)])
        yo_r = Rot([(carve(1024), Buf()) for _ in range(1)])
        ptr_r = Rot([(k.pb[0], k.bpb[0]), (k.pb[1], k.bpb[1])])
        pg_r = Rot([(k.pb[i], k.bpb[i]) for i in (2, 3)])
        pu_r = Rot([(k.pb[i], k.bpb[i]) for i in (4, 5)])
        po_r = Rot([(k.pb[i], k.bpb[i]) for i in (6, 7)])
        for g in range(S_LEN // TG):
            hT, bhT = hT_r.next()
            hT3 = hT.rearrange("p (k n) -> p k n", k=8)
            xts = []
            for s in range(TG // 128):
                t0 = g * TG + s * 128
                xt, bx = xt_r.next()
                xts.append((xt, bx))
                h, bh = h_r.next()
                S.dma("sp", xt, k.xres_d[t0:t0 + 128, :], writes=[bx])
                norm_tile(k, xt, bx, nwb, bnwb, h, bh, ss_r, junk_r)
                pt, bpt = ptr_r.next()
                ptb = pt[:].bitcast(BF16).rearrange("p (k n) -> p k n", k=8)
                for kk in range(8):
                    S.op("pe", lambda e, kk=kk, ptb=ptb, h=h: e.transpose(out=ptb[:, kk, :], in_=h[:, kk * 128:(kk + 1) * 128], identity=k.identb[:]),
                         reads=[bh, k.bidentb], writes=[bpt])
                S.op("dve", lambda e, ptb=ptb, hT3=hT3, s=s: e.tensor_copy(out=hT3[:, :, s * 128:(s + 1) * 128], in_=ptb),
                     reads=[bpt], writes=[bhT])
            aT, baT = aT_r.next()
            aT3 = aT.rearrange("p (f n) -> p f n", f=NFF)
            for f in range(NFF):
                pg, bpg = pg_r.next()
                pu, bpu = pu_r.next()
                for kk in range(8):
                    S.op("pe", lambda e, kk=kk, pg=pg, f=f: e.matmul(pg[:, 0:TG], lhsT=wg[:, kk, f * 128:(f + 1) * 128], rhs=hT3[:, kk, :],
                                                                    start=(kk == 0), stop=(kk == 7)),
                         reads=[bwg[kk], bhT], writes=[bpg])
                for kk in range(8):
                    S.op("pe", lambda e, kk=kk, pu=pu, f=f: e.matmul(pu[:, 0:TG], lhsT=wu[:, kk, f * 128:(f + 1) * 128], rhs=hT3[:, kk, :],
                                                                    start=(kk == 0), stop=(kk == 7)),
                         reads=[bwu[kk], bhT], writes=[bpu])
                sg, bsg = sg_r.next()
                S.op("act", lambda e, sg=sg, pg=pg: e.activation(out=sg, in_=pg[:, 0:TG], func=AF.Silu), reads=[bpg], writes=[bsg])
                S.op("dve", lambda e, sg=sg, pu=pu, f=f, aT3=aT3: e.tensor_tensor(out=aT3[:, f, :], in0=pu[:, 0:TG], in1=sg, op=ALU.mult),
                     reads=[bpu, bsg], writes=[baT])
            for s in range(TG // 128):
                t0 = g * TG + s * 128
                xt, bx = xts[s]
                xo, bxo = xo_r.next()
                for half in range(2):
                    po, bpo = po_r.next()
                    for f in range(NFF):
                        S.op("pe", lambda e, f=f, po=po, s=s, half=half, aT3=aT3: e.matmul(po[:], lhsT=aT3[:, f, s * 128:(s + 1) * 128],
                                                                                         rhs=wd[:, f, half * 512:(half + 1) * 512],
                                                                                         start=(f == 0), stop=(f == NFF - 1)),
                             reads=[baT, bwd], writes=[bpo])
                    S.op("dve", lambda e, xo=xo, po=po, xt=xt, half=half: e.tensor_tensor(out=xo[:, half * 512:(half + 1) * 512], in0=po[:],
                                                                                         in1=xt[:, half * 512:(half + 1) * 512], op=ALU.add),
                         reads=[bpo, bx], writes=[bxo])
                if not last:
                    S.dma("sp", k.xres_d[t0:t0 + 128, :], xo, reads=[bxo])
                else:
                    yo, byo = yo_r.next()
                    norm_tile(k, xo, bxo, fnw, bfnw, yo, byo, ss_r, junk_r)
                    S.dma("sp", k.out_d[t0:t0 + 128, :], yo, reads=[byo])
        S.barrier()


def build(dbg=False, phases=None):
    nc = bass.Bass("TRN2", target_bir_lowering=False)
    k = K()
    k.nc = nc
    k.uid = 0
    k.S = Sched(nc)
    ext = "ExternalInput"
    k.x_d = nc.dram_tensor("x", [S_LEN, D], F32, kind=ext).ap()
    k.pos_d = nc.dram_tensor("pos", [S_LEN], I32, kind=ext).ap()
    k.w_in_d = nc.dram_tensor("w_in", [L, D, NCOL], F32, kind=ext).ap()
    k.attn_nw_d = nc.dram_tensor("attn_nw", [L, D], F32, kind=ext).ap()
    k.w_out_d = nc.dram_tensor("w_out", [L, D, D], F32, kind=ext).ap()
    k.ffn_nw_d = nc.dram_tensor("ffn_nw", [L, D], F32, kind=ext).ap()
    k.w_gate_d = nc.dram_tensor("w_gate", [L, D, DFF], F32, kind=ext).ap()
    k.w_up_d = nc.dram_tensor("w_up", [L, D, DFF], F32, kind=ext).ap()
    k.w_down_d = nc.dram_tensor("w_down", [L, DFF, D], F32, kind=ext).ap()
    k.final_nw_d = nc.dram_tensor("final_nw", [D], F32, kind=ext).ap()
    k.smallp_d = nc.dram_tensor("smallp", [128, 1024], F32, kind=ext).ap()
    k.cst_d = nc.dram_tensor("cst", [128, 4], F32, kind=ext).ap()
    k.mask_d = nc.dram_tensor("maskb", [128, 17, 128], F32, kind=ext).ap()
    k.convw_d = nc.dram_tensor("convw", [L, 128, 30], F32, kind=ext).ap()
    k.tri_d = nc.dram_tensor("tri", [128, 4, 128], F32, kind=ext).ap()
    k.out_d = nc.dram_tensor("out", [S_LEN, D], F32, kind="ExternalOutput").ap()
    sk = "ExternalOutput" if dbg else "Internal"
    k.cosT_d = nc.dram_tensor("cosT", [128, S_LEN], F32, kind=sk).ap()
    k.sinT_d = nc.dram_tensor("sinT", [128, S_LEN], F32, kind=sk).ap()
    k.qkvT_a_d = nc.dram_tensor("qkvT_a", [768, S_LEN], F32, kind=sk).ap()
    k.za_d = nc.dram_tensor("za", [S_LEN, 272], F32, kind=sk).ap()
    k.qkT_b_d = nc.dram_tensor("qkT_b", [512, S_LEN], BF16, kind=sk).ap()
    k.qkT_c_d = nc.dram_tensor("qkT_c", [1024, S_LEN], BF16, kind=sk).ap()
    k.v_b_d = nc.dram_tensor("v_b", [S_LEN, 4 * 65], BF16, kind=sk).ap()
    k.v_c_d = nc.dram_tensor("v_c", [S_LEN, 4 * 129], BF16, kind=sk).ap()
    k.mix_d = nc.dram_tensor("mix", [S_LEN, D], BF16, kind=sk).ap()
    k.xres_d = nc.dram_tensor("xres", [S_LEN, D], F32, kind=sk).ap()
    k.identf = nc.alloc_sbuf_tensor("identf", [128, 128], F32)
    k.identb = nc.alloc_sbuf_tensor("identb", [128, 128], BF16)
    k.cst = nc.alloc_sbuf_tensor("cst_sb", [128, 4], F32)
    k.smallp = nc.alloc_sbuf_tensor("smallp_sb", [128, 1024], F32)
    k.bidentf, k.bidentb, k.bcst, k.bsmallp = Buf(), Buf(), Buf(), Buf()
    k.pb = [nc.alloc_psum_tensor("pb%d" % i, [128, 512], F32) for i in range(8)]
    k.bpb = [Buf("pb%d" % i, excl=True) for i in range(8)]
    k.S.dma("sp", k.smallp[:], k.smallp_d, writes=[k.bsmallp])
    if phases is None:
        phases = ["setup"] + [p + str(l) for l in range(L) for p in ("a", "c", "b", "g", "d1", "d2")]
    for ph in phases:
        if ph == "setup":
            phase_setup(k)
        elif ph[0] == "a":
            phase_a(k, int(ph[1:]))
        elif ph[0] == "c":
            phase_c(k, int(ph[1:]))
        elif ph[0] == "b":
            phase_b(k, int(ph[1:]))
        elif ph[0] == "g":
            phase_g(k, int(ph[1:]))
        elif ph[0] == "z":
            phase_z(k)
        elif ph[:2] == "d1":
            phase_d1(k, int(ph[2:]))
        elif ph[:2] == "d2":
            phase_d2(k, int(ph[2:]))
    k.S.finish()
    k.ninstr = k.S.ninstr
    return nc, k


def phase_z(k):
    S, nc = k.S, k.nc
    with nc.sbuf_tensor("z_t", [128, 256], BF16) as zt:
        bz = Buf()
        S.op("dve", lambda e: e.memset(zt[:], 0.0), writes=[bz])
        for t in range(NT):
            S.dma("sp", k.mix_d[t * 128:(t + 1) * 128, 0:256], zt[:], reads=[bz])
        S.barrier()


def phase_g(k, l):
    S, nc = k.S, k.nc
    k.uid += 1
    sp_ = k.smallp
    with nc.sbuf_tensor("g_qkvn%d" % l, [128, 6, S_LEN], F32) as qkvn, \
            nc.sbuf_tensor("g_oall%d" % l, [128, NT, 256], F32) as oall, \
            nc.sbuf_tensor("g_gate%d" % l, [128, NT, 16], F32) as gab, \
            nc.sbuf_tensor("g_g%d" % l, [128, NT, 8], F32) as gg, \
            nc.sbuf_tensor("g_beta%d" % l, [128, NT, 8], F32) as beta, \
            nc.sbuf_tensor("g_cw%d" % l, [128, 30], F32) as cw, \
            nc.sbuf_tensor("g_tri%d" % l, [128, 4, 128], F32) as tri, \
            nc.sbuf_tensor("g_bd%d" % l, [128, 128], F32) as bd, \
            nc.sbuf_tensor("g_ones%d" % l, [128, 128], F32) as onesf, \
            nc.sbuf_tensor("g_st%d" % l, [128, 2, 2, 128], F32) as st, \
            nc.sbuf_tensor("g_obt%d" % l, [128, 2, 256], BF16) as obt, \
            nc.sbuf_tensor("g_arena%d" % l, [128, 15000], F32) as arena:
        off = [0]

        def carve(n):
            a = arena[:, off[0]:off[0] + n]
            off[0] += n
            return a

        bq = [Buf() for _ in range(6)]
        boall, bgab, bgg, bbeta, bcw, btri, bbd, bones = Buf(), Buf(), Buf(), Buf(), Buf(), Buf(), Buf(), Buf()
        bst = [Buf(), Buf()]
        S.dma("sp", cw[:], k.convw_d[l], writes=[bcw])
        S.dma("sp", tri[:], k.tri_d, writes=[btri])
        for t in range(NT):
            S.dma("sp", gab[:, t, :], k.za_d[t * 128:(t + 1) * 128, 256:272], writes=[bgab])
        S.op("dve", lambda e: e.memset(bd[:], 0.0), writes=[bbd])
        S.op("dve", lambda e: e.memset(bd[0:64, 0:64], 1.0), writes=[bbd])
        S.op("dve", lambda e: e.memset(bd[64:128, 64:128], 1.0), writes=[bbd])
        S.op("dve", lambda e: e.memset(onesf[:], 1.0), writes=[bones])
        S.op("dve", lambda e: e.memset(oall[:], 0.0), writes=[boall])
        S.op("dve", lambda e: e.memset(st[:], 0.0), writes=bst)
        ab = sp_[:, 896 + 8 * l:904 + 8 * l]
        db = sp_[:, 912 + 8 * l:920 + 8 * l]
        eal, beal = carve(8), Buf()
        S.op("act", lambda e: e.activation(out=eal, in_=ab, func=AF.Exp), reads=[k.bsmallp], writes=[beal])
        S.op("dve", lambda e: e.tensor_tensor(out=gg[:], in0=gab[:, :, 0:8], in1=db.rearrange("p (o c) -> p o c", o=1).broadcast_to([128, NT, 8]), op=ALU.add),
             reads=[bgab, k.bsmallp], writes=[bgg])
        S.op("act", lambda e: e.activation(out=gg[:], in_=gg[:], func=AF.Exp), reads=[bgg], writes=[bgg])
        S.op("act", lambda e: e.activation(out=gg[:], in_=gg[:], func=AF.Ln, bias=1.0), reads=[bgg], writes=[bgg])
        S.op("dve", lambda e: e.scalar_tensor_tensor(out=gg[:], in0=gg[:], scalar=-1.0, in1=eal.rearrange("p (o c) -> p o c", o=1).broadcast_to([128, NT, 8]),
                                                     op0=ALU.mult, op1=ALU.mult), reads=[bgg, beal], writes=[bgg])
        S.op("act", lambda e: e.activation(out=beta[:], in_=gab[:, :, 8:16], func=AF.Sigmoid), reads=[bgab], writes=[bbeta])
        xin, bxin = carve(S_LEN + 4), Buf()
        rs_r = Rot([(carve(512), Buf()) for _ in range(2)])
        S.op("dve", lambda e: e.memset(xin[:, 0:2], 0.0), writes=[bxin])
        S.op("dve", lambda e: e.memset(xin[:, S_LEN + 2:S_LEN + 4], 0.0), writes=[bxin])
        for ch in range(6):
            S.dma("sp", xin[:, 2:S_LEN + 2], k.qkvT_a_d[ch * 128:(ch + 1) * 128, :], writes=[bxin])
            qc = qkvn[:, ch, :]
            S.op("dve", lambda e: e.tensor_scalar(out=qc, in0=xin[:, 0:S_LEN], scalar1=cw[:, ch * 5:ch * 5 + 1], scalar2=None, op0=ALU.mult),
                 reads=[bxin, bcw], writes=[bq[ch]])
            for j in range(1, 5):
                S.op("dve", lambda e: e.scalar_tensor_tensor(out=qc, in0=xin[:, j:j + S_LEN], scalar=cw[:, ch * 5 + j:ch * 5 + j + 1], in1=qc,
                                                             op0=ALU.mult, op1=ALU.add), reads=[bxin, bcw, bq[ch]], writes=[bq[ch]])
            S.op("act", lambda e: e.activation(out=qc, in_=qc, func=AF.Silu), reads=[bq[ch]], writes=[bq[ch]])
            if ch < 4:
                S.op("act", lambda e: e.activation(out=xin[:, 2:S_LEN + 2], in_=qc, func=AF.Square), reads=[bq[ch]], writes=[bxin])
                for c8 in range(8):
                    cs = slice(c8 * 512, (c8 + 1) * 512)
                    pp, bpp = k.pb[c8 % 2], k.bpb[c8 % 2]
                    S.op("pe", lambda e: e.matmul(pp[:], lhsT=bd[:], rhs=xin[:, 2 + c8 * 512:2 + (c8 + 1) * 512], start=True, stop=True),
                         reads=[bbd, bxin], writes=[bpp])
                    rs, brs = rs_r.next()
                    S.op("dve", lambda e: e.tensor_scalar(out=rs, in0=pp[:], scalar1=1e-6, scalar2=None, op0=ALU.add), reads=[bpp], writes=[brs])
                    S.op("act", lambda e: e.activation(out=rs, in_=rs, func=AF.Sqrt, scale=(64.0 if ch < 2 else 1.0)), reads=[brs], writes=[brs])
                    S.op("dve", lambda e: e.reciprocal(out=rs, in_=rs), reads=[brs], writes=[brs])
                    S.op("dve", lambda e: e.tensor_tensor(out=qkvn[:, ch, cs], in0=qkvn[:, ch, cs], in1=rs, op=ALU.mult), reads=[bq[ch], brs], writes=[bq[ch]])
        def rot(n, cnt):
            return Rot([(carve(n), Buf()) for _ in range(cnt)])
        gc_r = rot(8, 4)
        eg_r = rot(12, 4)
        bge_r = rot(4, 4)
        ktok_r, vtok_r = rot(256, 2), rot(256, 2)
        vb_r, kbg_r, kdec_r = rot(256, 2), rot(256, 2), rot(256, 2)
        dg_r, expg_r, d1_r, d2_r = rot(128, 2), rot(128, 2), rot(128, 2), rot(128, 2)
        a_r, at_r = rot(128, 2), rot(128, 2)
        intra_r = rot(128, 8)
        xa_r, xb_r, p_r = rot(128, 3), rot(128, 3), rot(128, 3)
        u_r, kcT_r, qgT_r = rot(256, 2), rot(256, 2), rot(256, 2)
        vnew_r = rot(256, 2)
        bank = lambda i: (k.pb[i], k.bpb[i])
        nm_r = Rot([bank(4), bank(5)])

        def unit(t, dr):
            ts = slice(t * 128, (t + 1) * 128)
            MI, MS, MSA = tri[:, dr, :], tri[:, 2 + dr, :], tri[:, 3 - dr, :]
            g_t = gg[:, t, dr * 4:(dr + 1) * 4]
            be_t = beta[:, t, dr * 4:(dr + 1) * 4]
            p0, bp0 = bank(0)
            S.op("pe", lambda e: e.matmul(p0[:, 0:4], lhsT=MI, rhs=g_t, start=True, stop=True), reads=[btri, bgg], writes=[bp0])
            S.op("pe", lambda e: e.matmul(p0[:, 4:8], lhsT=onesf[:], rhs=g_t, start=True, stop=True), reads=[bones, bgg], writes=[bp0])
            gc, bgc = gc_r.next()
            S.op("act", lambda e: e.copy(out=gc, in_=p0[:, 0:8]), reads=[bp0], writes=[bgc])
            eg, beg = eg_r.next()
            S.op("dve", lambda e: e.tensor_tensor(out=eg[:, 4:8], in0=gc[:, 4:8], in1=gc[:, 0:4], op=ALU.subtract), reads=[bgc], writes=[beg])
            S.op("act", lambda e: e.activation(out=eg[:, 0:4], in_=gc[:, 0:4], func=AF.Exp), reads=[bgc], writes=[beg])
            S.op("act", lambda e: e.activation(out=eg[:, 4:8], in_=eg[:, 4:8], func=AF.Exp), reads=[beg], writes=[beg])
            S.op("act", lambda e: e.activation(out=eg[:, 8:12], in_=gc[:, 4:8], func=AF.Exp), reads=[bgc], writes=[beg])
            bge, bbge = bge_r.next()
            S.op("dve", lambda e: e.tensor_tensor(out=bge, in0=be_t, in1=eg[:, 0:4], op=ALU.mult), reads=[bbeta, beg], writes=[bbge])
            p1, bp1 = bank(1)
            for i, ch in enumerate((2, 3, 4, 5)):
                S.op("pe", lambda e: e.transpose(out=p1[:, i * 128:(i + 1) * 128], in_=qkvn[:, ch, ts], identity=k.identf[:]),
                     reads=[bq[ch], k.bidentf], writes=[bp1])
            ktok, bktok = ktok_r.next()
            vtok, bvtok = vtok_r.next()
            S.op("act", lambda e: e.copy(out=ktok, in_=p1[:, 0:256]), reads=[bp1], writes=[bktok])
            S.op("dve", lambda e: e.tensor_copy(out=vtok, in_=p1[:, 256:512]), reads=[bp1], writes=[bvtok])
            v3 = lambda a: a.rearrange("p (h d) -> p h d", h=4)
            bc = lambda a: a.rearrange("p (h o) -> p h o", o=1).broadcast_to([128, 4, 64])
            vb, bvb = vb_r.next()
            kbg, bkbg = kbg_r.next()
            kdec, bkdec = kdec_r.next()
            S.op("dve", lambda e: e.tensor_tensor(out=v3(vb), in0=v3(vtok), in1=bc(be_t), op=ALU.mult), reads=[bvtok, bbeta], writes=[bvb])
            S.op("dve", lambda e: e.tensor_tensor(out=v3(kbg), in0=v3(ktok), in1=bc(bge), op=ALU.mult), reads=[bktok, bbge], writes=[bkbg])
            S.op("dve", lambda e: e.tensor_tensor(out=v3(kdec), in0=v3(ktok), in1=bc(eg[:, 4:8]), op=ALU.mult), reads=[bktok, beg], writes=[bkdec])
            u, bu = u_r.next()
            kcT, bkcT = kcT_r.next()
            qgT, bqgT = qgT_r.next()
            intras = []
            p6, bp6 = bank(6)
            for h in range(4):
                rs = slice((h % 2) * 64, (h % 2) * 64 + 64)
                qT_h = qkvn[rs, h // 2, ts]
                kT_h = qkvn[rs, 2 + h // 2, ts]
                dg, bdg = dg_r.next()
                S.op("dve", lambda e: e.tensor_scalar(out=dg, in0=k.identf[:], scalar1=gc[:, h:h + 1], scalar2=None, op0=ALU.mult),
                     reads=[k.bidentf, bgc], writes=[bdg])
                p2, bp2 = bank(2)
                S.op("pe", lambda e: e.matmul(p2[:, 0:128], lhsT=onesf[:], rhs=dg, start=True, stop=True), reads=[bones, bdg], writes=[bp2])
                expg, bexpg = expg_r.next()
                d1, bd1 = d1_r.next()
                d2, bd2 = d2_r.next()
                S.op("act", lambda e: e.activation(out=expg, in_=p2[:, 0:128], func=AF.Exp), reads=[bp2], writes=[bexpg])
                S.op("dve", lambda e: e.tensor_scalar(out=d1, in0=p2[:, 0:128], scalar1=gc[:, h:h + 1], scalar2=0.0, op0=ALU.subtract, op1=ALU.min),
                     reads=[bp2, bgc], writes=[bd1])
                S.op("dve", lambda e: e.tensor_scalar(out=d2, in0=p2[:, 0:128], scalar1=gc[:, h:h + 1], scalar2=0.0, op0=ALU.subtract, op1=ALU.max),
                     reads=[bp2, bgc], writes=[bd2])
                S.op("act", lambda e: e.activation(out=d1, in_=d1, func=AF.Exp), reads=[bd1], writes=[bd1])
                S.op("act", lambda e: e.activation(out=d2, in_=d2, func=AF.Exp, scale=-1.0), reads=[bd2], writes=[bd2])
                S.op("dve", lambda e: e.tensor_tensor(out=d1, in0=d1, in1=MI, op=ALU.mult), reads=[bd1, btri], writes=[bd1])
                S.op("dve", lambda e: e.tensor_tensor(out=d2, in0=d2, in1=MSA, op=ALU.mult), reads=[bd2, btri], writes=[bd2])
                p3, bp3 = bank(3)
                S.op("pe", lambda e: e.matmul(p3[:, 0:128], lhsT=kT_h, rhs=kT_h, start=True, stop=True), reads=[bq[2 + h // 2]], writes=[bp3])
                S.op("pe", lambda e: e.matmul(p3[:, 128:256], lhsT=kT_h, rhs=qT_h, start=True, stop=True), reads=[bq[2 + h // 2], bq[h // 2]], writes=[bp3])
                xa, bxa = a_r.next()
                S.op("dve", lambda e: e.scalar_tensor_tensor(out=xa, in0=p3[:, 0:128], scalar=be_t[:, h:h + 1], in1=d2, op0=ALU.mult, op1=ALU.mult),
                     reads=[bp3, bbeta, bd2], writes=[bxa])
                intra, bintra = intra_r.next()
                S.op("dve", lambda e: e.tensor_tensor(out=intra, in0=p3[:, 128:256], in1=d1, op=ALU.mult), reads=[bp3, bd1], writes=[bintra])
                intras.append((intra, bintra))
                S.op("dve", lambda e: e.tensor_tensor(out=qgT[rs, (h // 2) * 128:(h // 2) * 128 + 128], in0=qT_h, in1=expg[rs, :], op=ALU.mult),
                     reads=[bq[h // 2], bexpg], writes=[bqgT])
                pn, bpn = nm_r.next()
                S.op("pe", lambda e: e.transpose(out=pn[:, 0:128], in_=xa, identity=k.identf[:]), reads=[bxa, k.bidentf], writes=[bpn])
                xb, bxb = at_r.next()
                S.op("act", lambda e: e.copy(out=xb, in_=pn[:, 0:128]), reads=[bpn], writes=[bxb])
                P, bP = p_r.next()
                S.op("dve", lambda e: e.tensor_tensor(out=P, in0=k.identf[:], in1=xb, op=ALU.subtract), reads=[k.bidentf, bxb], writes=[bP])
                for it in range(6):
                    pn, bpn = nm_r.next()
                    S.op("pe", lambda e: e.matmul(pn[:, 0:128], lhsT=xb, rhs=xa, start=True, stop=True), reads=[bxb, bxa], writes=[bpn])
                    xa2, bxa2 = xa_r.next()
                    S.op("act", lambda e: e.copy(out=xa2, in_=pn[:, 0:128]), reads=[bpn], writes=[bxa2])
                    if it < 5:
                        pn2, bpn2 = nm_r.next()
                        S.op("pe", lambda e: e.matmul(pn2[:, 0:128], lhsT=xa, rhs=xb, start=True, stop=True), reads=[bxb, bxa], writes=[bpn2])
                        xb2, bxb2 = xb_r.next()
                        S.op("dve", lambda e: e.tensor_copy(out=xb2, in_=pn2[:, 0:128]), reads=[bpn2], writes=[bxb2])
                    pn3, bpn3 = nm_r.next()
                    S.op("pe", lambda e: e.matmul(pn3[:, 0:128], lhsT=xa2, rhs=P, start=True, stop=True), reads=[bxa2, bP], writes=[bpn3])
                    P2, bP2 = p_r.next()
                    S.op("dve", lambda e: e.tensor_tensor(out=P2, in0=pn3[:, 0:128], in1=P, op=ALU.add), reads=[bpn3, bP], writes=[bP2])
                    P, bP = P2, bP2
                    xa, bxa = xa2, bxa2
                    if it < 5:
                        xb, bxb = xb2, bxb2
                S.op("pe", lambda e: e.matmul(p6[:, h * 64:(h + 1) * 64], lhsT=P, rhs=vb[:, h * 64:(h + 1) * 64], start=True, stop=True),
                     reads=[bP, bvb], writes=[bp6])
                S.op("pe", lambda e: e.matmul(p6[rs, 256 + (h // 2) * 128:256 + (h // 2) * 128 + 128], lhsT=kbg[:, h * 64:(h + 1) * 64], rhs=P,
                                              start=True, stop=True), reads=[bP, bkbg], writes=[bp6])
            S.op("act", lambda e: e.copy(out=u, in_=p6[:, 0:256]), reads=[bp6], writes=[bu])
            S.op("dve", lambda e: e.tensor_copy(out=kcT, in_=p6[:, 256:512]), reads=[bp6], writes=[bkcT])
            return dict(u=(u, bu), kcT=(kcT, bkcT), qgT=(qgT, bqgT), intras=intras, kdec=(kdec, bkdec), eg=(eg, beg))

        def step(t, dr, un):
            u, bu = un["u"]
            kcT, bkcT = un["kcT"]
            qgT, bqgT = un["qgT"]
            kdec, bkdec = un["kdec"]
            eg, beg = un["eg"]
            p7, bp7 = bank(7)
            for pr in range(2):
                ps_ = slice(pr * 128, (pr + 1) * 128)
                S.op("pe", lambda e: e.matmul(p7[:, ps_], lhsT=kcT[:, ps_], rhs=st[:, dr, pr, :], start=True, stop=True),
                     reads=[bkcT, bst[dr]], writes=[bp7])
            vnew, bvnew = vnew_r.next()
            S.op("dve", lambda e: e.tensor_tensor(out=vnew, in0=u, in1=p7[:, 0:256], op=ALU.subtract), reads=[bu, bp7], writes=[bvnew])
            for pr in range(2):
                ps_ = slice(256 + pr * 128, 256 + (pr + 1) * 128)
                S.op("pe", lambda e: e.matmul(p7[:, ps_], lhsT=qgT[:, pr * 128:(pr + 1) * 128], rhs=st[:, dr, pr, :], start=True, stop=False),
                     reads=[bqgT, bst[dr]], writes=[bp7])
                for h in (2 * pr, 2 * pr + 1):
                    intra, bintra = un["intras"][h]
                    S.op("pe", lambda e: e.matmul(p7[:, 256 + h * 64:256 + (h + 1) * 64], lhsT=intra, rhs=vnew[:, h * 64:(h + 1) * 64],
                                                  start=False, stop=(h == 2 * pr + 1)), reads=[bintra, bvnew], writes=[bp7])
            S.op("dve", lambda e: e.tensor_tensor(out=oall[:, t, :], in0=p7[:, 256:512], in1=oall[:, t, :], op=ALU.add), reads=[bp7, boall], writes=[boall])
            p0, bp0 = bank(0)
            for pr in range(2):
                ps_ = slice(pr * 128, (pr + 1) * 128)
                S.op("pe", lambda e: e.matmul(p0[:, 64 + pr * 128:64 + (pr + 1) * 128], lhsT=kdec[:, ps_], rhs=vnew[:, ps_], start=True, stop=True),
                     reads=[bkdec, bvnew], writes=[bp0])
            for h in range(4):
                pr = h // 2
                rs = slice((h % 2) * 64, (h % 2) * 64 + 64)
                cs = slice((h % 2) * 64, (h % 2) * 64 + 64)
                sv = st[rs, dr, pr, cs]
                S.op("dve", lambda e: e.scalar_tensor_tensor(out=sv, in0=sv, scalar=eg[rs, 8 + h:9 + h],
                                                             in1=p0[rs, 64 + pr * 128 + (h % 2) * 64:64 + pr * 128 + (h % 2) * 64 + 64],
                                                             op0=ALU.mult, op1=ALU.add), reads=[bst[dr], beg, bp0], writes=[bst[dr]])

        for i in range(NT):
            for dr in range(2):
                t = i if dr == 0 else NT - 1 - i
                un = unit(t, dr)
                step(t, dr, un)
        z_r = rot(256, 2)
        sq_r = rot(256, 2)
        r4_r = rot(4, 2)
        gnw = sp_[:, 768 + 64 * l:768 + 64 * (l + 1)]
        ob_r = Rot([(obt[:, i, :], Buf()) for i in range(2)])
        for t in range(NT):
            o3 = oall[:, t, :].rearrange("p (h d) -> p h d", h=4)
            z, bz = z_r.next()
            S.dma("sp", z, k.za_d[t * 128:(t + 1) * 128, 0:256], writes=[bz])
            S.op("act", lambda e: e.activation(out=z, in_=z, func=AF.Silu), reads=[bz], writes=[bz])
            sq, bsq = sq_r.next()
            S.op("dve", lambda e: e.tensor_tensor(out=sq, in0=oall[:, t, :], in1=oall[:, t, :], op=ALU.mult), reads=[boall], writes=[bsq])
            r4, br4 = r4_r.next()
            S.op("dve", lambda e: e.tensor_reduce(out=r4, in_=sq.rearrange("p (h d) -> p h d", h=4), axis=AX.X, op=ALU.add), reads=[bsq], writes=[br4])
            S.op("dve", lambda e: e.tensor_scalar(out=r4, in0=r4, scalar1=1.0 / 64, scalar2=EPS, op0=ALU.mult, op1=ALU.add), reads=[br4], writes=[br4])
            S.op("act", lambda e: e.activation(out=r4, in_=r4, func=AF.Sqrt), reads=[br4], writes=[br4])
            S.op("dve", lambda e: e.reciprocal(out=r4, in_=r4), reads=[br4], writes=[br4])
            s3 = sq.rearrange("p (h d) -> p h d", h=4)
            S.op("dve", lambda e: e.tensor_tensor(out=s3, in0=o3, in1=r4.rearrange("p (h o) -> p h o", o=1).broadcast_to([128, 4, 64]), op=ALU.mult),
                 reads=[boall, br4], writes=[bsq])
            S.op("dve", lambda e: e.tensor_tensor(out=s3, in0=s3, in1=gnw.rearrange("p (o d) -> p o d", o=1).broadcast_to([128, 4, 64]), op=ALU.mult),
                 reads=[bsq, k.bsmallp], writes=[bsq])
            ob, bob = ob_r.next()
            S.op("dve", lambda e: e.tensor_tensor(out=ob, in0=sq, in1=z, op=ALU.mult), reads=[bsq, bz], writes=[bob])
            S.dma("sp", k.mix_d[t * 128:(t + 1) * 128, 0:256], ob, reads=[bob])
        S.barrier()


def _col_index():
    def rot(a):
        return a.reshape(-1, 2, 32)[:, ::-1, :].reshape(-1)
    bqk = np.arange(1040, 1552)
    cqk = np.arange(1808, 2832)
    return np.concatenate([np.arange(0, 768), bqk, rot(bqk), cqk, rot(cqk),
                           np.arange(768, 1040), np.arange(1552, 1808), np.arange(2832, 3344)])


def _consts():
    p = np.arange(128)
    inv = (10000.0 ** (-np.arange(0, 64, 2, dtype=np.float32) / np.float32(64))).astype(np.float32)
    cst = np.zeros((128, 4), np.float32)
    cst[:, 0] = inv[p % 32]
    cst[:, 1] = np.where((p % 64) < 32, -1.0, 1.0)
    cst[:, 2] = math.pi / 2
    kk = np.arange(128)[:, None, None]
    dd = np.arange(17)[None, :, None] - 8
    qq = np.arange(128)[None, None, :]
    dist = np.abs(dd * 128 + kk - qq)
    m = (dist <= 64).astype(np.float32) + ((dist % 4 == 0) & (dist <= 256)) + ((dist % 16 == 0) & (dist <= 1024))
    return cst, m.astype(np.float32)


def make_in_maps(inputs):
    f = lambda a: np.ascontiguousarray(np.asarray(a))
    idx = _col_index()
    w_in_ext = f(np.asarray(inputs["w_in"])[:, :, idx])
    cst, mask = _consts()
    sp = np.zeros((1024,), np.float32)
    for l in range(L):
        sp[256 * l:256 * l + 64] = inputs["lambda_q1"][l]
        sp[256 * l + 64:256 * l + 128] = inputs["lambda_k1"][l]
        sp[256 * l + 128:256 * l + 192] = inputs["lambda_q2"][l]
        sp[256 * l + 192:256 * l + 256] = inputs["lambda_k2"][l]
        sp[512 + 128 * l:512 + 128 * (l + 1)] = inputs["subln_w"][l]
        sp[768 + 64 * l:768 + 64 * (l + 1)] = inputs["gdn_norm_w"][l]
        sp[896 + 8 * l:896 + 8 * (l + 1)] = np.asarray(inputs["a_log"][l]).reshape(-1)
        sp[912 + 8 * l:912 + 8 * (l + 1)] = np.asarray(inputs["dt_bias"][l]).reshape(-1)
    smallp = f(np.broadcast_to(sp[None, :], (128, 1024)))
    cwl = np.asarray(inputs["conv_w"]).astype(np.float32)
    convw = f(cwl.transpose(0, 2, 1).reshape(L, 6, 128, 5).transpose(0, 2, 1, 3).reshape(L, 128, 30))
    r_, c_ = np.arange(128)[:, None], np.arange(128)[None, :]
    tri = f(np.stack([(c_ >= r_), (c_ <= r_), (c_ > r_), (c_ < r_)], axis=1).astype(np.float32))
    shared = {
        "convw": convw, "tri": tri,
        "w_in": w_in_ext, "attn_nw": f(inputs["attn_norm_w"]), "w_out": f(inputs["w_out"]),
        "ffn_nw": f(inputs["ffn_norm_w"]), "w_gate": f(inputs["w_gate"]), "w_up": f(inputs["w_up"]),
        "w_down": f(inputs["w_down"]), "final_nw": f(inputs["final_norm_w"]), "smallp": smallp,
        "cst": cst, "maskb": mask,
    }
    x = np.asarray(inputs["x"])
    pos = np.asarray(inputs["positions"]).astype(np.int32)
    maps = []
    for b in range(8):
        m = dict(shared)
        m["x"] = f(x[b])
        m["pos"] = f(pos[b])
        maps.append(m)
    return maps


def kernel(**inputs):
    nc, _ = build()
    maps = make_in_maps(inputs)
    res = run_bass_kernel_spmd(nc, maps, core_ids=list(range(8)))
    return np.stack([r["out"] for r in res.results], axis=0).astype(np.float32)
```

```python
import math
import numpy as np
import concourse.bass as bass
import concourse.mybir as mybir
from concourse.bass_utils import run_bass_kernel_spmd

F32 = mybir.dt.float32
BF16 = mybir.dt.bfloat16
I32 = mybir.dt.int32
AF = mybir.ActivationFunctionType
ALU = mybir.AluOpType
AX = mybir.AxisListType

S_LEN = 4096
D = 1024
NT = 32
L = 2
DFF = 2816
NCOL = 4880
EPS = 1e-6
TWO_PI = 2.0 * math.pi
C1 = 6.28125
C2 = TWO_PI - C1


class Buf:
    __slots__ = ("name", "writer", "readers", "excl")

    def __init__(self, name="", excl=False):
        self.name = name
        self.writer = None
        self.readers = []
        self.excl = excl


class _Rec:
    def __getattr__(self, name):
        def f(*a, **kw):
            self.__dict__["call"] = (name, a, kw)
            return self
        return f


def _bind(fn):
    rec = _Rec()
    fn(rec)
    name, a, kw = rec.call
    return lambda e: getattr(e, name)(*a, **kw)


class Sched:
    CENG = ("pe", "act", "dve", "pool")
    DQ = ("sp", "act", "pool")

    def __init__(self, nc, n_dma_sems=8):
        self.nc = nc
        self.prog = {e: [] for e in ("pe", "act", "dve", "pool", "sp")}
        self.csem = {e: nc.alloc_semaphore("c_" + e) for e in self.CENG}
        self.cnt = {e: 0 for e in self.CENG}
        self.nd = n_dma_sems
        self.dsem = {q: [nc.alloc_semaphore("d_%s%d" % (q, i)) for i in range(n_dma_sems)]
                     for q in self.DQ}
        self.dcnt = {q: 0 for q in self.DQ}
        self.seen = {e: {} for e in self.prog}
        self.ninstr = 0

    def _sem(self, key):
        return self.csem[key[1]] if key[0] == "c" else self.dsem[key[1]][key[2]]

    def _need(self, eng, tok, waits):
        key, val, _ = tok
        if self.seen[eng].get(key, 0) >= val:
            return
        self.seen[eng][key] = val
        waits.append((self._sem(key), val))

    def _deps(self, eng, reads, writes, is_dma):
        waits = []
        for b in reads:
            t = b.writer
            if t is not None and not (eng == "pe" and t[2] == "pe"):
                self._need(eng, t, waits)
        for b in writes:
            t = b.writer
            if t is not None and (is_dma or t[2] != eng or eng != "pe"):
                self._need(eng, t, waits)
            for t in b.readers:
                if is_dma or t[2] != eng:
                    self._need(eng, t, waits)
        return waits

    def _commit(self, tok, reads, writes):
        for b in reads:
            b.readers.append(tok)
        for b in writes:
            b.writer = tok
            b.readers = []

    def op(self, eng, fn, reads=(), writes=()):
        ex = [b for b in reads if b.excl and b not in writes]
        if ex:
            writes = list(writes) + ex
        waits = self._deps(eng, reads, writes, False)
        self.cnt[eng] += 1
        n = self.cnt[eng]
        self.prog[eng].append((waits, _bind(fn), (self.csem[eng], 1)))
        self._commit((("c", eng), n, eng), reads, writes)
        self.ninstr += 1

    def dma(self, q, out, in_, reads=(), writes=(), **kw):
        waits = self._deps(q, reads, writes, True)
        i = self.dcnt[q]
        self.dcnt[q] += 1
        j = i % self.nd
        key = ("d", q, j)
        prev = 16 * (i // self.nd)
        if prev > 0:
            self._need(q, (key, prev, None), waits)
        tgt = prev + 16
        self.prog[q].append((waits, lambda e: e.dma_start(out=out, in_=in_, **kw), (self.dsem[q][j], 16)))
        self._commit((key, tgt, None), reads, writes)
        self.ninstr += 1

    def _all_tokens(self):
        toks = []
        for q in self.DQ:
            for j in range(self.nd):
                n = (self.dcnt[q] - j + self.nd - 1) // self.nd
                if n > 0:
                    toks.append((("d", q, j), 16 * n, None))
        for e in self.CENG:
            if self.cnt[e] > 0:
                toks.append((("c", e), self.cnt[e], e))
        return toks

    def barrier(self):
        toks = self._all_tokens()
        for eng in self.prog:
            waits = []
            for t in toks:
                if t[2] == eng:
                    continue
                self._need(eng, t, waits)
            if waits:
                self.prog[eng].append((waits, None, None))

    def finish(self):
        nc = self.nc
        final = [(self._sem(k), v) for k, v, _ in self._all_tokens()]
        prog = self.prog

        def replay(eng, lst, extra=()):
            for waits, fn, inc in lst:
                for s, v in waits:
                    eng.wait_ge(s, v)
                if fn is None:
                    continue
                ins = fn(eng)
                if inc is not None:
                    ins.then_inc(inc[0], inc[1])
            for s, v in extra:
                eng.wait_ge(s, v)

        with nc.Block() as block:
            @block.sync
            def _(e):
                replay(e, prog["sp"], final)

            @block.tensor
            def _(e):
                replay(e, prog["pe"])

            @block.scalar
            def _(e):
                replay(e, prog["act"])

            @block.vector
            def _(e):
                replay(e, prog["dve"])

            @block.gpsimd
            def _(e):
                replay(e, prog["pool"])


class Rot:
    def __init__(self, items):
        self.items = items
        self.i = 0

    def next(self):
        it = self.items[self.i % len(self.items)]
        self.i += 1
        return it


class K:
    pass


def sb(k, name, shape, dt, n=1):
    items = []
    for i in range(n):
        t = k.nc.alloc_sbuf_tensor("%s_%d_%d" % (name, k.uid, i), list(shape), dt)
        items.append((t, Buf(name)))
    k.uid += 1
    return items[0] if n == 1 else Rot(items)


def phase_setup(k):
    S, nc = k.S, k.nc
    S.op("pool", lambda e: e.memset(k.identf[:], 0.0), writes=[k.bidentf])
    S.op("pool", lambda e: e.affine_select(out=k.identf[:], in_=k.identf[:], pattern=[[-1, 128]],
                                           compare_op=ALU.not_equal, fill=1.0, base=0, channel_multiplier=1),
         reads=[k.bidentf], writes=[k.bidentf])
    S.op("dve", lambda e: e.tensor_copy(out=k.identb[:], in_=k.identf[:]), reads=[k.bidentf], writes=[k.bidentb])
    S.dma("sp", k.cst[:], k.cst_d, writes=[k.bcst])
    with nc.sbuf_tensor("su_pi", [128, S_LEN], I32) as pi, nc.sbuf_tensor("su_a", [128, S_LEN], F32) as ang, \
            nc.sbuf_tensor("su_k", [128, S_LEN], I32) as ki, nc.sbuf_tensor("su_kf", [128, S_LEN], F32) as kf, \
            nc.sbuf_tensor("su_r", [128, S_LEN], F32) as r, nc.sbuf_tensor("su_o", [128, S_LEN], F32) as o:
        bpi, bang, bki, bkf, br, bo = Buf(), Buf(), Buf(), Buf(), Buf(), Buf()
        S.dma("sp", pi[:], k.pos_d.rearrange("(o n) -> o n", o=1).broadcast_to([128, S_LEN]), writes=[bpi])
        S.op("dve", lambda e: e.tensor_copy(out=ang[:], in_=pi[:]), reads=[bpi], writes=[bang])
        S.op("dve", lambda e: e.tensor_scalar(out=ang[:], in0=ang[:], scalar1=k.cst[:, 0:1], scalar2=None, op0=ALU.mult),
             reads=[bang, k.bcst], writes=[bang])
        S.op("dve", lambda e: e.tensor_scalar(out=ki[:], in0=ang[:], scalar1=1.0 / TWO_PI, scalar2=None, op0=ALU.mult),
             reads=[bang], writes=[bki])
        S.op("dve", lambda e: e.tensor_copy(out=kf[:], in_=ki[:]), reads=[bki], writes=[bkf])
        S.op("dve", lambda e: e.scalar_tensor_tensor(out=r[:], in0=kf[:], scalar=-C1, in1=ang[:], op0=ALU.mult, op1=ALU.add),
             reads=[bkf, bang], writes=[br])
        S.op("dve", lambda e: e.scalar_tensor_tensor(out=r[:], in0=kf[:], scalar=-C2, in1=r[:], op0=ALU.mult, op1=ALU.add),
             reads=[bkf, br], writes=[br])
        S.op("dve", lambda e: e.tensor_scalar(out=r[:], in0=r[:], scalar1=-3.1415925, scalar2=3.1415925, op0=ALU.max, op1=ALU.min),
             reads=[br], writes=[br])
        S.op("act", lambda e: e.activation(out=o[:], in_=r[:], func=AF.Sin), reads=[br], writes=[bo])
        S.op("dve", lambda e: e.tensor_scalar(out=o[:], in0=o[:], scalar1=k.cst[:, 1:2], scalar2=None, op0=ALU.mult),
             reads=[bo, k.bcst], writes=[bo])
        S.dma("sp", k.sinT_d, o[:], reads=[bo])
        S.op("act", lambda e: e.activation(out=r[:], in_=r[:], func=AF.Abs), reads=[br], writes=[br])
        S.op("act", lambda e: e.activation(out=kf[:], in_=r[:], func=AF.Sin, scale=-1.0, bias=k.cst[:, 2:3]),
             reads=[br, k.bcst], writes=[bkf])
        S.dma("sp", k.cosT_d, kf[:], reads=[bkf])
        S.barrier()


def norm_tile(k, xt, bx, nwb, bnwb, h, bh, ss_r, junk_r):
    S = k.S
    ss, bss = ss_r.next()
    junk, bj = junk_r.next()
    S.op("act", lambda e: e.activation(out=junk[:], in_=xt[:], func=AF.Square, accum_out=ss[:]),
         reads=[bx], writes=[bj, bss])
    S.op("dve", lambda e: e.tensor_scalar(out=ss[:], in0=ss[:], scalar1=1.0 / D, scalar2=EPS, op0=ALU.mult, op1=ALU.add),
         reads=[bss], writes=[bss])
    S.op("act", lambda e: e.activation(out=ss[:], in_=ss[:], func=AF.Sqrt), reads=[bss], writes=[bss])
    S.op("dve", lambda e: e.reciprocal(out=ss[:], in_=ss[:]), reads=[bss], writes=[bss])
    S.op("dve", lambda e: e.scalar_tensor_tensor(out=h[:], in0=xt[:], scalar=ss[:, 0:1], in1=nwb[:], op0=ALU.mult, op1=ALU.mult),
         reads=[bx, bss, bnwb], writes=[bh])


def phase_a(k, l):
    S, nc = k.S, k.nc
    k.uid += 1
    src = k.x_d if l == 0 else k.xres_d
    with nc.sbuf_tensor("a_wb%d" % l, [128, 8, NCOL], BF16) as wb, \
            nc.sbuf_tensor("a_nwb%d" % l, [128, D], F32) as nwb, \
            nc.sbuf_tensor("a_arena%d" % l, [128, 12000], F32) as arena, nc.sbuf_tensor("a_arenab%d" % l, [128, 16000], BF16) as arenab:
        off = [0]

        offb = [0]

        def carve(n_f32, dt=F32):
            if dt == BF16:
                a = arenab[:, offb[0]:offb[0] + 2 * n_f32]
                offb[0] += 2 * n_f32
                return a
            a = arena[:, off[0]:off[0] + n_f32]
            off[0] += n_f32
            return a

        bwb = [Buf() for _ in range(8)]
        bnwb = Buf()
        for kk in range(8):
            S.dma("pool", wb[:, kk, :], k.w_in_d[l, kk * 128:(kk + 1) * 128, :], writes=[bwb[kk]])
        S.dma("sp", nwb[:], k.attn_nw_d[l:l + 1, :].broadcast_to([128, D]), writes=[bnwb])
        xt_r = Rot([(carve(1024), Buf()) for _ in range(2)])
        junk_r = Rot([(carve(1024), Buf()) for _ in range(1)])
        h_r = Rot([(carve(512, BF16), Buf()) for _ in range(2)])
        ss_r = Rot([(carve(1), Buf()) for _ in range(2)])
        hT_r = Rot([(carve(2048, BF16), Buf()) for _ in range(2)])
        cos_r = Rot([(carve(512), Buf()) for _ in range(2)])
        sin_r = Rot([(carve(512), Buf()) for _ in range(2)])
        ofm_r = Rot([(carve(512), Buf()) for _ in range(3)])
        t1_r = Rot([(carve(512), Buf()) for _ in range(2)])
        t2_r = Rot([(carve(512), Buf()) for _ in range(2)])
        oqk_r = Rot([(carve(256, BF16), Buf()) for _ in range(3)])
        oza_r = Rot([(carve(272), Buf()) for _ in range(2)])
        ovb_r = Rot([(carve(130, BF16), Buf()) for _ in range(2)])
        ovc_r = Rot([(carve(258, BF16), Buf()) for _ in range(2)])
        for (t, b) in ovb_r.items:
            S.op("dve", lambda e, t=t: e.memset(t, 1.0), writes=[b])
        for (t, b) in ovc_r.items:
            S.op("dve", lambda e, t=t: e.memset(t, 1.0), writes=[b])
        ptr_r = Rot([(k.pb[0], k.bpb[0]), (k.pb[1], k.bpb[1])])
        pfm_r = Rot([(k.pb[i], k.bpb[i]) for i in (2, 3, 4, 5)])
        ptm_r = Rot([(k.pb[i], k.bpb[i]) for i in (6, 7)])

        for g in range(8):
            hT, bhT = hT_r.next()
            hT3 = hT.rearrange("p (k n) -> p k n", k=8)
            cg, bcg = cos_r.next()
            sg, bsg = sin_r.next()
            S.dma("sp", cg, k.cosT_d[:, g * 512:(g + 1) * 512], writes=[bcg])
            S.dma("sp", sg, k.sinT_d[:, g * 512:(g + 1) * 512], writes=[bsg])
            for s in range(4):
                t0 = g * 512 + s * 128
                xt, bx = xt_r.next()
                h, bh = h_r.next()
                S.dma("sp", xt, src[t0:t0 + 128, :], writes=[bx])
                norm_tile(k, xt, bx, nwb, bnwb, h, bh, ss_r, junk_r)
                pt, bpt = ptr_r.next()
                ptb = pt[:].bitcast(BF16).rearrange("p (k n) -> p k n", k=8)
                for kk in range(8):
                    S.op("pe", lambda e, kk=kk, ptb=ptb, h=h: e.transpose(out=ptb[:, kk, :], in_=h[:, kk * 128:(kk + 1) * 128],
                                                                          identity=k.identb[:]),
                         reads=[bh, k.bidentb], writes=[bpt])
                S.op("dve", lambda e, ptb=ptb, hT3=hT3, s=s: e.tensor_copy(out=hT3[:, :, s * 128:(s + 1) * 128], in_=ptb),
                     reads=[bpt], writes=[bhT])
            tsl = slice(g * 512, (g + 1) * 512)

            def fm_mm(ch):
                pf, bpf = pfm_r.next()
                for kk in range(8):
                    S.op("pe", lambda e, kk=kk, pf=pf, ch=ch: e.matmul(pf[:], lhsT=wb[:, kk, ch * 128:(ch + 1) * 128],
                                                                      rhs=hT3[:, kk, :], start=(kk == 0), stop=(kk == 7)),
                         reads=[bwb[kk], bhT], writes=[bpf])
                return pf, bpf

            for ch in range(6):
                pf, bpf = fm_mm(ch)
                o, bo = ofm_r.next()
                S.op("act", lambda e, o=o, pf=pf: e.copy(out=o, in_=pf[:]), reads=[bpf], writes=[bo])
                S.dma("sp", k.qkvT_a_d[ch * 128:(ch + 1) * 128, tsl], o, reads=[bo])
            for (c0, nch, dst) in ((6, 4, k.qkT_b_d), (14, 8, k.qkT_c_d)):
                for j in range(nch):
                    pf, bpf = fm_mm(c0 + j)
                    pr, bpr = fm_mm(c0 + nch + j)
                    t1, bt1 = t1_r.next()
                    t2, bt2 = t2_r.next()
                    o, bo = oqk_r.next()
                    S.op("dve", lambda e, t1=t1, pf=pf: e.tensor_tensor(out=t1, in0=pf[:], in1=cg, op=ALU.mult),
                         reads=[bpf, bcg], writes=[bt1])
                    S.op("dve", lambda e, t2=t2, pr=pr: e.tensor_tensor(out=t2, in0=pr[:], in1=sg, op=ALU.mult),
                         reads=[bpr, bsg], writes=[bt2])
                    S.op("dve", lambda e, t1=t1, t2=t2, o=o: e.tensor_tensor(out=o, in0=t1, in1=t2, op=ALU.add),
                         reads=[bt1, bt2], writes=[bo])
                    S.dma("sp", dst[j * 128:(j + 1) * 128, tsl], o, reads=[bo])
            for s in range(4):
                t0 = g * 512 + s * 128

                def tm_mm(c0, n):
                    pm, bpm = ptm_r.next()
                    for kk in range(8):
                        S.op("pe", lambda e, kk=kk, pm=pm: e.matmul(pm[:, 0:n], lhsT=hT3[:, kk, s * 128:(s + 1) * 128],
                                                                   rhs=wb[:, kk, c0:c0 + n], start=(kk == 0), stop=(kk == 7)),
                             reads=[bwb[kk], bhT], writes=[bpm])
                    return pm, bpm

                pm, bpm = tm_mm(3840, 272)
                o, bo = oza_r.next()
                S.op("act", lambda e, o=o, pm=pm: e.copy(out=o, in_=pm[:, 0:272]), reads=[bpm], writes=[bo])
                S.dma("sp", k.za_d[t0:t0 + 128, :], o, reads=[bo])
                pm, bpm = tm_mm(4112, 256)
                o, bo = ovb_r.next()
                o3 = o.rearrange("p (h d) -> p h d", h=4)
                S.op("act", lambda e, o3=o3, pm=pm: e.copy(out=o3[:, :, 0:64], in_=pm[:, 0:256].rearrange("p (h d) -> p h d", h=4)),
                     reads=[bpm], writes=[bo])
                S.dma("sp", k.v_b_d[t0:t0 + 128, :], o, reads=[bo])
                pm, bpm = tm_mm(4368, 512)
                o, bo = ovc_r.next()
                o3 = o.rearrange("p (h d) -> p h d", h=4)
                S.op("act", lambda e, o3=o3, pm=pm: e.copy(out=o3[:, :, 0:128], in_=pm[:, 0:512].rearrange("p (h d) -> p h d", h=4)),
                     reads=[bpm], writes=[bo])
                S.dma("sp", k.v_c_d[t0:t0 + 128, :], o, reads=[bo])
        S.barrier()


def phase_c(k, l):
    S, nc = k.S, k.nc
    k.uid += 1
    lam_init = 0.8 - 0.6 * math.exp(-0.3 * l)
    with nc.sbuf_tensor("c_kT%d" % l, [128, S_LEN], BF16) as kT, nc.sbuf_tensor("c_qT%d" % l, [128, S_LEN], BF16) as qT, \
            nc.sbuf_tensor("c_v%d" % l, [128, NT, 129], BF16) as va, \
            nc.sbuf_tensor("c_arena%d" % l, [128, 3000], F32) as arena, nc.sbuf_tensor("c_arenab%d" % l, [128, 4000], BF16) as arenab:
        off = [0]

        offb = [0]

        def carve(n_f32, dt=F32):
            if dt == BF16:
                a = arenab[:, offb[0]:offb[0] + 2 * n_f32]
                offb[0] += 2 * n_f32
                return a
            a = arena[:, off[0]:off[0] + n_f32]
            off[0] += n_f32
            return a

        bkT, bqT, bva = Buf(), Buf(), Buf()
        p_r = Rot([(carve(256, BF16), Buf()) for _ in range(4)])
        o1_r = Rot([(carve(512), Buf()) for _ in range(1)])
        rc_r = Rot([(carve(4), Buf()) for _ in range(2)])
        o_r = Rot([(carve(128), Buf()) for _ in range(2)])
        junk_r = Rot([(carve(128), Buf()) for _ in range(1)])
        ss_r = Rot([(carve(1), Buf()) for _ in range(2)])
        ob_r = Rot([(carve(64, BF16), Buf()) for _ in range(2)])
        lam, blam = carve(4), Buf()
        ps_r = Rot([(k.pb[i], k.bpb[i]) for i in (0, 1, 6, 7)])
        acc = [(k.pb[i], k.bpb[i]) for i in (2, 3, 4, 5)]
        sp_ = k.smallp
        c0 = 256 * l
        jk, bjk = junk_r.next()
        S.op("dve", lambda e: e.tensor_tensor(out=jk[:, 0:64], in0=sp_[:, c0:c0 + 64], in1=sp_[:, c0 + 64:c0 + 128], op=ALU.mult),
             reads=[k.bsmallp], writes=[bjk])
        S.op("dve", lambda e: e.tensor_reduce(out=lam[:, 0:1], in_=jk[:, 0:64], axis=AX.X, op=ALU.add), reads=[bjk], writes=[blam])
        S.op("dve", lambda e: e.tensor_tensor(out=jk[:, 0:64], in0=sp_[:, c0 + 128:c0 + 192], in1=sp_[:, c0 + 192:c0 + 256], op=ALU.mult),
             reads=[k.bsmallp, blam], writes=[bjk])
        S.op("dve", lambda e: e.tensor_reduce(out=lam[:, 1:2], in_=jk[:, 0:64], axis=AX.X, op=ALU.add), reads=[bjk, blam], writes=[blam])
        S.op("act", lambda e: e.activation(out=lam[:, 0:2], in_=lam[:, 0:2], func=AF.Exp), reads=[blam], writes=[blam])
        S.op("dve", lambda e: e.tensor_tensor(out=lam[:, 2:3], in0=lam[:, 1:2], in1=lam[:, 0:1], op=ALU.subtract), reads=[blam], writes=[blam])
        S.op("dve", lambda e: e.tensor_scalar(out=lam[:, 2:3], in0=lam[:, 2:3], scalar1=-lam_init, scalar2=None, op0=ALU.add),
             reads=[blam], writes=[blam])
        subw = sp_[:, 512 + 128 * l:512 + 128 * (l + 1)]
        for hh in range(4):
            S.dma("sp", qT[:], k.qkT_c_d[hh * 128:(hh + 1) * 128, :], writes=[bqT])
            S.dma("sp", kT[:], k.qkT_c_d[512 + hh * 128:512 + (hh + 1) * 128, :], writes=[bkT])
            S.dma("sp", va[:], k.v_c_d.rearrange("(t p) (h d) -> p t h d", p=128, h=4)[:, :, hh, :], writes=[bva])
            for qg in range(8):
                o1, bo1 = o1_r.next()
                for c in range(2):
                    rs = slice(c * 64, (c + 1) * 64)
                    pend = []
                    for kb in range(NT + 2):
                        if kb < NT:
                            ps, bps = ps_r.next()
                            S.op("pe", lambda e, ps=ps, kb=kb, rs=rs: e.matmul(ps[:], lhsT=kT[rs, kb * 128:(kb + 1) * 128],
                                                                              rhs=qT[rs, qg * 512:(qg + 1) * 512], start=True, stop=True),
                                 reads=[bkT, bqT], writes=[bps])
                            p, bp = p_r.next()
                            S.op("act", lambda e, p=p, ps=ps: e.activation(out=p, in_=ps[:], func=AF.Exp, scale=0.125),
                                 reads=[bps], writes=[bp])
                            pend.append((p, bp, kb))
                        if kb >= 2:
                            p, bp, kq = pend.pop(0)
                            for s in range(4):
                                a, ba = acc[s]
                                S.op("pe", lambda e, a=a, p=p, s=s, kq=kq: e.matmul(a[:, 0:129], lhsT=p[:, s * 128:(s + 1) * 128],
                                                                                   rhs=va[:, kq, :], start=(kq == 0), stop=(kq == NT - 1)),
                                     reads=[bp, bva], writes=[ba])
                    for s in range(4):
                        a, ba = acc[s]
                        rc, brc = rc_r.next()
                        S.op("dve", lambda e, rc=rc, a=a: e.reciprocal(out=rc[:, 0:1], in_=a[:, 128:129]), reads=[ba], writes=[brc])
                        if c == 0:
                            S.op("dve", lambda e, a=a, rc=rc, s=s: e.tensor_scalar(out=o1[:, s * 128:(s + 1) * 128], in0=a[:, 0:128],
                                                                                  scalar1=rc[:, 0:1], scalar2=None, op0=ALU.mult),
                                 reads=[ba, brc], writes=[bo1])
                        else:
                            S.op("dve", lambda e, rc=rc: e.tensor_tensor(out=rc[:, 1:2], in0=rc[:, 0:1], in1=lam[:, 2:3], op=ALU.mult),
                                 reads=[brc, blam], writes=[brc])
                            o, bo = o_r.next()
                            S.op("dve", lambda e, a=a, rc=rc, s=s, o=o: e.scalar_tensor_tensor(
                                out=o, in0=a[:, 0:128], scalar=rc[:, 1:2], in1=o1[:, s * 128:(s + 1) * 128], op0=ALU.mult, op1=ALU.add),
                                 reads=[ba, brc, bo1], writes=[bo])
                            ss, bss = ss_r.next()
                            jk, bjk = junk_r.next()
                            S.op("act", lambda e, jk=jk, o=o, ss=ss: e.activation(out=jk, in_=o, func=AF.Square, accum_out=ss),
                                 reads=[bo], writes=[bjk, bss])
                            S.op("dve", lambda e, ss=ss: e.tensor_scalar(out=ss, in0=ss, scalar1=1.0 / 128, scalar2=EPS, op0=ALU.mult, op1=ALU.add),
                                 reads=[bss], writes=[bss])
                            S.op("act", lambda e, ss=ss: e.activation(out=ss, in_=ss, func=AF.Sqrt), reads=[bss], writes=[bss])
                            S.op("dve", lambda e, ss=ss: e.reciprocal(out=ss, in_=ss), reads=[bss], writes=[bss])
                            S.op("dve", lambda e, ss=ss: e.tensor_scalar(out=ss, in0=ss, scalar1=1.0 - lam_init, scalar2=None, op0=ALU.mult),
                                 reads=[bss], writes=[bss])
                            ob, bob = ob_r.next()
                            S.op("dve", lambda e, ob=ob, o=o, ss=ss: e.scalar_tensor_tensor(out=ob, in0=o, scalar=ss[:, 0:1], in1=subw,
                                                                                            op0=ALU.mult, op1=ALU.mult),
                                 reads=[bo, bss, k.bsmallp], writes=[bob])
                            t0 = qg * 512 + s * 128
                            S.dma("sp", k.mix_d[t0:t0 + 128, 512 + hh * 128:512 + (hh + 1) * 128], ob, reads=[bob])
        S.barrier()


def phase_b(k, l):
    S, nc = k.S, k.nc
    k.uid += 1
    with nc.sbuf_tensor("b_kT%d" % l, [128, S_LEN], BF16) as kT, nc.sbuf_tensor("b_qT%d" % l, [128, S_LEN], BF16) as qT, \
            nc.sbuf_tensor("b_v%d" % l, [128, NT, 65], BF16) as va, \
            nc.sbuf_tensor("b_mask%d" % l, [128, 17, 128], F32) as mask, \
            nc.sbuf_tensor("b_arena%d" % l, [128, 2000], F32) as arena, nc.sbuf_tensor("b_arenab%d" % l, [128, 4000], BF16) as arenab:
        off = [0]

        offb = [0]

        def carve(n_f32, dt=F32):
            if dt == BF16:
                a = arenab[:, offb[0]:offb[0] + 2 * n_f32]
                offb[0] += 2 * n_f32
                return a
            a = arena[:, off[0]:off[0] + n_f32]
            off[0] += n_f32
            return a

        bkT, bqT, bva, bmask = Buf(), Buf(), Buf(), Buf()
        S.dma("sp", mask[:], k.mask_d, writes=[bmask])
        pe_r = Rot([(carve(512), Buf()) for _ in range(3)])
        p_r = Rot([(carve(256, BF16), Buf()) for _ in range(4)])
        rc_r = Rot([(carve(1), Buf()) for _ in range(2)])
        ob_r = Rot([(carve(32, BF16), Buf()) for _ in range(2)])
        ps_r = Rot([(k.pb[i], k.bpb[i]) for i in (0, 1, 2, 5)])
        acc_r = Rot([(k.pb[i], k.bpb[i]) for i in (3, 4, 6)])
        for hp in range(2):
            S.dma("sp", qT[:], k.qkT_b_d[hp * 128:(hp + 1) * 128, :], writes=[bqT])
            S.dma("sp", kT[:], k.qkT_b_d[256 + hp * 128:256 + (hp + 1) * 128, :], writes=[bkT])
            for hl in range(2):
                hh = hp * 2 + hl
                rs = slice(hl * 64, (hl + 1) * 64)
                S.dma("sp", va[:], k.v_b_d.rearrange("(t p) (h d) -> p t h d", p=128, h=4)[:, :, hh, :], writes=[bva])
                items = []
                for qb in range(NT):
                    lo, hi = max(0, qb - 8), min(NT - 1, qb + 8)
                    kbs = list(range(lo, hi + 1))
                    for b0 in range(0, len(kbs), 4):
                        items.append((qb, lo, hi, kbs[b0:b0 + 4]))
                pend = []
                cur_acc = {}
                for it in range(len(items) + 2):
                    if it < len(items):
                        qb, lo, hi, grp = items[it]
                        n = len(grp)
                        ps, bps = ps_r.next()
                        for i, kb in enumerate(grp):
                            S.op("pe", lambda e: e.matmul(ps[:, i * 128:(i + 1) * 128], lhsT=kT[rs, kb * 128:(kb + 1) * 128],
                                                          rhs=qT[rs, qb * 128:(qb + 1) * 128], start=True, stop=True),
                                 reads=[bkT, bqT], writes=[bps])
                        pe_, bpe = pe_r.next()
                        S.op("act", lambda e: e.activation(out=pe_[:, 0:n * 128], in_=ps[:, 0:n * 128], func=AF.Exp, scale=0.125),
                             reads=[bps], writes=[bpe])
                        p, bp = p_r.next()
                        d0 = grp[0] - qb + 8
                        S.op("dve", lambda e: e.tensor_tensor(out=p[:, 0:n * 128], in0=pe_[:, 0:n * 128],
                                                              in1=mask[:, d0:d0 + n, :].rearrange("p a b -> p (a b)"), op=ALU.mult),
                             reads=[bpe, bmask], writes=[bp])
                        pend.append((items[it], p, bp))
                    if it >= 2:
                        (qb, lo, hi, grp), p, bp = pend.pop(0)
                        if grp[0] == lo:
                            cur_acc[qb] = acc_r.next()
                        a, ba = cur_acc[qb]
                        for i, kb in enumerate(grp):
                            S.op("pe", lambda e: e.matmul(a[:, 0:65], lhsT=p[:, i * 128:(i + 1) * 128], rhs=va[:, kb, :],
                                                          start=(kb == lo), stop=(kb == hi)),
                                 reads=[bp, bva], writes=[ba])
                        if grp[-1] == hi:
                            rc, brc = rc_r.next()
                            S.op("dve", lambda e: e.reciprocal(out=rc, in_=a[:, 64:65]), reads=[ba], writes=[brc])
                            ob, bob = ob_r.next()
                            S.op("dve", lambda e: e.tensor_scalar(out=ob, in0=a[:, 0:64], scalar1=rc[:, 0:1], scalar2=None, op0=ALU.mult),
                                 reads=[ba, brc], writes=[bob])
                            S.dma("sp", k.mix_d[qb * 128:(qb + 1) * 128, 256 + hh * 64:256 + (hh + 1) * 64], ob, reads=[bob])
        S.barrier()


def phase_d1(k, l):
    S, nc = k.S, k.nc
    k.uid += 1
    src = k.x_d if l == 0 else k.xres_d
    with nc.sbuf_tensor("d1_w%d" % l, [128, 8, D], BF16) as wo, nc.sbuf_tensor("d1_arena%d" % l, [128, 5000], F32) as arena, nc.sbuf_tensor("d1_arenab%d" % l, [128, 5000], BF16) as arenab:
        off = [0]

        offb = [0]

        def carve(n_f32, dt=F32):
            if dt == BF16:
                a = arenab[:, offb[0]:offb[0] + 2 * n_f32]
                offb[0] += 2 * n_f32
                return a
            a = arena[:, off[0]:off[0] + n_f32]
            off[0] += n_f32
            return a

        bwo = Buf()
        S.dma("pool", wo[:], k.w_out_d[l].rearrange("(k p) n -> p k n", p=128), writes=[bwo])
        xt_r = Rot([(carve(1024), Buf()) for _ in range(2)])
        m_r = Rot([(carve(512, BF16), Buf()) for _ in range(2)])
        mT_r = Rot([(carve(512, BF16), Buf()) for _ in range(2)])
        xo_r = Rot([(carve(1024), Buf()) for _ in range(2)])
        ptr_r = Rot([(k.pb[0], k.bpb[0]), (k.pb[1], k.bpb[1])])
        po_r = Rot([(k.pb[i], k.bpb[i]) for i in (2, 3, 4, 5)])
        for t in range(NT):
            t0 = t * 128
            xt, bx = xt_r.next()
            m, bm = m_r.next()
            S.dma("sp", xt, src[t0:t0 + 128, :], writes=[bx])
            S.dma("sp", m, k.mix_d[t0:t0 + 128, :], writes=[bm])
            pt, bpt = ptr_r.next()
            ptb = pt[:].bitcast(BF16).rearrange("p (k n) -> p k n", k=8)
            for kk in range(8):
                S.op("pe", lambda e, kk=kk, ptb=ptb, m=m: e.transpose(out=ptb[:, kk, :], in_=m[:, kk * 128:(kk + 1) * 128], identity=k.identb[:]),
                     reads=[bm, k.bidentb], writes=[bpt])
            mT, bmT = mT_r.next()
            mT3 = mT.rearrange("p (k n) -> p k n", k=8)
            S.op("act", lambda e, mT3=mT3, ptb=ptb: e.copy(out=mT3, in_=ptb), reads=[bpt], writes=[bmT])
            xo, bxo = xo_r.next()
            for half in range(2):
                po, bpo = po_r.next()
                for kk in range(8):
                    S.op("pe", lambda e, kk=kk, po=po, mT3=mT3, half=half: e.matmul(po[:], lhsT=mT3[:, kk, :], rhs=wo[:, kk, half * 512:(half + 1) * 512],
                                                                                   start=(kk == 0), stop=(kk == 7)),
                         reads=[bmT, bwo], writes=[bpo])
                S.op("dve", lambda e, xo=xo, po=po, xt=xt, half=half: e.tensor_tensor(out=xo[:, half * 512:(half + 1) * 512], in0=po[:],
                                                                                     in1=xt[:, half * 512:(half + 1) * 512], op=ALU.add),
                     reads=[bpo, bx], writes=[bxo])
            S.dma("sp", k.xres_d[t0:t0 + 128, :], xo, reads=[bxo])
        S.barrier()


def phase_d2(k, l):
    S, nc = k.S, k.nc
    k.uid += 1
    last = (l == L - 1)
    NFF = DFF // 128
    with nc.sbuf_tensor("d2_wg%d" % l, [128, 8, DFF], BF16) as wg, nc.sbuf_tensor("d2_wu%d" % l, [128, 8, DFF], BF16) as wu, \
            nc.sbuf_tensor("d2_wd%d" % l, [128, NFF, D], BF16) as wd, nc.sbuf_tensor("d2_nwb%d" % l, [128, D], F32) as nwb, \
            nc.sbuf_tensor("d2_fnw%d" % l, [128, D], F32) as fnw, \
            nc.sbuf_tensor("d2_arena%d" % l, [128, 7700], F32) as arena, nc.sbuf_tensor("d2_arenab%d" % l, [128, 11800], BF16) as arenab:
        off = [0]

        offb = [0]

        def carve(n_f32, dt=F32):
            if dt == BF16:
                a = arenab[:, offb[0]:offb[0] + 2 * n_f32]
                offb[0] += 2 * n_f32
                return a
            a = arena[:, off[0]:off[0] + n_f32]
            off[0] += n_f32
            return a

        bwg = [Buf() for _ in range(8)]
        bwu = [Buf() for _ in range(8)]
        bwd, bnwb, bfnw = Buf(), Buf(), Buf()
        for kk in range(8):
            S.dma("pool", wg[:, kk, :], k.w_gate_d[l, kk * 128:(kk + 1) * 128, :], writes=[bwg[kk]])
            S.dma("pool", wu[:, kk, :], k.w_up_d[l, kk * 128:(kk + 1) * 128, :], writes=[bwu[kk]])
        S.dma("pool", wd[:], k.w_down_d[l].rearrange("(k p) n -> p k n", p=128), writes=[bwd])
        S.dma("sp", nwb[:], k.ffn_nw_d[l:l + 1, :].broadcast_to([128, D]), writes=[bnwb])
        if last:
            S.dma("sp", fnw[:], k.final_nw_d.rearrange("(o n) -> o n", o=1).broadcast_to([128, D]), writes=[bfnw])
        TG = 256
        xt_r = Rot([(carve(1024), Buf()) for _ in range(3)])
        junk_r = Rot([(carve(1024), Buf()) for _ in range(1)])
        h_r = Rot([(carve(512, BF16), Buf()) for _ in range(2)])
        ss_r = Rot([(carve(1), Buf()) for _ in range(2)])
        hT_r = Rot([(carve(1024, BF16), Buf()) for _ in range(2)])
        aT_r = Rot([(carve(NFF * TG // 2, BF16), Buf()) for _ in range(1)])
        sg_r = Rot([(carve(TG), Buf()) for _ in range(2)])
        xo_r = Rot([(carve(1024), Buf()) for _ in range(2)])
        yo_r = Rot([(carve(1024), Buf()) for _ in range(1)])
        ptr_r = Rot([(k.pb[0], k.bpb[0]), (k.pb[1], k.bpb[1])])
        pg_r = Rot([(k.pb[i], k.bpb[i]) for i in (2, 3)])
        pu_r = Rot([(k.pb[i], k.bpb[i]) for i in (4, 5)])
        po_r = Rot([(k.pb[i], k.bpb[i]) for i in (6, 7)])
        for g in range(S_LEN // TG):
            hT, bhT = hT_r.next()
            hT3 = hT.rearrange("p (k n) -> p k n", k=8)
            xts = []
            for s in range(TG // 128):
                t0 = g * TG + s * 128
                xt, bx = xt_r.next()
                xts.append((xt, bx))
                h, bh = h_r.next()
                S.dma("sp", xt, k.xres_d[t0:t0 + 128, :], writes=[bx])
                norm_tile(k, xt, bx, nwb, bnwb, h, bh, ss_r, junk_r)
                pt, bpt = ptr_r.next()
                ptb = pt[:].bitcast(BF16).rearrange("p (k n) -> p k n", k=8)
                for kk in range(8):
                    S.op("pe", lambda e, kk=kk, ptb=ptb, h=h: e.transpose(out=ptb[:, kk, :], in_=h[:, kk * 128:(kk + 1) * 128], identity=k.identb[:]),
                         reads=[bh, k.bidentb], writes=[bpt])
                S.op("dve", lambda e, ptb=ptb, hT3=hT3, s=s: e.tensor_copy(out=hT3[:, :, s * 128:(s + 1) * 128], in_=ptb),
                     reads=[bpt], writes=[bhT])
            aT, baT = aT_r.next()
            aT3 = aT.rearrange("p (f n) -> p f n", f=NFF)
            for f in range(NFF):
                pg, bpg = pg_r.next()
                pu, bpu = pu_r.next()
                for kk in range(8):
                    S.op("pe", lambda e, kk=kk, pg=pg, f=f: e.matmul(pg[:, 0:TG], lhsT=wg[:, kk, f * 128:(f + 1) * 128], rhs=hT3[:, kk, :],
                                                                    start=(kk == 0), stop=(kk == 7)),
                         reads=[bwg[kk], bhT], writes=[bpg])
                for kk in range(8):
                    S.op("pe", lambda e, kk=kk, pu=pu, f=f: e.matmul(pu[:, 0:TG], lhsT=wu[:, kk, f * 128:(f + 1) * 128], rhs=hT3[:, kk, :],
                                                                    start=(kk == 0), stop=(kk == 7)),
                         reads=[bwu[kk], bhT], writes=[bpu])
                sg, bsg = sg_r.next()
                S.op("act", lambda e, sg=sg, pg=pg: e.activation(out=sg, in_=pg[:, 0:TG], func=AF.Silu), reads=[bpg], writes=[bsg])
                S.op("dve", lambda e, sg=sg, pu=pu, f=f, aT3=aT3: e.tensor_tensor(out=aT3[:, f, :], in0=pu[:, 0:TG], in1=sg, op=ALU.mult),
                     reads=[bpu, bsg], writes=[baT])
            for s in range(TG // 128):
                t0 = g * TG + s * 128
                xt, bx = xts[s]
                xo, bxo = xo_r.next()
                for half in range(2):
                    po, bpo = po_r.next()
                    for f in range(NFF):
                        S.op("pe", lambda e, f=f, po=po, s=s, half=half, aT3=aT3: e.matmul(po[:], lhsT=aT3[:, f, s * 128:(s + 1) * 128],
                                                                                         rhs=wd[:, f, half * 512:(half + 1) * 512],
                                                                                         start=(f == 0), stop=(f == NFF - 1)),
                             reads=[baT, bwd], writes=[bpo])
                    S.op("dve", lambda e, xo=xo, po=po, xt=xt, half=half: e.tensor_tensor(out=xo[:, half * 512:(half + 1) * 512], in0=po[:],
                                                                                         in1=xt[:, half * 512:(half + 1) * 512], op=ALU.add),
                         reads=[bpo, bx], writes=[bxo])
                if not last:
                    S.dma("sp", k.xres_d[t0:t0 + 128, :], xo, reads=[bxo])
                else:
                    yo, byo = yo_r.next()
                    norm_tile(k, xo, bxo, fnw, bfnw, yo, byo, ss_r, junk_r)
                    S.dma("sp", k.out_d[t0:t0 + 128, :], yo, reads=[byo])
        S.barrier()


def build(dbg=False, phases=None):
    nc = bass.Bass("TRN2", target_bir_lowering=False)
    k = K()
    k.nc = nc
    k.uid = 0
    k.S = Sched(nc)
    ext = "ExternalInput"
    k.x_d = nc.dram_tensor("x", [S_LEN, D], F32, kind=ext).ap()
    k.pos_d = nc.dram_tensor("pos", [S_LEN], I32, kind=ext).ap()
    k.w_in_d = nc.dram_tensor("w_in", [L, D, NCOL], F32, kind=ext).ap()
    k.attn_nw_d = nc.dram_tensor("attn_nw", [L, D], F32, kind=ext).ap()
    k.w_out_d = nc.dram_tensor("w_out", [L, D, D], F32, kind=ext).ap()
    k.ffn_nw_d = nc.dram_tensor("ffn_nw", [L, D], F32, kind=ext).ap()
    k.w_gate_d = nc.dram_tensor("w_gate", [L, D, DFF], F32, kind=ext).ap()
    k.w_up_d = nc.dram_tensor("w_up", [L, D, DFF], F32, kind=ext).ap()
    k.w_down_d = nc.dram_tensor("w_down", [L, DFF, D], F32, kind=ext).ap()
    k.final_nw_d = nc.dram_tensor("final_nw", [D], F32, kind=ext).ap()
    k.smallp_d = nc.dram_tensor("smallp", [128, 1024], F32, kind=ext).ap()
    k.cst_d = nc.dram_tensor("cst", [128, 4], F32, kind=ext).ap()
    k.mask_d = nc.dram_tensor("maskb", [128, 17, 128], F32, kind=ext).ap()
    k.convw_d = nc.dram_tensor("convw", [L, 128, 30], F32, kind=ext).ap()
    k.tri_d = nc.dram_tensor("tri", [128, 4, 128], F32, kind=ext).ap()
    k.out_d = nc.dram_tensor("out", [S_LEN, D], F32, kind="ExternalOutput").ap()
    sk = "ExternalOutput" if dbg else "Internal"
    k.cosT_d = nc.dram_tensor("cosT", [128, S_LEN], F32, kind=sk).ap()
    k.sinT_d = nc.dram_tensor("sinT", [128, S_LEN], F32, kind=sk).ap()
    k.qkvT_a_d = nc.dram_tensor("qkvT_a", [768, S_LEN], F32, kind=sk).ap()
    k.za_d = nc.dram_tensor("za", [S_LEN, 272], F32, kind=sk).ap()
    k.qkT_b_d = nc.dram_tensor("qkT_b", [512, S_LEN], BF16, kind=sk).ap()
    k.qkT_c_d = nc.dram_tensor("qkT_c", [1024, S_LEN], BF16, kind=sk).ap()
    k.v_b_d = nc.dram_tensor("v_b", [S_LEN, 4 * 65], BF16, kind=sk).ap()
    k.v_c_d = nc.dram_tensor("v_c", [S_LEN, 4 * 129], BF16, kind=sk).ap()
    k.mix_d = nc.dram_tensor("mix", [S_LEN, D], BF16, kind=sk).ap()
    k.xres_d = nc.dram_tensor("xres", [S_LEN, D], F32, kind=sk).ap()
    k.identf = nc.alloc_sbuf_tensor("identf", [128, 128], F32)
    k.identb = nc.alloc_sbuf_tensor("identb", [128, 128], BF16)
    k.cst = nc.alloc_sbuf_tensor("cst_sb", [128, 4], F32)
    k.smallp = nc.alloc_sbuf_tensor("smallp_sb", [128, 1024], F32)
    k.bidentf, k.bidentb, k.bcst, k.bsmallp = Buf(), Buf(), Buf(), Buf()
    k.pb = [nc.alloc_psum_tensor("pb%d" % i, [128, 512], F32) for i in range(8)]
    k.bpb = [Buf("pb%d" % i, excl=True) for i in range(8)]
    k.S.dma("sp", k.smallp[:], k.smallp_d, writes=[k.bsmallp])
    if phases is None:
        phases = ["setup"] + [p + str(l) for l in range(L) for p in ("a", "c", "b", "g", "d1", "d2")]
    for ph in phases:
        if ph == "setup":
            phase_setup(k)
        elif ph[0] == "a":
            phase_a(k, int(ph[1:]))
        elif ph[0] == "c":
            phase_c(k, int(ph[1:]))
        elif ph[0] == "b":
            phase_b(k, int(ph[1:]))
        elif ph[0] == "g":
            phase_g(k, int(ph[1:]))
        elif ph[0] == "z":
            phase_z(k)
        elif ph[:2] == "d1":
            phase_d1(k, int(ph[2:]))
        elif ph[:2] == "d2":
            phase_d2(k, int(ph[2:]))
    k.S.finish()
    k.ninstr = k.S.ninstr
    return nc, k


def phase_z(k):
    S, nc = k.S, k.nc
    with nc.sbuf_tensor("z_t", [128, 256], BF16) as zt:
        bz = Buf()
        S.op("dve", lambda e: e.memset(zt[:], 0.0), writes=[bz])
        for t in range(NT):
            S.dma("sp", k.mix_d[t * 128:(t + 1) * 128, 0:256], zt[:], reads=[bz])
        S.barrier()


def phase_g(k, l):
    S, nc = k.S, k.nc
    k.uid += 1
    sp_ = k.smallp
    with nc.sbuf_tensor("g_qkvn%d" % l, [128, 6, S_LEN], F32) as qkvn, \
            nc.sbuf_tensor("g_oall%d" % l, [128, NT, 256], F32) as oall, \
            nc.sbuf_tensor("g_gate%d" % l, [128, NT, 16], F32) as gab, \
            nc.sbuf_tensor("g_g%d" % l, [128, NT, 8], F32) as gg, \
            nc.sbuf_tensor("g_beta%d" % l, [128, NT, 8], F32) as beta, \
            nc.sbuf_tensor("g_cw%d" % l, [128, 30], F32) as cw, \
            nc.sbuf_tensor("g_tri%d" % l, [128, 4, 128], F32) as tri, \
            nc.sbuf_tensor("g_bd%d" % l, [128, 128], F32) as bd, \
            nc.sbuf_tensor("g_ones%d" % l, [128, 128], F32) as onesf, \
            nc.sbuf_tensor("g_st%d" % l, [128, 2, 2, 128], F32) as st, \
            nc.sbuf_tensor("g_obt%d" % l, [128, 2, 256], BF16) as obt, \
            nc.sbuf_tensor("g_arenab%d" % l, [128, 16], BF16) as garenab, \
            nc.sbuf_tensor("g_arena%d" % l, [128, 16100], F32) as arena:
        off = [0]
        offb = [0]

        def carveb(n):
            a = garenab[:, offb[0]:offb[0] + n]
            offb[0] += n
            assert offb[0] <= 5120
            return a

        def carve(n):
            a = arena[:, off[0]:off[0] + n]
            off[0] += n
            assert off[0] <= 16100, off[0]
            return a

        bq = [Buf() for _ in range(6)]
        boall, bgab, bgg, bbeta, bcw, btri, bbd, bones = Buf(), Buf(), Buf(), Buf(), Buf(), Buf(), Buf(), Buf()
        bst = [Buf(), Buf()]
        S.dma("sp", cw[:], k.convw_d[l], writes=[bcw])
        S.dma("sp", tri[:], k.tri_d, writes=[btri])
        for t in range(NT):
            S.dma("sp", gab[:, t, :], k.za_d[t * 128:(t + 1) * 128, 256:272], writes=[bgab])
        S.op("dve", lambda e: e.memset(bd[:], 0.0), writes=[bbd])
        S.op("dve", lambda e: e.memset(bd[0:64, 0:64], 1.0), writes=[bbd])
        S.op("dve", lambda e: e.memset(bd[64:128, 64:128], 1.0), writes=[bbd])
        S.op("dve", lambda e: e.memset(onesf[:], 1.0), writes=[bones])
        S.op("dve", lambda e: e.memset(oall[:], 0.0), writes=[boall])
        S.op("dve", lambda e: e.memset(st[:], 0.0), writes=bst)
        ab = sp_[:, 896 + 8 * l:904 + 8 * l]
        db = sp_[:, 912 + 8 * l:920 + 8 * l]
        eal, beal = carve(8), Buf()
        S.op("act", lambda e: e.activation(out=eal, in_=ab, func=AF.Exp), reads=[k.bsmallp], writes=[beal])
        S.op("dve", lambda e: e.tensor_tensor(out=gg[:], in0=gab[:, :, 0:8], in1=db.rearrange("p (o c) -> p o c", o=1).broadcast_to([128, NT, 8]), op=ALU.add),
             reads=[bgab, k.bsmallp], writes=[bgg])
        S.op("act", lambda e: e.activation(out=gg[:], in_=gg[:], func=AF.Exp), reads=[bgg], writes=[bgg])
        S.op("act", lambda e: e.activation(out=gg[:], in_=gg[:], func=AF.Ln, bias=1.0), reads=[bgg], writes=[bgg])
        S.op("dve", lambda e: e.scalar_tensor_tensor(out=gg[:], in0=gg[:], scalar=-1.0, in1=eal.rearrange("p (o c) -> p o c", o=1).broadcast_to([128, NT, 8]),
                                                     op0=ALU.mult, op1=ALU.mult), reads=[bgg, beal], writes=[bgg])
        S.op("act", lambda e: e.activation(out=beta[:], in_=gab[:, :, 8:16], func=AF.Sigmoid), reads=[bgab], writes=[bbeta])
        xin, bxin = carve(S_LEN + 4), Buf()
        rs_r = Rot([(carve(512), Buf()) for _ in range(2)])
        S.op("dve", lambda e: e.memset(xin[:, 0:2], 0.0), writes=[bxin])
        S.op("dve", lambda e: e.memset(xin[:, S_LEN + 2:S_LEN + 4], 0.0), writes=[bxin])
        for ch in range(6):
            S.dma("sp", xin[:, 2:S_LEN + 2], k.qkvT_a_d[ch * 128:(ch + 1) * 128, :], writes=[bxin])
            qc = qkvn[:, ch, :]
            S.op("dve", lambda e: e.tensor_scalar(out=qc, in0=xin[:, 0:S_LEN], scalar1=cw[:, ch * 5:ch * 5 + 1], scalar2=None, op0=ALU.mult),
                 reads=[bxin, bcw], writes=[bq[ch]])
            for j in range(1, 5):
                S.op("dve", lambda e: e.scalar_tensor_tensor(out=qc, in0=xin[:, j:j + S_LEN], scalar=cw[:, ch * 5 + j:ch * 5 + j + 1], in1=qc,
                                                             op0=ALU.mult, op1=ALU.add), reads=[bxin, bcw, bq[ch]], writes=[bq[ch]])
            S.op("act", lambda e: e.activation(out=qc, in_=qc, func=AF.Silu), reads=[bq[ch]], writes=[bq[ch]])
            if ch < 4:
                S.op("act", lambda e: e.activation(out=xin[:, 2:S_LEN + 2], in_=qc, func=AF.Square), reads=[bq[ch]], writes=[bxin])
                for c8 in range(8):
                    cs = slice(c8 * 512, (c8 + 1) * 512)
                    pp, bpp = k.pb[c8 % 2], k.bpb[c8 % 2]
                    S.op("pe", lambda e: e.matmul(pp[:], lhsT=bd[:], rhs=xin[:, 2 + c8 * 512:2 + (c8 + 1) * 512], start=True, stop=True),
                         reads=[bbd, bxin], writes=[bpp])
                    rs, brs = rs_r.next()
                    S.op("dve", lambda e: e.tensor_scalar(out=rs, in0=pp[:], scalar1=1e-6, scalar2=None, op0=ALU.add), reads=[bpp], writes=[brs])
                    S.op("act", lambda e: e.activation(out=rs, in_=rs, func=AF.Sqrt, scale=(64.0 if ch < 2 else 1.0)), reads=[brs], writes=[brs])
                    S.op("dve", lambda e: e.reciprocal(out=rs, in_=rs), reads=[brs], writes=[brs])
                    S.op("dve", lambda e: e.tensor_tensor(out=qkvn[:, ch, cs], in0=qkvn[:, ch, cs], in1=rs, op=ALU.mult), reads=[bq[ch], brs], writes=[bq[ch]])
        S.barrier()
        off[0] = 8

        def rot(n, cnt):
            return Rot([(carve(n), Buf()) for _ in range(cnt)])
        gc_r = rot(8, 4)
        eg_r = rot(12, 4)
        bge_r = rot(4, 4)
        gq_r = rot(8, 4)
        ktok_r, vtok_r = rot(256, 1), rot(256, 1)
        vb_r, kbg_r, kdec_r = rot(256, 2), rot(256, 2), rot(256, 4)
        expg_r, d1_r, d2_r = rot(512, 1), rot(512, 2), rot(512, 2)
        intra_r = rot(512, 4)
        xa_r, xb_r, p_r = rot(512, 4), rot(512, 4), rot(512, 2)
        u_r, kcT_r, qgT_r = rot(256, 4), rot(256, 4), rot(256, 4)
        vnew_r = rot(256, 2)
        bank = lambda i: (k.pb[i], k.bpb[i])
        QH = (0, 2, 1, 3)
        v4 = lambda a: a.rearrange("p (q i) -> p q i", q=4)
        identq = k.identf[:].rearrange("p (o i) -> p o i", o=1).broadcast_to([128, 4, 128])

        def unit(t, dr):
            ts = slice(t * 128, (t + 1) * 128)
            MI, MSA = tri[:, dr, :], tri[:, 3 - dr, :]
            MIq = MI.rearrange("p (o i) -> p o i", o=1).broadcast_to([128, 4, 128])
            MSAq = MSA.rearrange("p (o i) -> p o i", o=1).broadcast_to([128, 4, 128])
            g_t = gg[:, t, dr * 4:(dr + 1) * 4]
            be_t = beta[:, t, dr * 4:(dr + 1) * 4]
            BA, BB, BC = ((5, 1, 2), (6, 3, 4))[dr]
            pn, bpn = bank(BA)
            pnb, bpnb = bank(BB)
            pnc, bpnc = bank(BC)
            p0, bp0 = bank(BC)
            S.op("pe", lambda e: e.matmul(p0[:, 0:4], lhsT=MI, rhs=g_t, start=True, stop=True), reads=[btri, bgg], writes=[bp0])
            S.op("pe", lambda e: e.matmul(p0[:, 4:8], lhsT=onesf[:], rhs=g_t, start=True, stop=True), reads=[bones, bgg], writes=[bp0])
            gc, bgc = gc_r.next()
            S.op("act", lambda e: e.copy(out=gc, in_=p0[:, 0:8]), reads=[bp0], writes=[bgc])
            eg, beg = eg_r.next()
            S.op("dve", lambda e: e.tensor_tensor(out=eg[:, 4:8], in0=gc[:, 4:8], in1=gc[:, 0:4], op=ALU.subtract), reads=[bgc], writes=[beg])
            S.op("act", lambda e: e.activation(out=eg[:, 0:4], in_=gc[:, 0:4], func=AF.Exp), reads=[bgc], writes=[beg])
            S.op("act", lambda e: e.activation(out=eg[:, 4:8], in_=eg[:, 4:8], func=AF.Exp), reads=[beg], writes=[beg])
            S.op("act", lambda e: e.activation(out=eg[:, 8:12], in_=gc[:, 4:8], func=AF.Exp), reads=[bgc], writes=[beg])
            bge, bbge = bge_r.next()
            S.op("dve", lambda e: e.tensor_tensor(out=bge, in0=be_t, in1=eg[:, 0:4], op=ALU.mult), reads=[bbeta, beg], writes=[bbge])
            gq, bgq = gq_r.next()
            perm = lambda a: a.rearrange("p (pr hl) -> p hl pr", pr=2)
            S.op("dve", lambda e: e.tensor_copy(out=gq[:, 0:4].rearrange("p (hl pr) -> p hl pr", hl=2), in_=perm(gc[:, 0:4])), reads=[bgc], writes=[bgq])
            S.op("dve", lambda e: e.tensor_copy(out=gq[:, 4:8].rearrange("p (hl pr) -> p hl pr", hl=2), in_=perm(be_t)), reads=[bbeta, bgq], writes=[bgq])
            bq_ = lambda a: a.rearrange("p (q o) -> p q o", o=1).broadcast_to([128, 4, 128])
            p1, bp1 = bank(BB)
            for i, ch in enumerate((2, 3, 4, 5)):
                S.op("pe", lambda e: e.transpose(out=p1[:, i * 128:(i + 1) * 128], in_=qkvn[:, ch, ts], identity=k.identf[:]),
                     reads=[bq[ch], k.bidentf], writes=[bp1])
            ktok, bktok = ktok_r.next()
            vtok, bvtok = vtok_r.next()
            S.op("act", lambda e: e.copy(out=ktok, in_=p1[:, 0:256]), reads=[bp1], writes=[bktok])
            S.op("dve", lambda e: e.tensor_copy(out=vtok, in_=p1[:, 256:512]), reads=[bp1], writes=[bvtok])
            v3 = lambda a: a.rearrange("p (h d) -> p h d", h=4)
            bc = lambda a: a.rearrange("p (h o) -> p h o", o=1).broadcast_to([128, 4, 64])
            vb, bvb = vb_r.next()
            kbg, bkbg = kbg_r.next()
            kdec, bkdec = kdec_r.next()
            S.op("dve", lambda e: e.tensor_tensor(out=v3(vb), in0=v3(vtok), in1=bc(be_t), op=ALU.mult), reads=[bvtok, bbeta], writes=[bvb])
            S.op("dve", lambda e: e.tensor_tensor(out=v3(kbg), in0=v3(ktok), in1=bc(bge), op=ALU.mult), reads=[bktok, bbge], writes=[bkbg])
            S.op("dve", lambda e: e.tensor_tensor(out=v3(kdec), in0=v3(ktok), in1=bc(eg[:, 4:8]), op=ALU.mult), reads=[bktok, beg], writes=[bkdec])
            yield
            d1, bd1 = d1_r.next()
            d2, bd2 = d2_r.next()
            dg, bdg = d2, bd2
            df, bdf = d1, bd1
            S.op("dve", lambda e: e.tensor_tensor(out=v4(dg), in0=identq, in1=bq_(gq[:, 0:4]), op=ALU.mult), reads=[k.bidentf, bgq], writes=[bdg])
            p2, bp2 = bank(BC)
            S.op("pe", lambda e: e.matmul(p2[:], lhsT=onesf[:], rhs=dg, start=True, stop=True), reads=[bones, bdg], writes=[bp2])
            expg, bexpg = expg_r.next()
            S.op("dve", lambda e: e.tensor_tensor(out=v4(df), in0=v4(p2[:]), in1=bq_(gq[:, 0:4]), op=ALU.subtract), reads=[bp2, bgq], writes=[bdf])
            S.op("act", lambda e: e.activation(out=expg, in_=p2[:], func=AF.Exp), reads=[bp2], writes=[bexpg])
            S.op("dve", lambda e: e.tensor_scalar(out=d2, in0=df, scalar1=0.0, scalar2=None, op0=ALU.max), reads=[bdf], writes=[bd2])
            S.op("dve", lambda e: e.tensor_scalar(out=d1, in0=df, scalar1=0.0, scalar2=None, op0=ALU.min), reads=[bdf], writes=[bd1])
            S.op("act", lambda e: e.activation(out=d1, in_=d1, func=AF.Exp), reads=[bd1], writes=[bd1])
            S.op("act", lambda e: e.activation(out=d2, in_=d2, func=AF.Exp, scale=-1.0), reads=[bd2], writes=[bd2])
            S.op("dve", lambda e: e.tensor_tensor(out=v4(d1), in0=v4(d1), in1=MIq, op=ALU.mult), reads=[bd1, btri], writes=[bd1])
            S.op("dve", lambda e: e.tensor_tensor(out=v4(d2), in0=v4(d2), in1=MSAq, op=ALU.mult), reads=[bd2, btri], writes=[bd2])
            S.op("dve", lambda e: e.tensor_tensor(out=v4(d2), in0=v4(d2), in1=bq_(gq[:, 4:8]), op=ALU.mult), reads=[bd2, bgq], writes=[bd2])
            qgT, bqgT = qgT_r.next()
            for hl in range(2):
                rs = slice(hl * 64, hl * 64 + 64)
                qs = slice(hl * 256, (hl + 1) * 256)
                S.op("dve", lambda e: e.tensor_tensor(out=qgT[rs, :].rearrange("p (pr i) -> p pr i", pr=2), in0=qkvn[rs, 0:2, ts],
                                                      in1=expg[rs, qs].rearrange("p (pr i) -> p pr i", pr=2), op=ALU.mult),
                     reads=[bq[0], bq[1], bexpg], writes=[bqgT])
            yield
            xa, bxa = xa_r.next()
            intra, bintra = intra_r.next()
            for hl in range(2):
                pk_, bpk = bank((BA, BB)[hl])
                rs = slice(hl * 64, hl * 64 + 64)
                for pr in range(2):
                    kT_h = qkvn[rs, 2 + pr, ts]
                    qT_h = qkvn[rs, pr, ts]
                    S.op("pe", lambda e: e.matmul(pk_[:, pr * 128:(pr + 1) * 128], lhsT=kT_h, rhs=kT_h, start=True, stop=True),
                         reads=[bq[2 + pr]], writes=[bpk])
                    S.op("pe", lambda e: e.matmul(pk_[:, 256 + pr * 128:256 + (pr + 1) * 128], lhsT=kT_h, rhs=qT_h, start=True, stop=True),
                         reads=[bq[2 + pr], bq[pr]], writes=[bpk])
                qs = slice(hl * 256, (hl + 1) * 256)
                S.op("dve", lambda e: e.tensor_tensor(out=xa[:, qs], in0=pk_[:, 0:256], in1=d2[:, qs], op=ALU.mult), reads=[bpk, bd2], writes=[bxa])
                S.op("dve", lambda e: e.tensor_tensor(out=intra[:, qs], in0=pk_[:, 256:512], in1=d1[:, qs], op=ALU.mult), reads=[bpk, bd1], writes=[bintra])
            yield
            for q in range(4):
                S.op("pe", lambda e: e.transpose(out=pnc[:, q * 128:(q + 1) * 128], in_=xa[:, q * 128:(q + 1) * 128], identity=k.identf[:]),
                     reads=[bxa, k.bidentf], writes=[bpnc])
            xb, bxb = xb_r.next()
            S.op("act", lambda e: e.copy(out=xb, in_=pnc[:]), reads=[bpnc], writes=[bxb])
            P, bP = p_r.next()
            S.op("dve", lambda e: e.tensor_tensor(out=v4(P), in0=identq, in1=v4(xb), op=ALU.subtract), reads=[k.bidentf, bxb], writes=[bP])
            yield
            for it in range(6):
                for q in range(4):
                    qs = slice(q * 128, (q + 1) * 128)
                    S.op("pe", lambda e: e.matmul(pn[:, qs], lhsT=xb[:, qs], rhs=xa[:, qs], start=True, stop=True), reads=[bxb, bxa], writes=[bpn])
                xa2, bxa2 = xa_r.next()
                S.op("act", lambda e: e.copy(out=xa2, in_=pn[:]), reads=[bpn], writes=[bxa2])
                if it < 5:
                    for q in range(4):
                        qs = slice(q * 128, (q + 1) * 128)
                        S.op("pe", lambda e: e.matmul(pnb[:, qs], lhsT=xa[:, qs], rhs=xb[:, qs], start=True, stop=True), reads=[bxb, bxa], writes=[bpnb])
                    xb2, bxb2 = xb_r.next()
                    S.op("act", lambda e: e.copy(out=xb2, in_=pnb[:]), reads=[bpnb], writes=[bxb2])
                for q in range(4):
                    qs = slice(q * 128, (q + 1) * 128)
                    S.op("pe", lambda e: e.matmul(pnc[:, qs], lhsT=xa2[:, qs], rhs=P[:, qs], start=True, stop=True), reads=[bxa2, bP], writes=[bpnc])
                S.op("dve", lambda e: e.tensor_tensor(out=P, in0=pnc[:], in1=P, op=ALU.add), reads=[bpnc, bP], writes=[bP])
                yield
                xa, bxa = xa2, bxa2
                if it < 5:
                    xb, bxb = xb2, bxb2
            u, bu = u_r.next()
            kcT, bkcT = kcT_r.next()
            for q in range(4):
                h = QH[q]
                rs = slice((h % 2) * 64, (h % 2) * 64 + 64)
                qs = slice(q * 128, (q + 1) * 128)
                S.op("pe", lambda e: e.matmul(pn[:, h * 64:(h + 1) * 64], lhsT=P[:, qs], rhs=vb[:, h * 64:(h + 1) * 64], start=True, stop=True),
                     reads=[bP, bvb], writes=[bpn])
                S.op("pe", lambda e: e.matmul(pn[rs, 256 + (h // 2) * 128:256 + (h // 2) * 128 + 128], lhsT=kbg[:, h * 64:(h + 1) * 64], rhs=P[:, qs],
                                              start=True, stop=True), reads=[bP, bkbg], writes=[bpn])
            S.op("act", lambda e: e.copy(out=u, in_=pn[:, 0:256]), reads=[bpn], writes=[bu])
            S.op("dve", lambda e: e.tensor_copy(out=kcT, in_=pn[:, 256:512]), reads=[bpn], writes=[bkcT])
            intras = [None] * 4
            for q in range(4):
                intras[QH[q]] = (intra[:, q * 128:(q + 1) * 128], bintra)
            res_[(t, dr)] = dict(u=(u, bu), kcT=(kcT, bkcT), qgT=(qgT, bqgT), intras=intras, kdec=(kdec, bkdec), eg=(eg, beg))

        res_ = {}

        def step(t, dr, un):
            u, bu = un["u"]
            kcT, bkcT = un["kcT"]
            qgT, bqgT = un["qgT"]
            kdec, bkdec = un["kdec"]
            eg, beg = un["eg"]
            p7, bp7 = bank(7 if dr == 0 else 0)
            for pr in range(2):
                ps_ = slice(pr * 128, (pr + 1) * 128)
                S.op("pe", lambda e: e.matmul(p7[:, ps_], lhsT=kcT[:, ps_], rhs=st[:, dr, pr, :], start=True, stop=True),
                     reads=[bkcT, bst[dr]], writes=[bp7])
            yield
            vnew, bvnew = vnew_r.next()
            S.op("dve", lambda e: e.tensor_tensor(out=vnew, in0=u, in1=p7[:, 0:256], op=ALU.subtract), reads=[bu, bp7], writes=[bvnew])
            yield
            for pr in range(2):
                ps_ = slice(256 + pr * 128, 256 + (pr + 1) * 128)
                S.op("pe", lambda e: e.matmul(p7[:, ps_], lhsT=qgT[:, pr * 128:(pr + 1) * 128], rhs=st[:, dr, pr, :], start=True, stop=False),
                     reads=[bqgT, bst[dr]], writes=[bp7])
                for h in (2 * pr, 2 * pr + 1):
                    intra, bintra = un["intras"][h]
                    S.op("pe", lambda e: e.matmul(p7[:, 256 + h * 64:256 + (h + 1) * 64], lhsT=intra, rhs=vnew[:, h * 64:(h + 1) * 64],
                                                  start=False, stop=(h == 2 * pr + 1)), reads=[bintra, bvnew], writes=[bp7])
            p0, bp0 = p7, bp7
            for pr in range(2):
                ps_ = slice(pr * 128, (pr + 1) * 128)
                S.op("pe", lambda e: e.matmul(p0[:, pr * 128:(pr + 1) * 128], lhsT=kdec[:, ps_], rhs=vnew[:, ps_], start=True, stop=True),
                     reads=[bkdec, bvnew], writes=[bp0])
            yield
            S.op("dve", lambda e: e.tensor_tensor(out=oall[:, t, :], in0=p7[:, 256:512], in1=oall[:, t, :], op=ALU.add), reads=[bp7, boall], writes=[boall])
            for h in range(4):
                pr = h // 2
                rs = slice((h % 2) * 64, (h % 2) * 64 + 64)
                cs = slice((h % 2) * 64, (h % 2) * 64 + 64)
                sv = st[rs, dr, pr, cs]
                S.op("dve", lambda e: e.scalar_tensor_tensor(out=sv, in0=sv, scalar=eg[rs, 8 + h:9 + h],
                                                             in1=p0[rs, pr * 128 + (h % 2) * 64:pr * 128 + (h % 2) * 64 + 64],
                                                             op0=ALU.mult, op1=ALU.add), reads=[bst[dr], beg, bp0], writes=[bst[dr]])

        def run_rr(gens):
            alive = [True] * len(gens)
            while any(alive):
                for gi in range(len(gens)):
                    if alive[gi]:
                        try:
                            next(gens[gi])
                        except StopIteration:
                            alive[gi] = False

        pending = []
        for i in range(NT):
            run_rr([unit(i, 0), unit(NT - 1 - i, 1)] + pending)
            pending = [step(i, 0, res_.pop((i, 0))), step(NT - 1 - i, 1, res_.pop((NT - 1 - i, 1)))]
        run_rr(pending)
        S.barrier()
        off[0] = 8
        z_r = rot(256, 2)
        sq_r = rot(256, 2)
        r4_r = rot(4, 2)
        gnw = sp_[:, 768 + 64 * l:768 + 64 * (l + 1)]
        ob_r = Rot([(obt[:, i, :], Buf()) for i in range(2)])
        for t in range(NT):
            o3 = oall[:, t, :].rearrange("p (h d) -> p h d", h=4)
            z, bz = z_r.next()
            S.dma("sp", z, k.za_d[t * 128:(t + 1) * 128, 0:256], writes=[bz])
            S.op("act", lambda e: e.activation(out=z, in_=z, func=AF.Silu), reads=[bz], writes=[bz])
            sq, bsq = sq_r.next()
            S.op("dve", lambda e: e.tensor_tensor(out=sq, in0=oall[:, t, :], in1=oall[:, t, :], op=ALU.mult), reads=[boall], writes=[bsq])
            r4, br4 = r4_r.next()
            S.op("dve", lambda e: e.tensor_reduce(out=r4, in_=sq.rearrange("p (h d) -> p h d", h=4), axis=AX.X, op=ALU.add), reads=[bsq], writes=[br4])
            S.op("dve", lambda e: e.tensor_scalar(out=r4, in0=r4, scalar1=1.0 / 64, scalar2=EPS, op0=ALU.mult, op1=ALU.add), reads=[br4], writes=[br4])
            S.op("act", lambda e: e.activation(out=r4, in_=r4, func=AF.Sqrt), reads=[br4], writes=[br4])
            S.op("dve", lambda e: e.reciprocal(out=r4, in_=r4), reads=[br4], writes=[br4])
            s3 = sq.rearrange("p (h d) -> p h d", h=4)
            S.op("dve", lambda e: e.tensor_tensor(out=s3, in0=o3, in1=r4.rearrange("p (h o) -> p h o", o=1).broadcast_to([128, 4, 64]), op=ALU.mult),
                 reads=[boall, br4], writes=[bsq])
            S.op("dve", lambda e: e.tensor_tensor(out=s3, in0=s3, in1=gnw.rearrange("p (o d) -> p o d", o=1).broadcast_to([128, 4, 64]), op=ALU.mult),
                 reads=[bsq, k.bsmallp], writes=[bsq])
            ob, bob = ob_r.next()
            S.op("dve", lambda e: e.tensor_tensor(out=ob, in0=sq, in1=z, op=ALU.mult), reads=[bsq, bz], writes=[bob])
            S.dma("sp", k.mix_d[t * 128:(t + 1) * 128, 0:256], ob, reads=[bob])
        S.barrier()


def _col_index():
    def rot(a):
        return a.reshape(-1, 2, 32)[:, ::-1, :].reshape(-1)
    bqk = np.arange(1040, 1552)
    cqk = np.arange(1808, 2832)
    return np.concatenate([np.arange(0, 768), bqk, rot(bqk), cqk, rot(cqk),
                           np.arange(768, 1040), np.arange(1552, 1808), np.arange(2832, 3344)])


def _consts():
    p = np.arange(128)
    inv = (10000.0 ** (-np.arange(0, 64, 2, dtype=np.float32) / np.float32(64))).astype(np.float32)
    cst = np.zeros((128, 4), np.float32)
    cst[:, 0] = inv[p % 32]
    cst[:, 1] = np.where((p % 64) < 32, -1.0, 1.0)
    cst[:, 2] = math.pi / 2
    kk = np.arange(128)[:, None, None]
    dd = np.arange(17)[None, :, None] - 8
    qq = np.arange(128)[None, None, :]
    dist = np.abs(dd * 128 + kk - qq)
    m = (dist <= 64).astype(np.float32) + ((dist % 4 == 0) & (dist <= 256)) + ((dist % 16 == 0) & (dist <= 1024))
    return cst, m.astype(np.float32)


def make_in_maps(inputs):
    f = lambda a: np.ascontiguousarray(np.asarray(a))
    idx = _col_index()
    w_in_ext = f(np.asarray(inputs["w_in"])[:, :, idx])
    cst, mask = _consts()
    sp = np.zeros((1024,), np.float32)
    for l in range(L):
        sp[256 * l:256 * l + 64] = inputs["lambda_q1"][l]
        sp[256 * l + 64:256 * l + 128] = inputs["lambda_k1"][l]
        sp[256 * l + 128:256 * l + 192] = inputs["lambda_q2"][l]
        sp[256 * l + 192:256 * l + 256] = inputs["lambda_k2"][l]
        sp[512 + 128 * l:512 + 128 * (l + 1)] = inputs["subln_w"][l]
        sp[768 + 64 * l:768 + 64 * (l + 1)] = inputs["gdn_norm_w"][l]
        sp[896 + 8 * l:896 + 8 * (l + 1)] = np.asarray(inputs["a_log"][l]).reshape(-1)
        sp[912 + 8 * l:912 + 8 * (l + 1)] = np.asarray(inputs["dt_bias"][l]).reshape(-1)
    smallp = f(np.broadcast_to(sp[None, :], (128, 1024)))
    cwl = np.asarray(inputs["conv_w"]).astype(np.float32)
    convw = f(cwl.transpose(0, 2, 1).reshape(L, 6, 128, 5).transpose(0, 2, 1, 3).reshape(L, 128, 30))
    r_, c_ = np.arange(128)[:, None], np.arange(128)[None, :]
    tri = f(np.stack([(c_ >= r_), (c_ <= r_), (c_ > r_), (c_ < r_)], axis=1).astype(np.float32))
    shared = {
        "convw": convw, "tri": tri,
        "w_in": w_in_ext, "attn_nw": f(inputs["attn_norm_w"]), "w_out": f(inputs["w_out"]),
        "ffn_nw": f(inputs["ffn_norm_w"]), "w_gate": f(inputs["w_gate"]), "w_up": f(inputs["w_up"]),
        "w_down": f(inputs["w_down"]), "final_nw": f(inputs["final_norm_w"]), "smallp": smallp,
        "cst": cst, "maskb": mask,
    }
    x = np.asarray(inputs["x"])
    pos = np.asarray(inputs["positions"]).astype(np.int32)
    maps = []
    for b in range(8):
        m = dict(shared)
        m["x"] = f(x[b])
        m["pos"] = f(pos[b])
        maps.append(m)
    return maps


def kernel(**inputs):
    nc, _ = build()
    maps = make_in_maps(inputs)
    res = run_bass_kernel_spmd(nc, maps, core_ids=list(range(8)))
    return np.stack([r["out"] for r in res.results], axis=0).astype(np.float32)
```

```python
import math
import numpy as np
import concourse.bass as bass
import concourse.mybir as mybir
from concourse.bass_utils import run_bass_kernel_spmd

F32 = mybir.dt.float32
BF16 = mybir.dt.bfloat16
I32 = mybir.dt.int32
AF = mybir.ActivationFunctionType
ALU = mybir.AluOpType
AX = mybir.AxisListType

S_LEN = 4096
D = 1024
NT = 32
L = 2
DFF = 2816
NCOL = 4880
EPS = 1e-6
TWO_PI = 2.0 * math.pi
C1 = 6.28125
C2 = TWO_PI - C1


class Buf:
    __slots__ = ("name", "writer", "readers", "excl")

    def __init__(self, name="", excl=False):
        self.name = name
        self.writer = None
        self.readers = []
        self.excl = excl


class _Rec:
    def __getattr__(self, name):
        def f(*a, **kw):
            self.__dict__["call"] = (name, a, kw)
            return self
        return f


def _bind(fn):
    rec = _Rec()
    fn(rec)
    name, a, kw = rec.call
    return lambda e: getattr(e, name)(*a, **kw)


class Sched:
    CENG = ("pe", "act", "dve", "pool")
    DQ = ("sp", "act", "pool")

    def __init__(self, nc, n_dma_sems=8):
        self.nc = nc
        self.prog = {e: [] for e in ("pe", "act", "dve", "pool", "sp")}
        self.csem = {e: nc.alloc_semaphore("c_" + e) for e in self.CENG}
        self.cnt = {e: 0 for e in self.CENG}
        self.nd = n_dma_sems
        self.dsem = {q: [nc.alloc_semaphore("d_%s%d" % (q, i)) for i in range(n_dma_sems)]
                     for q in self.DQ}
        self.dcnt = {q: 0 for q in self.DQ}
        self.seen = {e: {} for e in self.prog}
        self.ninstr = 0

    def _sem(self, key):
        return self.csem[key[1]] if key[0] == "c" else self.dsem[key[1]][key[2]]

    def _need(self, eng, tok, waits):
        key, val, _ = tok
        if self.seen[eng].get(key, 0) >= val:
            return
        self.seen[eng][key] = val
        waits.append((self._sem(key), val))

    def _deps(self, eng, reads, writes, is_dma):
        waits = []
        for b in reads:
            t = b.writer
            if t is not None and not (eng == "pe" and t[2] == "pe"):
                self._need(eng, t, waits)
        for b in writes:
            t = b.writer
            if t is not None and (is_dma or t[2] != eng or eng != "pe"):
                self._need(eng, t, waits)
            for t in b.readers:
                if is_dma or t[2] != eng:
                    self._need(eng, t, waits)
        return waits

    def _commit(self, tok, reads, writes):
        for b in reads:
            b.readers.append(tok)
        for b in writes:
            b.writer = tok
            b.readers = []

    def op(self, eng, fn, reads=(), writes=()):
        ex = [b for b in reads if b.excl and b not in writes]
        if ex:
            writes = list(writes) + ex
        waits = self._deps(eng, reads, writes, False)
        self.cnt[eng] += 1
        n = self.cnt[eng]
        self.prog[eng].append((waits, _bind(fn), (self.csem[eng], 1)))
        self._commit((("c", eng), n, eng), reads, writes)
        self.ninstr += 1

    def dma(self, q, out, in_, reads=(), writes=(), **kw):
        waits = self._deps(q, reads, writes, True)
        i = self.dcnt[q]
        self.dcnt[q] += 1
        j = i % self.nd
        key = ("d", q, j)
        prev = 16 * (i // self.nd)
        if prev > 0:
            self._need(q, (key, prev, None), waits)
        tgt = prev + 16
        self.prog[q].append((waits, lambda e: e.dma_start(out=out, in_=in_, **kw), (self.dsem[q][j], 16)))
        self._commit((key, tgt, None), reads, writes)
        self.ninstr += 1

    def _all_tokens(self):
        toks = []
        for q in self.DQ:
            for j in range(self.nd):
                n = (self.dcnt[q] - j + self.nd - 1) // self.nd
                if n > 0:
                    toks.append((("d", q, j), 16 * n, None))
        for e in self.CENG:
            if self.cnt[e] > 0:
                toks.append((("c", e), self.cnt[e], e))
        return toks

    def barrier(self):
        toks = self._all_tokens()
        for eng in self.prog:
            waits = []
            for t in toks:
                if t[2] == eng:
                    continue
                self._need(eng, t, waits)
            if waits:
                self.prog[eng].append((waits, None, None))

    def finish(self):
        nc = self.nc
        final = [(self._sem(k), v) for k, v, _ in self._all_tokens()]
        prog = self.prog

        def replay(eng, lst, extra=()):
            for waits, fn, inc in lst:
                for s, v in waits:
                    eng.wait_ge(s, v)
                if fn is None:
                    continue
                ins = fn(eng)
                if inc is not None:
                    ins.then_inc(inc[0], inc[1])
            for s, v in extra:
                eng.wait_ge(s, v)

        with nc.Block() as block:
            @block.sync
            def _(e):
                replay(e, prog["sp"], final)

            @block.tensor
            def _(e):
                replay(e, prog["pe"])

            @block.scalar
            def _(e):
                replay(e, prog["act"])

            @block.vector
            def _(e):
                replay(e, prog["dve"])

            @block.gpsimd
            def _(e):
                replay(e, prog["pool"])


class Rot:
    def __init__(self, items):
        self.items = items
        self.i = 0

    def next(self):
        it = self.items[self.i % len(self.items)]
        self.i += 1
        return it


class K:
    pass


def sb(k, name, shape, dt, n=1):
    items = []
    for i in range(n):
        t = k.nc.alloc_sbuf_tensor("%s_%d_%d" % (name, k.uid, i), list(shape), dt)
        items.append((t, Buf(name)))
    k.uid += 1
    return items[0] if n == 1 else Rot(items)


def phase_setup(k):
    S, nc = k.S, k.nc
    S.op("pool", lambda e: e.memset(k.identf[:], 0.0), writes=[k.bidentf])
    S.op("pool", lambda e: e.affine_select(out=k.identf[:], in_=k.identf[:], pattern=[[-1, 128]],
                                           compare_op=ALU.not_equal, fill=1.0, base=0, channel_multiplier=1),
         reads=[k.bidentf], writes=[k.bidentf])
    S.op("dve", lambda e: e.tensor_copy(out=k.identb[:], in_=k.identf[:]), reads=[k.bidentf], writes=[k.bidentb])
    S.dma("sp", k.cst[:], k.cst_d, writes=[k.bcst])
    with nc.sbuf_tensor("su_pi", [128, S_LEN], I32) as pi, nc.sbuf_tensor("su_a", [128, S_LEN], F32) as ang, \
            nc.sbuf_tensor("su_k", [128, S_LEN], I32) as ki, nc.sbuf_tensor("su_kf", [128, S_LEN], F32) as kf, \
            nc.sbuf_tensor("su_r", [128, S_LEN], F32) as r, nc.sbuf_tensor("su_o", [128, S_LEN], F32) as o:
        bpi, bang, bki, bkf, br, bo = Buf(), Buf(), Buf(), Buf(), Buf(), Buf()
        S.dma("sp", pi[:], k.pos_d.rearrange("(o n) -> o n", o=1).broadcast_to([128, S_LEN]), writes=[bpi])
        S.op("dve", lambda e: e.tensor_copy(out=ang[:], in_=pi[:]), reads=[bpi], writes=[bang])
        S.op("dve", lambda e: e.tensor_scalar(out=ang[:], in0=ang[:], scalar1=k.cst[:, 0:1], scalar2=None, op0=ALU.mult),
             reads=[bang, k.bcst], writes=[bang])
        S.op("dve", lambda e: e.tensor_scalar(out=ki[:], in0=ang[:], scalar1=1.0 / TWO_PI, scalar2=None, op0=ALU.mult),
             reads=[bang], writes=[bki])
        S.op("dve", lambda e: e.tensor_copy(out=kf[:], in_=ki[:]), reads=[bki], writes=[bkf])
        S.op("dve", lambda e: e.scalar_tensor_tensor(out=r[:], in0=kf[:], scalar=-C1, in1=ang[:], op0=ALU.mult, op1=ALU.add),
             reads=[bkf, bang], writes=[br])
        S.op("dve", lambda e: e.scalar_tensor_tensor(out=r[:], in0=kf[:], scalar=-C2, in1=r[:], op0=ALU.mult, op1=ALU.add),
             reads=[bkf, br], writes=[br])
        S.op("dve", lambda e: e.tensor_scalar(out=r[:], in0=r[:], scalar1=-3.1415925, scalar2=3.1415925, op0=ALU.max, op1=ALU.min),
             reads=[br], writes=[br])
        S.op("act", lambda e: e.activation(out=o[:], in_=r[:], func=AF.Sin), reads=[br], writes=[bo])
        S.op("dve", lambda e: e.tensor_scalar(out=o[:], in0=o[:], scalar1=k.cst[:, 1:2], scalar2=None, op0=ALU.mult),
             reads=[bo, k.bcst], writes=[bo])
        S.dma("sp", k.sinT_d, o[:], reads=[bo])
        S.op("act", lambda e: e.activation(out=r[:], in_=r[:], func=AF.Abs), reads=[br], writes=[br])
        S.op("act", lambda e: e.activation(out=kf[:], in_=r[:], func=AF.Sin, scale=-1.0, bias=k.cst[:, 2:3]),
             reads=[br, k.bcst], writes=[bkf])
        S.dma("sp", k.cosT_d, kf[:], reads=[bkf])
        S.barrier()


def norm_tile(k, xt, bx, nwb, bnwb, h, bh, ss_r, junk_r):
    S = k.S
    ss, bss = ss_r.next()
    junk, bj = junk_r.next()
    S.op("act", lambda e: e.activation(out=junk[:], in_=xt[:], func=AF.Square, accum_out=ss[:]),
         reads=[bx], writes=[bj, bss])
    S.op("dve", lambda e: e.tensor_scalar(out=ss[:], in0=ss[:], scalar1=1.0 / D, scalar2=EPS, op0=ALU.mult, op1=ALU.add),
         reads=[bss], writes=[bss])
    S.op("act", lambda e: e.activation(out=ss[:], in_=ss[:], func=AF.Sqrt), reads=[bss], writes=[bss])
    S.op("dve", lambda e: e.reciprocal(out=ss[:], in_=ss[:]), reads=[bss], writes=[bss])
    S.op("dve", lambda e: e.scalar_tensor_tensor(out=h[:], in0=xt[:], scalar=ss[:, 0:1], in1=nwb[:], op0=ALU.mult, op1=ALU.mult),
         reads=[bx, bss, bnwb], writes=[bh])


def phase_a(k, l):
    S, nc = k.S, k.nc
    k.uid += 1
    src = k.x_d if l == 0 else k.xres_d
    with nc.sbuf_tensor("a_wb%d" % l, [128, 8, NCOL], BF16) as wb, \
            nc.sbuf_tensor("a_nwb%d" % l, [128, D], F32) as nwb, \
            nc.sbuf_tensor("a_arena%d" % l, [128, 12000], F32) as arena, nc.sbuf_tensor("a_arenab%d" % l, [128, 16000], BF16) as arenab:
        off = [0]

        offb = [0]

        def carve(n_f32, dt=F32):
            if dt == BF16:
                a = arenab[:, offb[0]:offb[0] + 2 * n_f32]
                offb[0] += 2 * n_f32
                return a
            a = arena[:, off[0]:off[0] + n_f32]
            off[0] += n_f32
            return a

        bwb = [Buf() for _ in range(8)]
        bnwb = Buf()
        for kk in range(8):
            S.dma("pool", wb[:, kk, :], k.w_in_d[l, kk * 128:(kk + 1) * 128, :], writes=[bwb[kk]])
        S.dma("sp", nwb[:], k.attn_nw_d[l:l + 1, :].broadcast_to([128, D]), writes=[bnwb])
        xt_r = Rot([(carve(1024), Buf()) for _ in range(2)])
        junk_r = Rot([(carve(1024), Buf()) for _ in range(1)])
        h_r = Rot([(carve(512, BF16), Buf()) for _ in range(2)])
        ss_r = Rot([(carve(1), Buf()) for _ in range(2)])
        hT_r = Rot([(carve(2048, BF16), Buf()) for _ in range(2)])
        cos_r = Rot([(carve(512), Buf()) for _ in range(2)])
        sin_r = Rot([(carve(512), Buf()) for _ in range(2)])
        ofm_r = Rot([(carve(512), Buf()) for _ in range(3)])
        t1_r = Rot([(carve(512), Buf()) for _ in range(2)])
        t2_r = Rot([(carve(512), Buf()) for _ in range(2)])
        oqk_r = Rot([(carve(256, BF16), Buf()) for _ in range(3)])
        oza_r = Rot([(carve(272), Buf()) for _ in range(2)])
        ovb_r = Rot([(carve(130, BF16), Buf()) for _ in range(2)])
        ovc_r = Rot([(carve(258, BF16), Buf()) for _ in range(2)])
        for (t, b) in ovb_r.items:
            S.op("dve", lambda e, t=t: e.memset(t, 1.0), writes=[b])
        for (t, b) in ovc_r.items:
            S.op("dve", lambda e, t=t: e.memset(t, 1.0), writes=[b])
        ptr_r = Rot([(k.pb[0], k.bpb[0]), (k.pb[1], k.bpb[1])])
        pfm_r = Rot([(k.pb[i], k.bpb[i]) for i in (2, 3, 4, 5)])
        ptm_r = Rot([(k.pb[i], k.bpb[i]) for i in (6, 7)])

        for g in range(8):
            hT, bhT = hT_r.next()
            hT3 = hT.rearrange("p (k n) -> p k n", k=8)
            cg, bcg = cos_r.next()
            sg, bsg = sin_r.next()
            S.dma("sp", cg, k.cosT_d[:, g * 512:(g + 1) * 512], writes=[bcg])
            S.dma("sp", sg, k.sinT_d[:, g * 512:(g + 1) * 512], writes=[bsg])
            for s in range(4):
                t0 = g * 512 + s * 128
                xt, bx = xt_r.next()
                h, bh = h_r.next()
                S.dma("sp", xt, src[t0:t0 + 128, :], writes=[bx])
                norm_tile(k, xt, bx, nwb, bnwb, h, bh, ss_r, junk_r)
                pt, bpt = ptr_r.next()
                ptb = pt[:].bitcast(BF16).rearrange("p (k n) -> p k n", k=8)
                for kk in range(8):
                    S.op("pe", lambda e, kk=kk, ptb=ptb, h=h: e.transpose(out=ptb[:, kk, :], in_=h[:, kk * 128:(kk + 1) * 128],
                                                                          identity=k.identb[:]),
                         reads=[bh, k.bidentb], writes=[bpt])
                S.op("dve", lambda e, ptb=ptb, hT3=hT3, s=s: e.tensor_copy(out=hT3[:, :, s * 128:(s + 1) * 128], in_=ptb),
                     reads=[bpt], writes=[bhT])
            tsl = slice(g * 512, (g + 1) * 512)

            def fm_mm(ch):
                pf, bpf = pfm_r.next()
                for kk in range(8):
                    S.op("pe", lambda e, kk=kk, pf=pf, ch=ch: e.matmul(pf[:], lhsT=wb[:, kk, ch * 128:(ch + 1) * 128],
                                                                      rhs=hT3[:, kk, :], start=(kk == 0), stop=(kk == 7)),
                         reads=[bwb[kk], bhT], writes=[bpf])
                return pf, bpf

            for ch in range(6):
                pf, bpf = fm_mm(ch)
                o, bo = ofm_r.next()
                S.op("act", lambda e, o=o, pf=pf: e.copy(out=o, in_=pf[:]), reads=[bpf], writes=[bo])
                S.dma("sp", k.qkvT_a_d[ch * 128:(ch + 1) * 128, tsl], o, reads=[bo])
            for (c0, nch, dst) in ((6, 4, k.qkT_b_d), (14, 8, k.qkT_c_d)):
                for j in range(nch):
                    pf, bpf = fm_mm(c0 + j)
                    pr, bpr = fm_mm(c0 + nch + j)
                    t1, bt1 = t1_r.next()
                    t2, bt2 = t2_r.next()
                    o, bo = oqk_r.next()
                    S.op("dve", lambda e, t1=t1, pf=pf: e.tensor_tensor(out=t1, in0=pf[:], in1=cg, op=ALU.mult),
                         reads=[bpf, bcg], writes=[bt1])
                    S.op("dve", lambda e, t2=t2, pr=pr: e.tensor_tensor(out=t2, in0=pr[:], in1=sg, op=ALU.mult),
                         reads=[bpr, bsg], writes=[bt2])
                    S.op("dve", lambda e, t1=t1, t2=t2, o=o: e.tensor_tensor(out=o, in0=t1, in1=t2, op=ALU.add),
                         reads=[bt1, bt2], writes=[bo])
                    S.dma("sp", dst[j * 128:(j + 1) * 128, tsl], o, reads=[bo])
            for s in range(4):
                t0 = g * 512 + s * 128

                def tm_mm(c0, n):
                    pm, bpm = ptm_r.next()
                    for kk in range(8):
                        S.op("pe", lambda e, kk=kk, pm=pm: e.matmul(pm[:, 0:n], lhsT=hT3[:, kk, s * 128:(s + 1) * 128],
                                                                   rhs=wb[:, kk, c0:c0 + n], start=(kk == 0), stop=(kk == 7)),
                             reads=[bwb[kk], bhT], writes=[bpm])
                    return pm, bpm

                pm, bpm = tm_mm(3840, 272)
                o, bo = oza_r.next()
                S.op("act", lambda e, o=o, pm=pm: e.copy(out=o, in_=pm[:, 0:272]), reads=[bpm], writes=[bo])
                S.dma("sp", k.za_d[t0:t0 + 128, :], o, reads=[bo])
                pm, bpm = tm_mm(4112, 256)
                o, bo = ovb_r.next()
                o3 = o.rearrange("p (h d) -> p h d", h=4)
                S.op("act", lambda e, o3=o3, pm=pm: e.copy(out=o3[:, :, 0:64], in_=pm[:, 0:256].rearrange("p (h d) -> p h d", h=4)),
                     reads=[bpm], writes=[bo])
                S.dma("sp", k.v_b_d[t0:t0 + 128, :], o, reads=[bo])
                pm, bpm = tm_mm(4368, 512)
                o, bo = ovc_r.next()
                o3 = o.rearrange("p (h d) -> p h d", h=4)
                S.op("act", lambda e, o3=o3, pm=pm: e.copy(out=o3[:, :, 0:128], in_=pm[:, 0:512].rearrange("p (h d) -> p h d", h=4)),
                     reads=[bpm], writes=[bo])
                S.dma("sp", k.v_c_d[t0:t0 + 128, :], o, reads=[bo])
        S.barrier()


def phase_c(k, l):
    S, nc = k.S, k.nc
    k.uid += 1
    lam_init = 0.8 - 0.6 * math.exp(-0.3 * l)
    with nc.sbuf_tensor("c_kT%d" % l, [128, S_LEN], BF16) as kT, nc.sbuf_tensor("c_qT%d" % l, [128, S_LEN], BF16) as qT, \
            nc.sbuf_tensor("c_v%d" % l, [128, NT, 129], BF16) as va, \
            nc.sbuf_tensor("c_arena%d" % l, [128, 3000], F32) as arena, nc.sbuf_tensor("c_arenab%d" % l, [128, 4000], BF16) as arenab:
        off = [0]

        offb = [0]

        def carve(n_f32, dt=F32):
            if dt == BF16:
                a = arenab[:, offb[0]:offb[0] + 2 * n_f32]
                offb[0] += 2 * n_f32
                return a
            a = arena[:, off[0]:off[0] + n_f32]
            off[0] += n_f32
            return a

        bkT, bqT, bva = Buf(), Buf(), Buf()
        p_r = Rot([(carve(128, BF16), Buf()) for _ in range(8)])
        o1_r = Rot([(carve(128), Buf()) for _ in range(2)])
        rc_r = Rot([(carve(4), Buf()) for _ in range(2)])
        o_r = Rot([(carve(128), Buf()) for _ in range(2)])
        junk_r = Rot([(carve(128), Buf()) for _ in range(1)])
        ss_r = Rot([(carve(1), Buf()) for _ in range(2)])
        ob_r = Rot([(carve(64, BF16), Buf()) for _ in range(2)])
        lam, blam = carve(4), Buf()
        ps_rc = [Rot([(k.pb[i], k.bpb[i]) for i in (0, 1)]), Rot([(k.pb[i], k.bpb[i]) for i in (6, 7)])]
        acc = [(k.pb[i], k.bpb[i]) for i in (2, 3, 4, 5)]
        sp_ = k.smallp
        c0 = 256 * l
        jk, bjk = junk_r.next()
        S.op("dve", lambda e: e.tensor_tensor(out=jk[:, 0:64], in0=sp_[:, c0:c0 + 64], in1=sp_[:, c0 + 64:c0 + 128], op=ALU.mult),
             reads=[k.bsmallp], writes=[bjk])
        S.op("dve", lambda e: e.tensor_reduce(out=lam[:, 0:1], in_=jk[:, 0:64], axis=AX.X, op=ALU.add), reads=[bjk], writes=[blam])
        S.op("dve", lambda e: e.tensor_tensor(out=jk[:, 0:64], in0=sp_[:, c0 + 128:c0 + 192], in1=sp_[:, c0 + 192:c0 + 256], op=ALU.mult),
             reads=[k.bsmallp, blam], writes=[bjk])
        S.op("dve", lambda e: e.tensor_reduce(out=lam[:, 1:2], in_=jk[:, 0:64], axis=AX.X, op=ALU.add), reads=[bjk, blam], writes=[blam])
        S.op("act", lambda e: e.activation(out=lam[:, 0:2], in_=lam[:, 0:2], func=AF.Exp), reads=[blam], writes=[blam])
        S.op("dve", lambda e: e.tensor_tensor(out=lam[:, 2:3], in0=lam[:, 1:2], in1=lam[:, 0:1], op=ALU.subtract), reads=[blam], writes=[blam])
        S.op("dve", lambda e: e.tensor_scalar(out=lam[:, 2:3], in0=lam[:, 2:3], scalar1=-lam_init, scalar2=None, op0=ALU.add),
             reads=[blam], writes=[blam])
        subw = sp_[:, 512 + 128 * l:512 + 128 * (l + 1)]
        for hh in range(4):
            S.dma("sp", qT[:], k.qkT_c_d[hh * 128:(hh + 1) * 128, :], writes=[bqT])
            S.dma("sp", kT[:], k.qkT_c_d[512 + hh * 128:512 + (hh + 1) * 128, :], writes=[bkT])
            S.dma("sp", va[:], k.v_c_d.rearrange("(t p) (h d) -> p t h d", p=128, h=4)[:, :, hh, :], writes=[bva])
            for qg in range(16):
                qsl = slice(qg * 256, (qg + 1) * 256)
                pend = []
                for kb in range(NT + 2):
                    if kb < NT:
                        pcs = []
                        pss = []
                        for c in range(2):
                            rs = slice(c * 64, (c + 1) * 64)
                            ps, bps = ps_rc[c].next()
                            S.op("pe", lambda e: e.matmul(ps[:, 0:256], lhsT=kT[rs, kb * 128:(kb + 1) * 128], rhs=qT[rs, qsl], start=True, stop=True),
                                 reads=[bkT, bqT], writes=[bps])
                            pss.append((ps, bps))
                        for c in range(2):
                            ps, bps = pss[c]
                            p, bp = p_r.next()
                            S.op("act", lambda e: e.activation(out=p, in_=ps[:, 0:256], func=AF.Exp, scale=0.125), reads=[bps], writes=[bp])
                            pcs.append((p, bp))
                        pend.append((pcs, kb))
                    if kb >= 2:
                        pcs, kq = pend.pop(0)
                        for c in range(2):
                            p, bp = pcs[c]
                            for s in range(2):
                                a, ba = acc[c * 2 + s]
                                S.op("pe", lambda e: e.matmul(a[:, 0:129], lhsT=p[:, s * 128:(s + 1) * 128], rhs=va[:, kq, :],
                                                              start=(kq == 0), stop=(kq == NT - 1)), reads=[bp, bva], writes=[ba])
                for s in range(2):
                    a1, ba1 = acc[s]
                    a2, ba2 = acc[2 + s]
                    rc, brc = rc_r.next()
                    o1, bo1 = o1_r.next()
                    S.op("dve", lambda e: e.reciprocal(out=rc[:, 0:1], in_=a1[:, 128:129]), reads=[ba1], writes=[brc])
                    S.op("dve", lambda e: e.tensor_scalar(out=o1[:, 0:128], in0=a1[:, 0:128], scalar1=rc[:, 0:1], scalar2=None, op0=ALU.mult),
                         reads=[ba1, brc], writes=[bo1])
                    S.op("dve", lambda e: e.reciprocal(out=rc[:, 2:3], in_=a2[:, 128:129]), reads=[ba2, brc], writes=[brc])
                    S.op("dve", lambda e: e.tensor_tensor(out=rc[:, 1:2], in0=rc[:, 2:3], in1=lam[:, 2:3], op=ALU.mult), reads=[brc, blam], writes=[brc])
                    o, bo = o_r.next()
                    S.op("dve", lambda e: e.scalar_tensor_tensor(out=o, in0=a2[:, 0:128], scalar=rc[:, 1:2], in1=o1[:, 0:128], op0=ALU.mult, op1=ALU.add),
                         reads=[ba2, brc, bo1], writes=[bo])
                    ss, bss = ss_r.next()
                    jk, bjk = junk_r.next()
                    S.op("act", lambda e: e.activation(out=jk, in_=o, func=AF.Square, accum_out=ss), reads=[bo], writes=[bjk, bss])
                    S.op("dve", lambda e: e.tensor_scalar(out=ss, in0=ss, scalar1=1.0 / 128, scalar2=EPS, op0=ALU.mult, op1=ALU.add), reads=[bss], writes=[bss])
                    S.op("act", lambda e: e.activation(out=ss, in_=ss, func=AF.Sqrt), reads=[bss], writes=[bss])
                    S.op("dve", lambda e: e.reciprocal(out=ss, in_=ss), reads=[bss], writes=[bss])
                    S.op("dve", lambda e: e.tensor_scalar(out=ss, in0=ss, scalar1=1.0 - lam_init, scalar2=None, op0=ALU.mult), reads=[bss], writes=[bss])
                    ob, bob = ob_r.next()
                    S.op("dve", lambda e: e.scalar_tensor_tensor(out=ob, in0=o, scalar=ss[:, 0:1], in1=subw, op0=ALU.mult, op1=ALU.mult),
                         reads=[bo, bss, k.bsmallp], writes=[bob])
                    t0 = qg * 256 + s * 128
                    S.dma("sp", k.mix_d[t0:t0 + 128, 512 + hh * 128:512 + (hh + 1) * 128], ob, reads=[bob])
        S.barrier()


def phase_b(k, l):
    S, nc = k.S, k.nc
    k.uid += 1
    with nc.sbuf_tensor("b_kT%d" % l, [128, S_LEN], BF16) as kT, nc.sbuf_tensor("b_qT%d" % l, [128, S_LEN], BF16) as qT, \
            nc.sbuf_tensor("b_v%d" % l, [128, NT, 65], BF16) as va, \
            nc.sbuf_tensor("b_mask%d" % l, [128, 17, 128], F32) as mask, \
            nc.sbuf_tensor("b_arena%d" % l, [128, 2000], F32) as arena, nc.sbuf_tensor("b_arenab%d" % l, [128, 4000], BF16) as arenab:
        off = [0]

        offb = [0]

        def carve(n_f32, dt=F32):
            if dt == BF16:
                a = arenab[:, offb[0]:offb[0] + 2 * n_f32]
                offb[0] += 2 * n_f32
                return a
            a = arena[:, off[0]:off[0] + n_f32]
            off[0] += n_f32
            return a

        bkT, bqT, bva, bmask = Buf(), Buf(), Buf(), Buf()
        S.dma("sp", mask[:], k.mask_d, writes=[bmask])
        pe_r = Rot([(carve(512), Buf()) for _ in range(3)])
        p_r = Rot([(carve(256, BF16), Buf()) for _ in range(4)])
        rc_r = Rot([(carve(1), Buf()) for _ in range(2)])
        ob_r = Rot([(carve(32, BF16), Buf()) for _ in range(2)])
        ps_r = Rot([(k.pb[i], k.bpb[i]) for i in (0, 1, 2, 5)])
        acc_r = Rot([(k.pb[i], k.bpb[i]) for i in (3, 4, 6)])
        for hp in range(2):
            S.dma("sp", qT[:], k.qkT_b_d[hp * 128:(hp + 1) * 128, :], writes=[bqT])
            S.dma("sp", kT[:], k.qkT_b_d[256 + hp * 128:256 + (hp + 1) * 128, :], writes=[bkT])
            for hl in range(2):
                hh = hp * 2 + hl
                rs = slice(hl * 64, (hl + 1) * 64)
                S.dma("sp", va[:], k.v_b_d.rearrange("(t p) (h d) -> p t h d", p=128, h=4)[:, :, hh, :], writes=[bva])
                items = []
                for qb in range(NT):
                    lo, hi = max(0, qb - 8), min(NT - 1, qb + 8)
                    kbs = list(range(lo, hi + 1))
                    for b0 in range(0, len(kbs), 4):
                        items.append((qb, lo, hi, kbs[b0:b0 + 4]))
                pend = []
                cur_acc = {}
                for it in range(len(items) + 2):
                    if it < len(items):
                        qb, lo, hi, grp = items[it]
                        n = len(grp)
                        ps, bps = ps_r.next()
                        for i, kb in enumerate(grp):
                            S.op("pe", lambda e: e.matmul(ps[:, i * 128:(i + 1) * 128], lhsT=kT[rs, kb * 128:(kb + 1) * 128],
                                                          rhs=qT[rs, qb * 128:(qb + 1) * 128], start=True, stop=True),
                                 reads=[bkT, bqT], writes=[bps])
                        pe_, bpe = pe_r.next()
                        S.op("act", lambda e: e.activation(out=pe_[:, 0:n * 128], in_=ps[:, 0:n * 128], func=AF.Exp, scale=0.125),
                             reads=[bps], writes=[bpe])
                        p, bp = p_r.next()
                        d0 = grp[0] - qb + 8
                        S.op("dve", lambda e: e.tensor_tensor(out=p[:, 0:n * 128], in0=pe_[:, 0:n * 128],
                                                              in1=mask[:, d0:d0 + n, :].rearrange("p a b -> p (a b)"), op=ALU.mult),
                             reads=[bpe, bmask], writes=[bp])
                        pend.append((items[it], p, bp))
                    if it >= 2:
                        (qb, lo, hi, grp), p, bp = pend.pop(0)
                        if grp[0] == lo:
                            cur_acc[qb] = acc_r.next()
                        a, ba = cur_acc[qb]
                        for i, kb in enumerate(grp):
                            S.op("pe", lambda e: e.matmul(a[:, 0:65], lhsT=p[:, i * 128:(i + 1) * 128], rhs=va[:, kb, :],
                                                          start=(kb == lo), stop=(kb == hi)),
                                 reads=[bp, bva], writes=[ba])
                        if grp[-1] == hi:
                            rc, brc = rc_r.next()
                            S.op("dve", lambda e: e.reciprocal(out=rc, in_=a[:, 64:65]), reads=[ba], writes=[brc])
                            ob, bob = ob_r.next()
                            S.op("dve", lambda e: e.tensor_scalar(out=ob, in0=a[:, 0:64], scalar1=rc[:, 0:1], scalar2=None, op0=ALU.mult),
                                 reads=[ba, brc], writes=[bob])
                            S.dma("sp", k.mix_d[qb * 128:(qb + 1) * 128, 256 + hh * 64:256 + (hh + 1) * 64], ob, reads=[bob])
        S.barrier()


def phase_d1(k, l):
    S, nc = k.S, k.nc
    k.uid += 1
    src = k.x_d if l == 0 else k.xres_d
    with nc.sbuf_tensor("d1_w%d" % l, [128, 8, D], BF16) as wo, nc.sbuf_tensor("d1_arena%d" % l, [128, 5000], F32) as arena, nc.sbuf_tensor("d1_arenab%d" % l, [128, 5000], BF16) as arenab:
        off = [0]

        offb = [0]

        def carve(n_f32, dt=F32):
            if dt == BF16:
                a = arenab[:, offb[0]:offb[0] + 2 * n_f32]
                offb[0] += 2 * n_f32
                return a
            a = arena[:, off[0]:off[0] + n_f32]
            off[0] += n_f32
            return a

        bwo = Buf()
        S.dma("pool", wo[:], k.w_out_d[l].rearrange("(k p) n -> p k n", p=128), writes=[bwo])
        xt_r = Rot([(carve(1024), Buf()) for _ in range(2)])
        m_r = Rot([(carve(512, BF16), Buf()) for _ in range(2)])
        mT_r = Rot([(carve(512, BF16), Buf()) for _ in range(2)])
        xo_r = Rot([(carve(1024), Buf()) for _ in range(2)])
        ptr_r = Rot([(k.pb[0], k.bpb[0]), (k.pb[1], k.bpb[1])])
        po_r = Rot([(k.pb[i], k.bpb[i]) for i in (2, 3, 4, 5)])
        for t in range(NT):
            t0 = t * 128
            xt, bx = xt_r.next()
            m, bm = m_r.next()
            S.dma("sp", xt, src[t0:t0 + 128, :], writes=[bx])
            S.dma("sp", m, k.mix_d[t0:t0 + 128, :], writes=[bm])
            pt, bpt = ptr_r.next()
            ptb = pt[:].bitcast(BF16).rearrange("p (k n) -> p k n", k=8)
            for kk in range(8):
                S.op("pe", lambda e, kk=kk, ptb=ptb, m=m: e.transpose(out=ptb[:, kk, :], in_=m[:, kk * 128:(kk + 1) * 128], identity=k.identb[:]),
                     reads=[bm, k.bidentb], writes=[bpt])
            mT, bmT = mT_r.next()
            mT3 = mT.rearrange("p (k n) -> p k n", k=8)
            S.op("act", lambda e, mT3=mT3, ptb=ptb: e.copy(out=mT3, in_=ptb), reads=[bpt], writes=[bmT])
            xo, bxo = xo_r.next()
            for half in range(2):
                po, bpo = po_r.next()
                for kk in range(8):
                    S.op("pe", lambda e, kk=kk, po=po, mT3=mT3, half=half: e.matmul(po[:], lhsT=mT3[:, kk, :], rhs=wo[:, kk, half * 512:(half + 1) * 512],
                                                                                   start=(kk == 0), stop=(kk == 7)),
                         reads=[bmT, bwo], writes=[bpo])
                S.op("dve", lambda e, xo=xo, po=po, xt=xt, half=half: e.tensor_tensor(out=xo[:, half * 512:(half + 1) * 512], in0=po[:],
                                                                                     in1=xt[:, half * 512:(half + 1) * 512], op=ALU.add),
                     reads=[bpo, bx], writes=[bxo])
            S.dma("sp", k.xres_d[t0:t0 + 128, :], xo, reads=[bxo])
        S.barrier()


def phase_d2(k, l):
    S, nc = k.S, k.nc
    k.uid += 1
    last = (l == L - 1)
    NFF = DFF // 128
    with nc.sbuf_tensor("d2_wg%d" % l, [128, 8, DFF], BF16) as wg, nc.sbuf_tensor("d2_wu%d" % l, [128, 8, DFF], BF16) as wu, \
            nc.sbuf_tensor("d2_wd%d" % l, [128, NFF, D], BF16) as wd, nc.sbuf_tensor("d2_nwb%d" % l, [128, D], F32) as nwb, \
            nc.sbuf_tensor("d2_fnw%d" % l, [128, D], F32) as fnw, \
            nc.sbuf_tensor("d2_arena%d" % l, [128, 7700], F32) as arena, nc.sbuf_tensor("d2_arenab%d" % l, [128, 11800], BF16) as arenab:
        off = [0]

        offb = [0]

        def carve(n_f32, dt=F32):
            if dt == BF16:
                a = arenab[:, offb[0]:offb[0] + 2 * n_f32]
                offb[0] += 2 * n_f32
                return a
            a = arena[:, off[0]:off[0] + n_f32]
            off[0] += n_f32
            return a

        bwg = [Buf() for _ in range(8)]
        bwu = [Buf() for _ in range(8)]
        bwd, bnwb, bfnw = Buf(), Buf(), Buf()
        for kk in range(8):
            S.dma("pool", wg[:, kk, :], k.w_gate_d[l, kk * 128:(kk + 1) * 128, :], writes=[bwg[kk]])
            S.dma("pool", wu[:, kk, :], k.w_up_d[l, kk * 128:(kk + 1) * 128, :], writes=[bwu[kk]])
        S.dma("pool", wd[:], k.w_down_d[l].rearrange("(k p) n -> p k n", p=128), writes=[bwd])
        S.dma("sp", nwb[:], k.ffn_nw_d[l:l + 1, :].broadcast_to([128, D]), writes=[bnwb])
        if last:
            S.dma("sp", fnw[:], k.final_nw_d.rearrange("(o n) -> o n", o=1).broadcast_to([128, D]), writes=[bfnw])
        TG = 256
        xt_r = Rot([(carve(1024), Buf()) for _ in range(3)])
        junk_r = Rot([(carve(1024), Buf()) for _ in range(1)])
        h_r = Rot([(carve(512, BF16), Buf()) for _ in range(2)])
        ss_r = Rot([(carve(1), Buf()) for _ in range(2)])
        hT_r = Rot([(carve(1024, BF16), Buf()) for _ in range(2)])
        aT_r = Rot([(carve(NFF * TG // 2, BF16), Buf()) for _ in range(1)])
        sg_r = Rot([(carve(TG), Buf()) for _ in range(2)])
        xo_r = Rot([(carve(1024), Buf()) for _ in range(2)])
        yo_r = Rot([(carve(1024), Buf()) for _ in range(1)])
        ptr_r = Rot([(k.pb[0], k.bpb[0]), (k.pb[1], k.bpb[1])])
        pg_r = Rot([(k.pb[i], k.bpb[i]) for i in (2, 3)])
        pu_r = Rot([(k.pb[i], k.bpb[i]) for i in (4, 5)])
        po_r = Rot([(k.pb[i], k.bpb[i]) for i in (6, 7)])
        for g in range(S_LEN // TG):
            hT, bhT = hT_r.next()
            hT3 = hT.rearrange("p (k n) -> p k n", k=8)
            xts = []
            for s in range(TG // 128):
                t0 = g * TG + s * 128
                xt, bx = xt_r.next()
                xts.append((xt, bx))
                h, bh = h_r.next()
                S.dma("sp", xt, k.xres_d[t0:t0 + 128, :], writes=[bx])
                norm_tile(k, xt, bx, nwb, bnwb, h, bh, ss_r, junk_r)
                pt, bpt = ptr_r.next()
                ptb = pt[:].bitcast(BF16).rearrange("p (k n) -> p k n", k=8)
                for kk in range(8):
                    S.op("pe", lambda e, kk=kk, ptb=ptb, h=h: e.transpose(out=ptb[:, kk, :], in_=h[:, kk * 128:(kk + 1) * 128], identity=k.identb[:]),
                         reads=[bh, k.bidentb], writes=[bpt])
                S.op("dve", lambda e, ptb=ptb, hT3=hT3, s=s: e.tensor_copy(out=hT3[:, :, s * 128:(s + 1) * 128], in_=ptb),
                     reads=[bpt], writes=[bhT])
            aT, baT = aT_r.next()
            aT3 = aT.rearrange("p (f n) -> p f n", f=NFF)
            for f in range(NFF):
                pg, bpg = pg_r.next()
                pu, bpu = pu_r.next()
                for kk in range(8):
                    S.op("pe", lambda e, kk=kk, pg=pg, f=f: e.matmul(pg[:, 0:TG], lhsT=wg[:, kk, f * 128:(f + 1) * 128], rhs=hT3[:, kk, :],
                                                                    start=(kk == 0), stop=(kk == 7)),
                         reads=[bwg[kk], bhT], writes=[bpg])
                for kk in range(8):
                    S.op("pe", lambda e, kk=kk, pu=pu, f=f: e.matmul(pu[:, 0:TG], lhsT=wu[:, kk, f * 128:(f + 1) * 128], rhs=hT3[:, kk, :],
                                                                    start=(kk == 0), stop=(kk == 7)),
                         reads=[bwu[kk], bhT], writes=[bpu])
                sg, bsg = sg_r.next()
                S.op("act", lambda e, sg=sg, pg=pg: e.activation(out=sg, in_=pg[:, 0:TG], func=AF.Silu), reads=[bpg], writes=[bsg])
                S.op("dve", lambda e, sg=sg, pu=pu, f=f, aT3=aT3: e.tensor_tensor(out=aT3[:, f, :], in0=pu[:, 0:TG], in1=sg, op=ALU.mult),
                     reads=[bpu, bsg], writes=[baT])
            for s in range(TG // 128):
                t0 = g * TG + s * 128
                xt, bx = xts[s]
                xo, bxo = xo_r.next()
                for half in range(2):
                    po, bpo = po_r.next()
                    for f in range(NFF):
                        S.op("pe", lambda e, f=f, po=po, s=s, half=half, aT3=aT3: e.matmul(po[:], lhsT=aT3[:, f, s * 128:(s + 1) * 128],
                                                                                         rhs=wd[:, f, half * 512:(half + 1) * 512],
                                                                                         start=(f == 0), stop=(f == NFF - 1)),
                             reads=[baT, bwd], writes=[bpo])
                    S.op("dve", lambda e, xo=xo, po=po, xt=xt, half=half: e.tensor_tensor(out=xo[:, half * 512:(half + 1) * 512], in0=po[:],
                                                                                         in1=xt[:, half * 512:(half + 1) * 512], op=ALU.add),
                         reads=[bpo, bx], writes=[bxo])
                if not last:
                    S.dma("sp", k.xres_d[t0:t0 + 128, :], xo, reads=[bxo])
                else:
                    yo, byo = yo_r.next()
                    norm_tile(k, xo, bxo, fnw, bfnw, yo, byo, ss_r, junk_r)
                    S.dma("sp", k.out_d[t0:t0 + 128, :], yo, reads=[byo])
        S.barrier()


def build(dbg=False, phases=None):
    nc = bass.Bass("TRN2", target_bir_lowering=False)
    k = K()
    k.nc = nc
    k.uid = 0
    k.S = Sched(nc)
    ext = "ExternalInput"
    k.x_d = nc.dram_tensor("x", [S_LEN, D], F32, kind=ext).ap()
    k.pos_d = nc.dram_tensor("pos", [S_LEN], I32, kind=ext).ap()
    k.w_in_d = nc.dram_tensor("w_in", [L, D, NCOL], F32, kind=ext).ap()
    k.attn_nw_d = nc.dram_tensor("attn_nw", [L, D], F32, kind=ext).ap()
    k.w_out_d = nc.dram_tensor("w_out", [L, D, D], F32, kind=ext).ap()
    k.ffn_nw_d = nc.dram_tensor("ffn_nw", [L, D], F32, kind=ext).ap()
    k.w_gate_d = nc.dram_tensor("w_gate", [L, D, DFF], F32, kind=ext).ap()
    k.w_up_d = nc.dram_tensor("w_up", [L, D, DFF], F32, kind=ext).ap()
    k.w_down_d = nc.dram_tensor("w_down", [L, DFF, D], F32, kind=ext).ap()
    k.final_nw_d = nc.dram_tensor("final_nw", [D], F32, kind=ext).ap()
    k.smallp_d = nc.dram_tensor("smallp", [128, 1024], F32, kind=ext).ap()
    k.cst_d = nc.dram_tensor("cst", [128, 4], F32, kind=ext).ap()
    k.mask_d = nc.dram_tensor("maskb", [128, 17, 128], F32, kind=ext).ap()
    k.convw_d = nc.dram_tensor("convw", [L, 128, 30], F32, kind=ext).ap()
    k.tri_d = nc.dram_tensor("tri", [128, 4, 128], F32, kind=ext).ap()
    k.out_d = nc.dram_tensor("out", [S_LEN, D], F32, kind="ExternalOutput").ap()
    sk = "ExternalOutput" if dbg else "Internal"
    k.cosT_d = nc.dram_tensor("cosT", [128, S_LEN], F32, kind=sk).ap()
    k.sinT_d = nc.dram_tensor("sinT", [128, S_LEN], F32, kind=sk).ap()
    k.qkvT_a_d = nc.dram_tensor("qkvT_a", [768, S_LEN], F32, kind=sk).ap()
    k.za_d = nc.dram_tensor("za", [S_LEN, 272], F32, kind=sk).ap()
    k.qkT_b_d = nc.dram_tensor("qkT_b", [512, S_LEN], BF16, kind=sk).ap()
    k.qkT_c_d = nc.dram_tensor("qkT_c", [1024, S_LEN], BF16, kind=sk).ap()
    k.v_b_d = nc.dram_tensor("v_b", [S_LEN, 4 * 65], BF16, kind=sk).ap()
    k.v_c_d = nc.dram_tensor("v_c", [S_LEN, 4 * 129], BF16, kind=sk).ap()
    k.mix_d = nc.dram_tensor("mix", [S_LEN, D], BF16, kind=sk).ap()
    k.xres_d = nc.dram_tensor("xres", [S_LEN, D], F32, kind=sk).ap()
    k.identf = nc.alloc_sbuf_tensor("identf", [128, 128], F32)
    k.identb = nc.alloc_sbuf_tensor("identb", [128, 128], BF16)
    k.cst = nc.alloc_sbuf_tensor("cst_sb", [128, 4], F32)
    k.smallp = nc.alloc_sbuf_tensor("smallp_sb", [128, 1024], F32)
    k.bidentf, k.bidentb, k.bcst, k.bsmallp = Buf(), Buf(), Buf(), Buf()
    k.pb = [nc.alloc_psum_tensor("pb%d" % i, [128, 512], F32) for i in range(8)]
    k.bpb = [Buf("pb%d" % i, excl=True) for i in range(8)]
    k.S.dma("sp", k.smallp[:], k.smallp_d, writes=[k.bsmallp])
    if phases is None:
        phases = ["setup"] + [p + str(l) for l in range(L) for p in ("a", "c", "b", "g", "d1", "d2")]
    for ph in phases:
        if ph == "setup":
            phase_setup(k)
        elif ph[0] == "a":
            phase_a(k, int(ph[1:]))
        elif ph[0] == "c":
            phase_c(k, int(ph[1:]))
        elif ph[0] == "b":
            phase_b(k, int(ph[1:]))
        elif ph[0] == "g":
            phase_g(k, int(ph[1:]))
        elif ph[0] == "z":
            phase_z(k)
        elif ph[:2] == "d1":
            phase_d1(k, int(ph[2:]))
        elif ph[:2] == "d2":
            phase_d2(k, int(ph[2:]))
    k.S.finish()
    k.ninstr = k.S.ninstr
    return nc, k


def phase_z(k):
    S, nc = k.S, k.nc
    with nc.sbuf_tensor("z_t", [128, 256], BF16) as zt:
        bz = Buf()
        S.op("dve", lambda e: e.memset(zt[:], 0.0), writes=[bz])
        for t in range(NT):
            S.dma("sp", k.mix_d[t * 128:(t + 1) * 128, 0:256], zt[:], reads=[bz])
        S.barrier()


def phase_g(k, l):
    S, nc = k.S, k.nc
    k.uid += 1
    sp_ = k.smallp
    with nc.sbuf_tensor("g_qkvn%d" % l, [128, 6, S_LEN], F32) as qkvn, \
            nc.sbuf_tensor("g_oall%d" % l, [128, NT, 256], F32) as oall, \
            nc.sbuf_tensor("g_gate%d" % l, [128, NT, 16], F32) as gab, \
            nc.sbuf_tensor("g_g%d" % l, [128, NT, 8], F32) as gg, \
            nc.sbuf_tensor("g_beta%d" % l, [128, NT, 8], F32) as beta, \
            nc.sbuf_tensor("g_cw%d" % l, [128, 30], F32) as cw, \
            nc.sbuf_tensor("g_tri%d" % l, [128, 4, 128], F32) as tri, \
            nc.sbuf_tensor("g_bd%d" % l, [128, 128], F32) as bd, \
            nc.sbuf_tensor("g_ones%d" % l, [128, 128], F32) as onesf, \
            nc.sbuf_tensor("g_st%d" % l, [128, 2, 2, 128], F32) as st, \
            nc.sbuf_tensor("g_obt%d" % l, [128, 2, 256], BF16) as obt, \
            nc.sbuf_tensor("g_arenab%d" % l, [128, 16], BF16) as garenab, \
            nc.sbuf_tensor("g_arena%d" % l, [128, 16100], F32) as arena:
        off = [0]
        offb = [0]

        def carveb(n):
            a = garenab[:, offb[0]:offb[0] + n]
            offb[0] += n
            assert offb[0] <= 5120
            return a

        def carve(n):
            a = arena[:, off[0]:off[0] + n]
            off[0] += n
            assert off[0] <= 16100, off[0]
            return a

        bq = [Buf() for _ in range(6)]
        boall, bgab, bgg, bbeta, bcw, btri, bbd, bones = Buf(), Buf(), Buf(), Buf(), Buf(), Buf(), Buf(), Buf()
        bst = [Buf(), Buf()]
        S.dma("sp", cw[:], k.convw_d[l], writes=[bcw])
        S.dma("sp", tri[:], k.tri_d, writes=[btri])
        for t in range(NT):
            S.dma("sp", gab[:, t, :], k.za_d[t * 128:(t + 1) * 128, 256:272], writes=[bgab])
        S.op("dve", lambda e: e.memset(bd[:], 0.0), writes=[bbd])
        S.op("dve", lambda e: e.memset(bd[0:64, 0:64], 1.0), writes=[bbd])
        S.op("dve", lambda e: e.memset(bd[64:128, 64:128], 1.0), writes=[bbd])
        S.op("dve", lambda e: e.memset(onesf[:], 1.0), writes=[bones])
        S.op("dve", lambda e: e.memset(oall[:], 0.0), writes=[boall])
        S.op("dve", lambda e: e.memset(st[:], 0.0), writes=bst)
        ab = sp_[:, 896 + 8 * l:904 + 8 * l]
        db = sp_[:, 912 + 8 * l:920 + 8 * l]
        eal, beal = carve(8), Buf()
        S.op("act", lambda e: e.activation(out=eal, in_=ab, func=AF.Exp), reads=[k.bsmallp], writes=[beal])
        S.op("dve", lambda e: e.tensor_tensor(out=gg[:], in0=gab[:, :, 0:8], in1=db.rearrange("p (o c) -> p o c", o=1).broadcast_to([128, NT, 8]), op=ALU.add),
             reads=[bgab, k.bsmallp], writes=[bgg])
        S.op("act", lambda e: e.activation(out=gg[:], in_=gg[:], func=AF.Exp), reads=[bgg], writes=[bgg])
        S.op("act", lambda e: e.activation(out=gg[:], in_=gg[:], func=AF.Ln, bias=1.0), reads=[bgg], writes=[bgg])
        S.op("dve", lambda e: e.scalar_tensor_tensor(out=gg[:], in0=gg[:], scalar=-1.0, in1=eal.rearrange("p (o c) -> p o c", o=1).broadcast_to([128, NT, 8]),
                                                     op0=ALU.mult, op1=ALU.mult), reads=[bgg, beal], writes=[bgg])
        S.op("act", lambda e: e.activation(out=beta[:], in_=gab[:, :, 8:16], func=AF.Sigmoid), reads=[bgab], writes=[bbeta])
        xin, bxin = carve(S_LEN + 4), Buf()
        rs_r = Rot([(carve(512), Buf()) for _ in range(2)])
        S.op("dve", lambda e: e.memset(xin[:, 0:2], 0.0), writes=[bxin])
        S.op("dve", lambda e: e.memset(xin[:, S_LEN + 2:S_LEN + 4], 0.0), writes=[bxin])
        for ch in range(6):
            S.dma("sp", xin[:, 2:S_LEN + 2], k.qkvT_a_d[ch * 128:(ch + 1) * 128, :], writes=[bxin])
            qc = qkvn[:, ch, :]
            S.op("dve", lambda e: e.tensor_scalar(out=qc, in0=xin[:, 0:S_LEN], scalar1=cw[:, ch * 5:ch * 5 + 1], scalar2=None, op0=ALU.mult),
                 reads=[bxin, bcw], writes=[bq[ch]])
            for j in range(1, 5):
                S.op("dve", lambda e: e.scalar_tensor_tensor(out=qc, in0=xin[:, j:j + S_LEN], scalar=cw[:, ch * 5 + j:ch * 5 + j + 1], in1=qc,
                                                             op0=ALU.mult, op1=ALU.add), reads=[bxin, bcw, bq[ch]], writes=[bq[ch]])
            S.op("act", lambda e: e.activation(out=qc, in_=qc, func=AF.Silu), reads=[bq[ch]], writes=[bq[ch]])
            if ch < 4:
                S.op("act", lambda e: e.activation(out=xin[:, 2:S_LEN + 2], in_=qc, func=AF.Square), reads=[bq[ch]], writes=[bxin])
                for c8 in range(8):
                    cs = slice(c8 * 512, (c8 + 1) * 512)
                    pp, bpp = k.pb[c8 % 2], k.bpb[c8 % 2]
                    S.op("pe", lambda e: e.matmul(pp[:], lhsT=bd[:], rhs=xin[:, 2 + c8 * 512:2 + (c8 + 1) * 512], start=True, stop=True),
                         reads=[bbd, bxin], writes=[bpp])
                    rs, brs = rs_r.next()
                    S.op("dve", lambda e: e.tensor_scalar(out=rs, in0=pp[:], scalar1=1e-6, scalar2=None, op0=ALU.add), reads=[bpp], writes=[brs])
                    S.op("act", lambda e: e.activation(out=rs, in_=rs, func=AF.Sqrt, scale=(64.0 if ch < 2 else 1.0)), reads=[brs], writes=[brs])
                    S.op("dve", lambda e: e.reciprocal(out=rs, in_=rs), reads=[brs], writes=[brs])
                    S.op("dve", lambda e: e.tensor_tensor(out=qkvn[:, ch, cs], in0=qkvn[:, ch, cs], in1=rs, op=ALU.mult), reads=[bq[ch], brs], writes=[bq[ch]])
        S.barrier()
        off[0] = 8

        def rot(n, cnt):
            return Rot([(carve(n), Buf()) for _ in range(cnt)])
        gc_r = rot(8, 4)
        eg_r = rot(12, 4)
        bge_r = rot(4, 4)
        gq_r = rot(8, 4)
        ktok_r, vtok_r = rot(256, 1), rot(256, 1)
        vb_r, kbg_r, kdec_r = rot(256, 2), rot(256, 2), rot(256, 4)
        expg_r, d1_r, d2_r = rot(512, 1), rot(512, 2), rot(512, 2)
        intra_r = rot(512, 4)
        xa_r, xb_r, p_r = rot(512, 4), rot(512, 4), rot(512, 2)
        u_r, kcT_r, qgT_r = rot(256, 4), rot(256, 4), rot(256, 4)
        vnew_r = rot(256, 2)
        bank = lambda i: (k.pb[i], k.bpb[i])
        QH = (0, 2, 1, 3)
        v4 = lambda a: a.rearrange("p (q i) -> p q i", q=4)
        identq = k.identf[:].rearrange("p (o i) -> p o i", o=1).broadcast_to([128, 4, 128])

        def unit(t, dr):
            ts = slice(t * 128, (t + 1) * 128)
            MI, MSA = tri[:, dr, :], tri[:, 3 - dr, :]
            MIq = MI.rearrange("p (o i) -> p o i", o=1).broadcast_to([128, 4, 128])
            MSAq = MSA.rearrange("p (o i) -> p o i", o=1).broadcast_to([128, 4, 128])
            g_t = gg[:, t, dr * 4:(dr + 1) * 4]
            be_t = beta[:, t, dr * 4:(dr + 1) * 4]
            BA, BB, BC = ((5, 1, 2), (6, 3, 4))[dr]
            pn, bpn = bank(BA)
            pnb, bpnb = bank(BB)
            pnc, bpnc = bank(BC)
            p0, bp0 = bank(BC)
            S.op("pe", lambda e: e.matmul(p0[:, 0:4], lhsT=MI, rhs=g_t, start=True, stop=True), reads=[btri, bgg], writes=[bp0])
            S.op("pe", lambda e: e.matmul(p0[:, 4:8], lhsT=onesf[:], rhs=g_t, start=True, stop=True), reads=[bones, bgg], writes=[bp0])
            gc, bgc = gc_r.next()
            S.op("act", lambda e: e.copy(out=gc, in_=p0[:, 0:8]), reads=[bp0], writes=[bgc])
            eg, beg = eg_r.next()
            S.op("dve", lambda e: e.tensor_tensor(out=eg[:, 4:8], in0=gc[:, 4:8], in1=gc[:, 0:4], op=ALU.subtract), reads=[bgc], writes=[beg])
            S.op("act", lambda e: e.activation(out=eg[:, 0:4], in_=gc[:, 0:4], func=AF.Exp), reads=[bgc], writes=[beg])
            S.op("act", lambda e: e.activation(out=eg[:, 4:8], in_=eg[:, 4:8], func=AF.Exp), reads=[beg], writes=[beg])
            S.op("act", lambda e: e.activation(out=eg[:, 8:12], in_=gc[:, 4:8], func=AF.Exp), reads=[bgc], writes=[beg])
            bge, bbge = bge_r.next()
            S.op("dve", lambda e: e.tensor_tensor(out=bge, in0=be_t, in1=eg[:, 0:4], op=ALU.mult), reads=[bbeta, beg], writes=[bbge])
            gq, bgq = gq_r.next()
            perm = lambda a: a.rearrange("p (pr hl) -> p hl pr", pr=2)
            S.op("dve", lambda e: e.tensor_copy(out=gq[:, 0:4].rearrange("p (hl pr) -> p hl pr", hl=2), in_=perm(gc[:, 0:4])), reads=[bgc], writes=[bgq])
            S.op("dve", lambda e: e.tensor_copy(out=gq[:, 4:8].rearrange("p (hl pr) -> p hl pr", hl=2), in_=perm(be_t)), reads=[bbeta, bgq], writes=[bgq])
            bq_ = lambda a: a.rearrange("p (q o) -> p q o", o=1).broadcast_to([128, 4, 128])
            p1, bp1 = bank(BB)
            for i, ch in enumerate((2, 3, 4, 5)):
                S.op("pe", lambda e: e.transpose(out=p1[:, i * 128:(i + 1) * 128], in_=qkvn[:, ch, ts], identity=k.identf[:]),
                     reads=[bq[ch], k.bidentf], writes=[bp1])
            ktok, bktok = ktok_r.next()
            vtok, bvtok = vtok_r.next()
            S.op("act", lambda e: e.copy(out=ktok, in_=p1[:, 0:256]), reads=[bp1], writes=[bktok])
            S.op("dve", lambda e: e.tensor_copy(out=vtok, in_=p1[:, 256:512]), reads=[bp1], writes=[bvtok])
            v3 = lambda a: a.rearrange("p (h d) -> p h d", h=4)
            bc = lambda a: a.rearrange("p (h o) -> p h o", o=1).broadcast_to([128, 4, 64])
            vb, bvb = vb_r.next()
            kbg, bkbg = kbg_r.next()
            kdec, bkdec = kdec_r.next()
            S.op("dve", lambda e: e.tensor_tensor(out=v3(vb), in0=v3(vtok), in1=bc(be_t), op=ALU.mult), reads=[bvtok, bbeta], writes=[bvb])
            S.op("dve", lambda e: e.tensor_tensor(out=v3(kbg), in0=v3(ktok), in1=bc(bge), op=ALU.mult), reads=[bktok, bbge], writes=[bkbg])
            S.op("dve", lambda e: e.tensor_tensor(out=v3(kdec), in0=v3(ktok), in1=bc(eg[:, 4:8]), op=ALU.mult), reads=[bktok, beg], writes=[bkdec])
            yield
            d1, bd1 = d1_r.next()
            d2, bd2 = d2_r.next()
            dg, bdg = d2, bd2
            df, bdf = d1, bd1
            S.op("dve", lambda e: e.tensor_tensor(out=v4(dg), in0=identq, in1=bq_(gq[:, 0:4]), op=ALU.mult), reads=[k.bidentf, bgq], writes=[bdg])
            p2, bp2 = bank(BC)
            S.op("pe", lambda e: e.matmul(p2[:], lhsT=onesf[:], rhs=dg, start=True, stop=True), reads=[bones, bdg], writes=[bp2])
            expg, bexpg = expg_r.next()
            S.op("dve", lambda e: e.tensor_tensor(out=v4(df), in0=v4(p2[:]), in1=bq_(gq[:, 0:4]), op=ALU.subtract), reads=[bp2, bgq], writes=[bdf])
            S.op("act", lambda e: e.activation(out=expg, in_=p2[:], func=AF.Exp), reads=[bp2], writes=[bexpg])
            S.op("dve", lambda e: e.tensor_scalar(out=d2, in0=df, scalar1=0.0, scalar2=None, op0=ALU.max), reads=[bdf], writes=[bd2])
            S.op("dve", lambda e: e.tensor_scalar(out=d1, in0=df, scalar1=0.0, scalar2=None, op0=ALU.min), reads=[bdf], writes=[bd1])
            S.op("act", lambda e: e.activation(out=d1, in_=d1, func=AF.Exp), reads=[bd1], writes=[bd1])
            S.op("act", lambda e: e.activation(out=d2, in_=d2, func=AF.Exp, scale=-1.0), reads=[bd2], writes=[bd2])
            S.op("dve", lambda e: e.tensor_tensor(out=v4(d1), in0=v4(d1), in1=MIq, op=ALU.mult), reads=[bd1, btri], writes=[bd1])
            S.op("dve", lambda e: e.tensor_tensor(out=v4(d2), in0=v4(d2), in1=MSAq, op=ALU.mult), reads=[bd2, btri], writes=[bd2])
            S.op("dve", lambda e: e.tensor_tensor(out=v4(d2), in0=v4(d2), in1=bq_(gq[:, 4:8]), op=ALU.mult), reads=[bd2, bgq], writes=[bd2])
            qgT, bqgT = qgT_r.next()
            for hl in range(2):
                rs = slice(hl * 64, hl * 64 + 64)
                qs = slice(hl * 256, (hl + 1) * 256)
                S.op("dve", lambda e: e.tensor_tensor(out=qgT[rs, :].rearrange("p (pr i) -> p pr i", pr=2), in0=qkvn[rs, 0:2, ts],
                                                      in1=expg[rs, qs].rearrange("p (pr i) -> p pr i", pr=2), op=ALU.mult),
                     reads=[bq[0], bq[1], bexpg], writes=[bqgT])
            yield
            xa, bxa = xa_r.next()
            intra, bintra = intra_r.next()
            for hl in range(2):
                pk_, bpk = bank((BA, BB)[hl])
                rs = slice(hl * 64, hl * 64 + 64)
                for pr in range(2):
                    kT_h = qkvn[rs, 2 + pr, ts]
                    qT_h = qkvn[rs, pr, ts]
                    S.op("pe", lambda e: e.matmul(pk_[:, pr * 128:(pr + 1) * 128], lhsT=kT_h, rhs=kT_h, start=True, stop=True),
                         reads=[bq[2 + pr]], writes=[bpk])
                    S.op("pe", lambda e: e.matmul(pk_[:, 256 + pr * 128:256 + (pr + 1) * 128], lhsT=kT_h, rhs=qT_h, start=True, stop=True),
                         reads=[bq[2 + pr], bq[pr]], writes=[bpk])
                qs = slice(hl * 256, (hl + 1) * 256)
                S.op("dve", lambda e: e.tensor_tensor(out=xa[:, qs], in0=pk_[:, 0:256], in1=d2[:, qs], op=ALU.mult), reads=[bpk, bd2], writes=[bxa])
                S.op("dve", lambda e: e.tensor_tensor(out=intra[:, qs], in0=pk_[:, 256:512], in1=d1[:, qs], op=ALU.mult), reads=[bpk, bd1], writes=[bintra])
            yield
            for q in range(4):
                S.op("pe", lambda e: e.transpose(out=pnc[:, q * 128:(q + 1) * 128], in_=xa[:, q * 128:(q + 1) * 128], identity=k.identf[:]),
                     reads=[bxa, k.bidentf], writes=[bpnc])
            xb, bxb = xb_r.next()
            S.op("act", lambda e: e.copy(out=xb, in_=pnc[:]), reads=[bpnc], writes=[bxb])
            P, bP = p_r.next()
            S.op("dve", lambda e: e.tensor_tensor(out=v4(P), in0=identq, in1=v4(xb), op=ALU.subtract), reads=[k.bidentf, bxb], writes=[bP])
            yield
            for it in range(6):
                for q in range(4):
                    qs = slice(q * 128, (q + 1) * 128)
                    S.op("pe", lambda e: e.matmul(pn[:, qs], lhsT=xb[:, qs], rhs=xa[:, qs], start=True, stop=True), reads=[bxb, bxa], writes=[bpn])
                xa2, bxa2 = xa_r.next()
                S.op("act", lambda e: e.copy(out=xa2, in_=pn[:]), reads=[bpn], writes=[bxa2])
                if it < 5:
                    for q in range(4):
                        qs = slice(q * 128, (q + 1) * 128)
                        S.op("pe", lambda e: e.matmul(pnb[:, qs], lhsT=xa[:, qs], rhs=xb[:, qs], start=True, stop=True), reads=[bxb, bxa], writes=[bpnb])
                    xb2, bxb2 = xb_r.next()
                    S.op("act", lambda e: e.copy(out=xb2, in_=pnb[:]), reads=[bpnb], writes=[bxb2])
                for q in range(4):
                    qs = slice(q * 128, (q + 1) * 128)
                    S.op("pe", lambda e: e.matmul(pnc[:, qs], lhsT=xa2[:, qs], rhs=P[:, qs], start=True, stop=True), reads=[bxa2, bP], writes=[bpnc])
                S.op("dve", lambda e: e.tensor_tensor(out=P, in0=pnc[:], in1=P, op=ALU.add), reads=[bpnc, bP], writes=[bP])
                yield
                xa, bxa = xa2, bxa2
                if it < 5:
                    xb, bxb = xb2, bxb2
            u, bu = u_r.next()
            kcT, bkcT = kcT_r.next()
            for q in range(4):
                h = QH[q]
                rs = slice((h % 2) * 64, (h % 2) * 64 + 64)
                qs = slice(q * 128, (q + 1) * 128)
                S.op("pe", lambda e: e.matmul(pn[:, h * 64:(h + 1) * 64], lhsT=P[:, qs], rhs=vb[:, h * 64:(h + 1) * 64], start=True, stop=True),
                     reads=[bP, bvb], writes=[bpn])
                S.op("pe", lambda e: e.matmul(pn[rs, 256 + (h // 2) * 128:256 + (h // 2) * 128 + 128], lhsT=kbg[:, h * 64:(h + 1) * 64], rhs=P[:, qs],
                                              start=True, stop=True), reads=[bP, bkbg], writes=[bpn])
            S.op("act", lambda e: e.copy(out=u, in_=pn[:, 0:256]), reads=[bpn], writes=[bu])
            S.op("dve", lambda e: e.tensor_copy(out=kcT, in_=pn[:, 256:512]), reads=[bpn], writes=[bkcT])
            intras = [None] * 4
            for q in range(4):
                intras[QH[q]] = (intra[:, q * 128:(q + 1) * 128], bintra)
            res_[(t, dr)] = dict(u=(u, bu), kcT=(kcT, bkcT), qgT=(qgT, bqgT), intras=intras, kdec=(kdec, bkdec), eg=(eg, beg))

        res_ = {}

        def step(t, dr, un):
            u, bu = un["u"]
            kcT, bkcT = un["kcT"]
            qgT, bqgT = un["qgT"]
            kdec, bkdec = un["kdec"]
            eg, beg = un["eg"]
            p7, bp7 = bank(7 if dr == 0 else 0)
            for pr in range(2):
                ps_ = slice(pr * 128, (pr + 1) * 128)
                S.op("pe", lambda e: e.matmul(p7[:, ps_], lhsT=kcT[:, ps_], rhs=st[:, dr, pr, :], start=True, stop=True),
                     reads=[bkcT, bst[dr]], writes=[bp7])
            yield
            vnew, bvnew = vnew_r.next()
            S.op("dve", lambda e: e.tensor_tensor(out=vnew, in0=u, in1=p7[:, 0:256], op=ALU.subtract), reads=[bu, bp7], writes=[bvnew])
            yield
            for pr in range(2):
                ps_ = slice(256 + pr * 128, 256 + (pr + 1) * 128)
                S.op("pe", lambda e: e.matmul(p7[:, ps_], lhsT=qgT[:, pr * 128:(pr + 1) * 128], rhs=st[:, dr, pr, :], start=True, stop=False),
                     reads=[bqgT, bst[dr]], writes=[bp7])
                for h in (2 * pr, 2 * pr + 1):
                    intra, bintra = un["intras"][h]
                    S.op("pe", lambda e: e.matmul(p7[:, 256 + h * 64:256 + (h + 1) * 64], lhsT=intra, rhs=vnew[:, h * 64:(h + 1) * 64],
                                                  start=False, stop=(h == 2 * pr + 1)), reads=[bintra, bvnew], writes=[bp7])
            p0, bp0 = p7, bp7
            for pr in range(2):
                ps_ = slice(pr * 128, (pr + 1) * 128)
                S.op("pe", lambda e: e.matmul(p0[:, pr * 128:(pr + 1) * 128], lhsT=kdec[:, ps_], rhs=vnew[:, ps_], start=True, stop=True),
                     reads=[bkdec, bvnew], writes=[bp0])
            yield
            S.op("dve", lambda e: e.tensor_tensor(out=oall[:, t, :], in0=p7[:, 256:512], in1=oall[:, t, :], op=ALU.add), reads=[bp7, boall], writes=[boall])
            for h in range(4):
                pr = h // 2
                rs = slice((h % 2) * 64, (h % 2) * 64 + 64)
                cs = slice((h % 2) * 64, (h % 2) * 64 + 64)
                sv = st[rs, dr, pr, cs]
                S.op("dve", lambda e: e.scalar_tensor_tensor(out=sv, in0=sv, scalar=eg[rs, 8 + h:9 + h],
                                                             in1=p0[rs, pr * 128 + (h % 2) * 64:pr * 128 + (h % 2) * 64 + 64],
                                                             op0=ALU.mult, op1=ALU.add), reads=[bst[dr], beg, bp0], writes=[bst[dr]])

        def run_rr(gens):
            alive = [True] * len(gens)
            while any(alive):
                for gi in range(len(gens)):
                    if alive[gi]:
                        try:
                            next(gens[gi])
                        except StopIteration:
                            alive[gi] = False

        pending = []
        for i in range(NT):
            run_rr([unit(i, 0), unit(NT - 1 - i, 1)] + pending)
            pending = [step(i, 0, res_.pop((i, 0))), step(NT - 1 - i, 1, res_.pop((NT - 1 - i, 1)))]
        run_rr(pending)
        S.barrier()
        off[0] = 8
        z_r = rot(256, 2)
        sq_r = rot(256, 2)
        r4_r = rot(4, 2)
        gnw = sp_[:, 768 + 64 * l:768 + 64 * (l + 1)]
        ob_r = Rot([(obt[:, i, :], Buf()) for i in range(2)])
        for t in range(NT):
            o3 = oall[:, t, :].rearrange("p (h d) -> p h d", h=4)
            z, bz = z_r.next()
            S.dma("sp", z, k.za_d[t * 128:(t + 1) * 128, 0:256], writes=[bz])
            S.op("act", lambda e: e.activation(out=z, in_=z, func=AF.Silu), reads=[bz], writes=[bz])
            sq, bsq = sq_r.next()
            S.op("dve", lambda e: e.tensor_tensor(out=sq, in0=oall[:, t, :], in1=oall[:, t, :], op=ALU.mult), reads=[boall], writes=[bsq])
            r4, br4 = r4_r.next()
            S.op("dve", lambda e: e.tensor_reduce(out=r4, in_=sq.rearrange("p (h d) -> p h d", h=4), axis=AX.X, op=ALU.add), reads=[bsq], writes=[br4])
            S.op("dve", lambda e: e.tensor_scalar(out=r4, in0=r4, scalar1=1.0 / 64, scalar2=EPS, op0=ALU.mult, op1=ALU.add), reads=[br4], writes=[br4])
            S.op("act", lambda e: e.activation(out=r4, in_=r4, func=AF.Sqrt), reads=[br4], writes=[br4])
            S.op("dve", lambda e: e.reciprocal(out=r4, in_=r4), reads=[br4], writes=[br4])
            s3 = sq.rearrange("p (h d) -> p h d", h=4)
            S.op("dve", lambda e: e.tensor_tensor(out=s3, in0=o3, in1=r4.rearrange("p (h o) -> p h o", o=1).broadcast_to([128, 4, 64]), op=ALU.mult),
                 reads=[boall, br4], writes=[bsq])
            S.op("dve", lambda e: e.tensor_tensor(out=s3, in0=s3, in1=gnw.rearrange("p (o d) -> p o d", o=1).broadcast_to([128, 4, 64]), op=ALU.mult),
                 reads=[bsq, k.bsmallp], writes=[bsq])
            ob, bob = ob_r.next()
            S.op("dve", lambda e: e.tensor_tensor(out=ob, in0=sq, in1=z, op=ALU.mult), reads=[bsq, bz], writes=[bob])
            S.dma("sp", k.mix_d[t * 128:(t + 1) * 128, 0:256], ob, reads=[bob])
        S.barrier()


def _col_index():
    def rot(a):
        return a.reshape(-1, 2, 32)[:, ::-1, :].reshape(-1)
    bqk = np.arange(1040, 1552)
    cqk = np.arange(1808, 2832)
    return np.concatenate([np.arange(0, 768), bqk, rot(bqk), cqk, rot(cqk),
                           np.arange(768, 1040), np.arange(1552, 1808), np.arange(2832, 3344)])


def _consts():
    p = np.arange(128)
    inv = (10000.0 ** (-np.arange(0, 64, 2, dtype=np.float32) / np.float32(64))).astype(np.float32)
    cst = np.zeros((128, 4), np.float32)
    cst[:, 0] = inv[p % 32]
    cst[:, 1] = np.where((p % 64) < 32, -1.0, 1.0)
    cst[:, 2] = math.pi / 2
    kk = np.arange(128)[:, None, None]
    dd = np.arange(17)[None, :, None] - 8
    qq = np.arange(128)[None, None, :]
    dist = np.abs(dd * 128 + kk - qq)
    m = (dist <= 64).astype(np.float32) + ((dist % 4 == 0) & (dist <= 256)) + ((dist % 16 == 0) & (dist <= 1024))
    return cst, m.astype(np.float32)


def make_in_maps(inputs):
    f = lambda a: np.ascontiguousarray(np.asarray(a))
    idx = _col_index()
    w_in_ext = f(np.asarray(inputs["w_in"])[:, :, idx])
    cst, mask = _consts()
    sp = np.zeros((1024,), np.float32)
    for l in range(L):
        sp[256 * l:256 * l + 64] = inputs["lambda_q1"][l]
        sp[256 * l + 64:256 * l + 128] = inputs["lambda_k1"][l]
        sp[256 * l + 128:256 * l + 192] = inputs["lambda_q2"][l]
        sp[256 * l + 192:256 * l + 256] = inputs["lambda_k2"][l]
        sp[512 + 128 * l:512 + 128 * (l + 1)] = inputs["subln_w"][l]
        sp[768 + 64 * l:768 + 64 * (l + 1)] = inputs["gdn_norm_w"][l]
        sp[896 + 8 * l:896 + 8 * (l + 1)] = np.asarray(inputs["a_log"][l]).reshape(-1)
        sp[912 + 8 * l:912 + 8 * (l + 1)] = np.asarray(inputs["dt_bias"][l]).reshape(-1)
    smallp = f(np.broadcast_to(sp[None, :], (128, 1024)))
    cwl = np.asarray(inputs["conv_w"]).astype(np.float32)
    convw = f(cwl.transpose(0, 2, 1).reshape(L, 6, 128, 5).transpose(0, 2, 1, 3).reshape(L, 128, 30))
    r_, c_ = np.arange(128)[:, None], np.arange(128)[None, :]
    tri = f(np.stack([(c_ >= r_), (c_ <= r_), (c_ > r_), (c_ < r_)], axis=1).astype(np.float32))
    shared = {
        "convw": convw, "tri": tri,
        "w_in": w_in_ext, "attn_nw": f(inputs["attn_norm_w"]), "w_out": f(inputs["w_out"]),
        "ffn_nw": f(inputs["ffn_norm_w"]), "w_gate": f(inputs["w_gate"]), "w_up": f(inputs["w_up"]),
        "w_down": f(inputs["w_down"]), "final_nw": f(inputs["final_norm_w"]), "smallp": smallp,
        "cst": cst, "maskb": mask,
    }
    x = np.asarray(inputs["x"])
    pos = np.asarray(inputs["positions"]).astype(np.int32)
    maps = []
    for b in range(8):
        m = dict(shared)
        m["x"] = f(x[b])
        m["pos"] = f(pos[b])
        maps.append(m)
    return maps


def kernel(**inputs):
    nc, _ = build()
    maps = make_in_maps(inputs)
    res = run_bass_kernel_spmd(nc, maps, core_ids=list(range(8)))
    return np.stack([r["out"] for r in res.results], axis=0).astype(np.float32)
```
